# Optimizing a Trainium2 kernel written in Bass

```python
import math
import jax, jax.numpy as jnp
from jax import lax
import numpy as np

D_MODEL = 1024
BATCH = 16
SEQ = 2048
DEPTH = 2

CHUNK = 64
D_BRANCH = 512
N_BRANCH = 3
S5_GROUP = 16
S5_GROUPS = D_BRANCH // S5_GROUP
S5_STATE = 64
DT_MIN = 1e-3
DT_MAX = 1e-1
POOL_WINDOWS = (2, 4, 8, 16)
POOL_GROUPS = len(POOL_WINDOWS)
POOL_GROUP = D_BRANCH // POOL_GROUPS
SGU_BLOCK = 128
SGU_HEADS = 4
SGU_HEAD_DIM = D_BRANCH // SGU_HEADS
IN_WIDTHS = (D_BRANCH, D_BRANCH, D_BRANCH, D_BRANCH, D_BRANCH, D_BRANCH, D_BRANCH, N_BRANCH * D_MODEL)
D_IN = sum(IN_WIDTHS)
RMS_EPS = 1e-6
LN_EPS = 1e-5

kernel_name = "hybrid_s5_pool_sgu_gated_trunk"


def rmsnorm(x, g):
    xf = x.astype(jnp.float32)
    y = xf * lax.rsqrt(jnp.mean(xf * xf, axis=-1, keepdims=True) + RMS_EPS)
    return (y * g.astype(jnp.float32)).astype(x.dtype)


def s5_mixer(u, lam_re, lam_im, log_dt, b_re, b_im, c_re, c_im, d_skip, w_glu, b_glu):
    bsz, seq, _ = u.shape
    uf = u.astype(jnp.float32).reshape(bsz, seq, S5_GROUPS, S5_GROUP)
    dt = jnp.exp(log_dt.astype(jnp.float32))[:, None]
    lr = lam_re.astype(jnp.float32)
    li = lam_im.astype(jnp.float32)
    mag = jnp.exp(lr * dt)
    ab_re = mag * jnp.cos(li * dt)
    ab_im = mag * jnp.sin(li * dt)
    den = lr * lr + li * li
    nr = ab_re - 1.0
    ni = ab_im
    k_re = (nr * lr + ni * li) / den
    k_im = (ni * lr - nr * li) / den
    br = b_re.astype(jnp.float32)
    bi = b_im.astype(jnp.float32)
    bb_re = k_re[..., None] * br - k_im[..., None] * bi
    bb_im = k_re[..., None] * bi + k_im[..., None] * br
    bu_re = jnp.einsum('gpc,blgc->blgp', bb_re, uf)
    bu_im = jnp.einsum('gpc,blgc->blgp', bb_im, uf)
    a_re = jnp.broadcast_to(ab_re[None, None], (1, seq, S5_GROUPS, S5_STATE))
    a_im = jnp.broadcast_to(ab_im[None, None], (1, seq, S5_GROUPS, S5_STATE))

    def combine(e1, e2):
        a1r, a1i, b1r, b1i = e1
        a2r, a2i, b2r, b2i = e2
        return (a2r * a1r - a2i * a1i,
                a2r * a1i + a2i * a1r,
                a2r * b1r - a2i * b1i + b2r,
                a2r * b1i + a2i * b1r + b2i)

    _, _, h_re, h_im = lax.associative_scan(combine, (a_re, a_im, bu_re, bu_im), axis=1)
    y = (jnp.einsum('gcp,blgp->blgc', c_re.astype(jnp.float32), h_re)
         - jnp.einsum('gcp,blgp->blgc', c_im.astype(jnp.float32), h_im))
    y = y + d_skip.astype(jnp.float32).reshape(S5_GROUPS, S5_GROUP) * uf
    y = jax.nn.gelu(y.reshape(bsz, seq, D_BRANCH))
    y = y * jax.nn.sigmoid(y @ w_glu.astype(jnp.float32) + b_glu.astype(jnp.float32))
    return y.astype(u.dtype)


def pool_mixer(u, w_pool, pool_scale):
    bsz, seq, _ = u.shape
    uf = u.astype(jnp.float32).reshape(bsz, seq, POOL_GROUPS, POOL_GROUP)
    cs = jnp.cumsum(uf, axis=1)
    pos = jnp.arange(1, seq + 1, dtype=jnp.float32)
    outs = []
    for gi, w in enumerate(POOL_WINDOWS):
        c = cs[:, :, gi]
        c_prev = jnp.pad(c, ((0, 0), (w, 0), (0, 0)))[:, :seq]
        mean = (c - c_prev) / jnp.minimum(pos, float(w))[None, :, None]
        outs.append(mean - uf[:, :, gi])
    p = jnp.stack(outs, axis=2)
    y = jnp.einsum('blgc,gcd->blgd', p, w_pool.astype(jnp.float32)).reshape(bsz, seq, D_BRANCH)
    return (y * pool_scale.astype(jnp.float32)).astype(u.dtype)


def sgu_mixer(u, v, ln_g, ln_b, w_s, b_s):
    bsz, seq, _ = v.shape
    vf = v.astype(jnp.float32)
    mu = jnp.mean(vf, axis=-1, keepdims=True)
    var = jnp.mean(jnp.square(vf - mu), axis=-1, keepdims=True)
    vn = (vf - mu) * lax.rsqrt(var + LN_EPS) * ln_g.astype(jnp.float32) + ln_b.astype(jnp.float32)
    vn = vn.reshape(bsz, seq // SGU_BLOCK, SGU_BLOCK, SGU_HEADS, SGU_HEAD_DIM)
    t = jnp.arange(SGU_BLOCK)
    mask = (t[None, :] // CHUNK) <= (t[:, None] // CHUNK)
    ws = jnp.where(mask[None], w_s.astype(jnp.float32), 0.0)
    z = jnp.einsum('hts,bnshc->bnthc', ws, vn) + b_s.astype(jnp.float32).T[None, None, :, :, None]
    z = z.reshape(bsz, seq, D_BRANCH)
    return (u.astype(jnp.float32) * z).astype(u.dtype)


def setup_inputs(seed: int = 0) -> dict:
    key = jax.random.key(seed)
    ks = jax.random.split(key, 24)
    f32 = jnp.float32
    n = jnp.arange(S5_STATE, dtype=f32)
    x = jax.random.normal(ks[0], (BATCH, SEQ, D_MODEL), f32)
    norm_g = 1.0 + 0.02 * jax.random.normal(ks[1], (DEPTH, D_MODEL), f32)
    w_in = jax.random.normal(ks[2], (DEPTH, D_MODEL, D_IN), f32) * D_MODEL ** -0.5
    s5_lam_re = -0.5 + 0.01 * jax.random.normal(ks[3], (DEPTH, S5_GROUPS, S5_STATE), f32)
    s5_lam_im = math.pi * n + 0.01 * jax.random.normal(ks[4], (DEPTH, S5_GROUPS, S5_STATE), f32)
    s5_log_dt = jax.random.uniform(ks[5], (DEPTH, S5_GROUPS), f32, math.log(DT_MIN), math.log(DT_MAX))
    bscale = (2.0 * S5_GROUP) ** -0.5
    s5_b_re = jax.random.normal(ks[6], (DEPTH, S5_GROUPS, S5_STATE, S5_GROUP), f32) * bscale
    s5_b_im = jax.random.normal(ks[7], (DEPTH, S5_GROUPS, S5_STATE, S5_GROUP), f32) * bscale
    cscale = (2.0 * S5_STATE) ** -0.5
    s5_c_re = jax.random.normal(ks[8], (DEPTH, S5_GROUPS, S5_GROUP, S5_STATE), f32) * cscale
    s5_c_im = jax.random.normal(ks[9], (DEPTH, S5_GROUPS, S5_GROUP, S5_STATE), f32) * cscale
    s5_d = jax.random.normal(ks[10], (DEPTH, D_BRANCH), f32)
    s5_w_glu = jax.random.normal(ks[11], (DEPTH, D_BRANCH, D_BRANCH), f32) * D_BRANCH ** -0.5
    s5_b_glu = 0.02 * jax.random.normal(ks[12], (DEPTH, D_BRANCH), f32)
    pool_w = jax.random.normal(ks[13], (DEPTH, POOL_GROUPS, POOL_GROUP, POOL_GROUP), f32) * POOL_GROUP ** -0.5
    pool_scale = 1.0 + 0.02 * jax.random.normal(ks[14], (DEPTH, D_BRANCH), f32)
    sgu_ln_g = 1.0 + 0.02 * jax.random.normal(ks[15], (DEPTH, D_BRANCH), f32)
    sgu_ln_b = 0.02 * jax.random.normal(ks[16], (DEPTH, D_BRANCH), f32)
    sgu_w = jax.random.normal(ks[17], (DEPTH, SGU_HEADS, SGU_BLOCK, SGU_BLOCK), f32) * SGU_BLOCK ** -0.5
    sgu_b = 1.0 + 0.02 * jax.random.normal(ks[18], (DEPTH, SGU_HEADS, SGU_BLOCK), f32)
    w_branch = jax.random.normal(ks[19], (DEPTH, N_BRANCH, D_BRANCH, D_MODEL), f32) * D_BRANCH ** -0.5
    w_out = jax.random.normal(ks[20], (DEPTH, D_MODEL, D_MODEL), f32) * D_MODEL ** -0.5
    final_norm_g = 1.0 + 0.02 * jax.random.normal(ks[21], (D_MODEL,), f32)
    return {"x": x, "norm_g": norm_g, "w_in": w_in,
            "s5_lam_re": s5_lam_re, "s5_lam_im": s5_lam_im, "s5_log_dt": s5_log_dt,
            "s5_b_re": s5_b_re, "s5_b_im": s5_b_im, "s5_c_re": s5_c_re, "s5_c_im": s5_c_im,
            "s5_d": s5_d, "s5_w_glu": s5_w_glu, "s5_b_glu": s5_b_glu,
            "pool_w": pool_w, "pool_scale": pool_scale,
            "sgu_ln_g": sgu_ln_g, "sgu_ln_b": sgu_ln_b, "sgu_w": sgu_w, "sgu_b": sgu_b,
            "w_branch": w_branch, "w_out": w_out, "final_norm_g": final_norm_g}


def reference(x, norm_g, w_in, s5_lam_re, s5_lam_im, s5_log_dt, s5_b_re, s5_b_im, s5_c_re, s5_c_im,
              s5_d, s5_w_glu, s5_b_glu, pool_w, pool_scale, sgu_ln_g, sgu_ln_b, sgu_w, sgu_b,
              w_branch, w_out, final_norm_g):
    bsz, seq, _ = x.shape
    split_idx = [int(v) for v in np.cumsum(IN_WIDTHS)[:-1]]
    for l in range(DEPTH):
        h = rmsnorm(x, norm_g[l])
        z = h @ w_in[l]
        a_val, a_gate, b_val, b_gate, c_u, c_v, c_gate, gates = jnp.split(z, split_idx, axis=-1)
        ya = s5_mixer(a_val, s5_lam_re[l], s5_lam_im[l], s5_log_dt[l], s5_b_re[l], s5_b_im[l],
                      s5_c_re[l], s5_c_im[l], s5_d[l], s5_w_glu[l], s5_b_glu[l]) * jax.nn.silu(a_gate)
        yb = pool_mixer(b_val, pool_w[l], pool_scale[l]) * jax.nn.silu(b_gate)
        yc = sgu_mixer(c_u, c_v, sgu_ln_g[l], sgu_ln_b[l], sgu_w[l], sgu_b[l]) * jax.nn.silu(c_gate)
        ys = jnp.stack([ya, yb, yc], axis=2)
        proj = jnp.einsum('blkc,kcd->blkd', ys, w_branch[l])
        g = jax.nn.sigmoid(gates.reshape(bsz, seq, N_BRANCH, D_MODEL))
        merged = jnp.sum(g * proj, axis=2)
        x = x + merged @ w_out[l]
    return rmsnorm(x, final_norm_g)
```

```python
import numpy as np
from contextlib import ExitStack
import concourse.bass as bass
import concourse.mybir as mybir
from concourse.bass_utils import run_bass_kernel_spmd

F32 = mybir.dt.float32
BF16 = mybir.dt.bfloat16
I32 = mybir.dt.int32
AF = mybir.ActivationFunctionType
ALU = mybir.AluOpType
AX = mybir.AxisListType

D = 1024
SEQ = 2048
BATCH = 16
DEPTH = 2
NCORES = 8
SPC = BATCH // NCORES
DB = 512
DIN = 6656
KT = D // 128
NB = 512
NQ = NB // 8
NTT = NB // 128
BPS = SEQ // NB
RMS_EPS = 1e-6
LN_EPS = 1e-5
NM = 9 + 8 + NQ
TWO_PI_SAFE = 6.283185
SLOT = 4096
NSLAB = 19


def _ap_range(ap):
    es = mybir.dt.size(ap.dtype)
    pat = ap.ap
    off = int(ap.offset)
    space = type(ap.tensor).__name__
    if space.startswith("DRam"):
        ext = 1
        for st, cnt in pat:
            ext += (cnt - 1) * abs(st)
        return (ap.tensor.name, 0, 1, off * es, (off + ext) * es)
    pstep, pcnt = pat[0]
    if pstep == 0:
        pstep = 1 << 40
    p0 = off // pstep if pstep < (1 << 40) else 0
    col = off - p0 * pstep if pstep < (1 << 40) else off
    ext = 1
    for st, cnt in pat[1:]:
        ext += (cnt - 1) * abs(st)
    return (ap.tensor.name, p0, p0 + pcnt, col * es, (col + ext) * es)


class Tok:
    _n = 0

    def __init__(self, name="tok"):
        Tok._n += 1
        self.key = (f"__tok{Tok._n}_{name}", 0, 1, 0, 1)


class Op:
    __slots__ = ("eng", "fn", "deps", "is_dma", "dsem", "dcum", "signal", "cnt", "idx", "tag", "rw")


class Sched:
    ENG = ("pe", "act", "dve", "pool", "sp")
    EMAP = {"pe": "tensor", "act": "scalar", "dve": "vector", "pool": "gpsimd", "sp": "sync"}

    def __init__(self, nc, es):
        self.nc = nc
        self.es = es
        self.ops = []
        self.recs = {}
        self.sems = {e: es.enter_context(nc.semaphore(f"s_{e}")) for e in self.ENG}
        self.dstreams = {}
        import os
        self.debug_rw = bool(os.environ.get('DEBUG_RW'))

    def sbuf(self, name, shape, dtype):
        return self.es.enter_context(self.nc.sbuf_tensor(name, list(shape), dtype))

    def psum(self, name, shape, dtype=F32):
        return self.es.enter_context(self.nc.psum_tensor(name, list(shape), dtype))

    @staticmethod
    def _is_psum(x):
        return (not isinstance(x, Tok)) and type(x.tensor).__name__.startswith("PSum")

    def _rng(self, x):
        if isinstance(x, Tok):
            return x.key
        if self._is_psum(x):
            return (x.tensor.name, 0, 128, 0, 2048)
        return _ap_range(x)

    @staticmethod
    def _remainders(r, p0, p1, b0, b1):
        rp0, rp1, rb0, rb1 = r[0], r[1], r[2], r[3]
        out = []
        if rp0 < p0:
            out.append((rp0, p0, rb0, rb1))
        if p1 < rp1:
            out.append((p1, rp1, rb0, rb1))
        q0, q1 = max(rp0, p0), min(rp1, p1)
        if rb0 < b0:
            out.append((q0, q1, rb0, b0))
        if b1 < rb1:
            out.append((q0, q1, b1, rb1))
        return out

    def _access(self, x, idx, write, deps):
        name, p0, p1, b0, b1 = self._rng(x)
        lst = self.recs.setdefault(name, [])
        keep = []
        hit = False
        for r in lst:
            if r[0] < p1 and p0 < r[1] and r[2] < b1 and b0 < r[3]:
                hit = True
                if r[4] is not None:
                    deps.add(r[4])
                if write:
                    deps.update(r[5])
                else:
                    keep.append([max(r[0], p0), min(r[1], p1), max(r[2], b0), min(r[3], b1), r[4], r[5] + [idx]])
                for (a0, a1, c0, c1) in self._remainders(r, p0, p1, b0, b1):
                    keep.append([a0, a1, c0, c1, r[4], list(r[5])])
            else:
                keep.append(r)
        if write:
            keep.append([p0, p1, b0, b1, idx, []])
        elif not hit:
            keep.append([p0, p1, b0, b1, None, [idx]])
        else:
            covered = sum((min(r[1], p1) - max(r[0], p0)) * (min(r[3], b1) - max(r[2], b0)) for r in keep
                          if r[0] < p1 and p0 < r[1] and r[2] < b1 and b0 < r[3] and idx in r[5])
            if covered < (p1 - p0) * (b1 - b0):
                keep.append([p0, p1, b0, b1, None, [idx]])
        self.recs[name] = keep

    def op(self, eng, fn, reads=(), writes=(), dstream=None):
        o = Op()
        o.eng = eng
        o.fn = fn
        o.idx = len(self.ops)
        o.tag = getattr(self, 'stage', '')
        deps = set()
        for x in reads:
            self._access(x, o.idx, self._is_psum(x), deps)
        for x in writes:
            self._access(x, o.idx, True, deps)
        deps.discard(o.idx)
        o.deps = deps
        o.rw = ([self._rng(x) for x in reads], [self._rng(x) for x in writes]) if getattr(self, 'debug_rw', False) else None
        o.is_dma = dstream is not None
        o.signal = False
        o.cnt = 0
        if o.is_dma:
            if dstream not in self.dstreams:
                self.dstreams[dstream] = [self.es.enter_context(self.nc.semaphore(f"d_{dstream}")), 0]
            st = self.dstreams[dstream]
            if len(st) > 2 and not getattr(self, "_par", False):
                o.deps.add(st[2])
            st[1] += 16
            o.dsem, o.dcum = st[0], st[1]
            if len(st) > 2:
                st[2] = o.idx
            else:
                st.append(o.idx)
            self._par = False
        self.ops.append(o)
        return o

    def dma(self, queue, out_ap, in_ap, stream, extra_reads=(), extra_writes=(), parallel=False, **kw):
        self._par = parallel
        if stream == "prep":
            self._prr = getattr(self, "_prr", 0) + 1
            stream = f"prep{self._prr % 6}"
        return self.op(queue, lambda e: e.dma_start(out=out_ap, in_=in_ap, **kw),
                       reads=[in_ap, *extra_reads], writes=[out_ap, *extra_writes], dstream=stream)

    def emit(self, final_streams=()):
        nc, ops = self.nc, self.ops
        for o in ops:
            for d in o.deps:
                od = ops[d]
                if od.is_dma:
                    continue
                if od.eng == "pe" and o.eng == "pe" and not o.is_dma:
                    continue
                od.signal = True
        cnts = {e: 0 for e in self.ENG}
        for o in ops:
            if not o.is_dma and o.signal:
                cnts[o.eng] += 1
                o.cnt = cnts[o.eng]
        self.final_counts = cnts
        with nc.Block() as block:
            for ename in self.ENG:
                def body(eng, ename=ename):
                    known = {}
                    for o in ops:
                        if o.eng != ename:
                            continue
                        need = {}
                        for d in o.deps:
                            od = ops[d]
                            if od.is_dma:
                                key, sem, val = ("d", id(od.dsem)), od.dsem, od.dcum
                            else:
                                if od.eng == "pe" and ename == "pe" and not o.is_dma:
                                    continue
                                key, sem, val = ("e", od.eng), self.sems[od.eng], od.cnt
                            if known.get(key, 0) >= val:
                                continue
                            if key not in need or need[key][1] < val:
                                need[key] = (sem, val)
                        for key, (sem, val) in need.items():
                            eng.wait_ge(sem, val)
                            known[key] = val
                        ins = o.fn(eng)
                        if o.is_dma:
                            ins.then_inc(o.dsem, 16)
                        elif o.signal:
                            ins.then_inc(self.sems[ename], 1)
                    if ename == "sp":
                        for s in final_streams:
                            st = self.dstreams[s]
                            eng.wait_ge(st[0], st[1])
                getattr(block, self.EMAP[ename])(body)


def host_consts():
    c = {}
    c["ident"] = np.eye(128, dtype=np.float32)
    t = np.arange(128)
    c["sgumask"] = ((t[None, :] // 64) <= (t[:, None] // 64)).astype(np.float32)
    mult = np.concatenate([np.arange(9), np.arange(7, -1, -1), 8 * (np.arange(NQ) + 1)]).astype(np.float32)
    c["mult"] = np.tile(mult[None, :], (128, 1))
    rc = np.zeros((128, 4, 16), np.float32)
    for gi, w in enumerate((2, 4, 8, 16)):
        rc[:, gi, :] = 1.0 / np.minimum(np.arange(1, 17), w)
    c["rcfix"] = rc.reshape(128, 64)
    sel = np.zeros((128, 2, 64), np.float32)
    for q in range(64):
        for h in range(2):
            sel[2 * q + h, h, q] = 1.0
    c["sel"] = sel.reshape(128, 128)
    return c


CONST_LAYOUT = [("ident", 128), ("sgumask", 128), ("mult", NM), ("rcfix", 64), ("sel", 128)]
NCONST = sum(w for _, w in CONST_LAYOUT)


def pack_consts():
    c = host_consts()
    return np.concatenate([c[k] for k, _ in CONST_LAYOUT], axis=1).astype(np.float32)


ORDER = [0, 19, 1, 2, 3, 4, 6, 20, 21, 22, 13, 5, 7, 8, 14, 9, 10, 15, 11, 12, 16, 17, 18]
NSLAB_ALL = 23
NRING = 5

WEIGHT_NAMES = ["norm_g", "w_in", "s5_lam_re", "s5_lam_im", "s5_log_dt", "s5_b_re", "s5_b_im", "s5_c_re",
                "s5_c_im", "s5_d", "s5_w_glu", "s5_b_glu", "pool_w", "pool_scale", "sgu_ln_g", "sgu_ln_b",
                "sgu_w", "sgu_b", "w_branch", "w_out", "final_norm_g"]
WEIGHT_SHAPES = {
    "norm_g": [DEPTH, D], "w_in": [DEPTH, D, DIN], "s5_lam_re": [DEPTH, 32, 64], "s5_lam_im": [DEPTH, 32, 64],
    "s5_log_dt": [DEPTH, 32], "s5_b_re": [DEPTH, 32, 64, 16], "s5_b_im": [DEPTH, 32, 64, 16],
    "s5_c_re": [DEPTH, 32, 16, 64], "s5_c_im": [DEPTH, 32, 16, 64], "s5_d": [DEPTH, DB],
    "s5_w_glu": [DEPTH, DB, DB], "s5_b_glu": [DEPTH, DB], "pool_w": [DEPTH, 4, 128, 128],
    "pool_scale": [DEPTH, DB], "sgu_ln_g": [DEPTH, DB], "sgu_ln_b": [DEPTH, DB], "sgu_w": [DEPTH, 4, 128, 128],
    "sgu_b": [DEPTH, 4, 128], "w_branch": [DEPTH, 3, DB, D], "w_out": [DEPTH, D, D], "final_norm_g": [D],
}


def _numel(shape):
    n = 1
    for s in shape:
        n *= s
    return n


class Carver:
    def __init__(self, tensor_f32, base=0, hole=None):
        self.t = tensor_f32
        self.off = base
        self.hole = hole

    def take(self, nelem, dtype, parts=128, p0=0):
        nbytes = nelem * mybir.dt.size(dtype)
        words = (nbytes + 3) // 4
        words = (words + 7) // 8 * 8
        if self.hole is not None and self.off < self.hole[1] and self.off + words > self.hole[0]:
            self.off = self.hole[1]
        ap = self.t[p0:p0 + parts, self.off:self.off + words]
        self.off += words
        if dtype != F32:
            ap = ap.bitcast(dtype)
        return ap[:, 0:nelem]


def build_program(mode="full", nblocks=None, layers=(0, 1), do_final=True, x_is_T=False):
    nc = bass.Bass("TRN2", target_bir_lowering=False)
    NTOK = SPC * SEQ
    dram = {}
    for n in WEIGHT_NAMES:
        dram[n] = nc.dram_tensor(n, WEIGHT_SHAPES[n], F32, kind="ExternalInput")
    x_d = nc.dram_tensor("x", [D, NTOK], F32, kind="ExternalInput").ap()
    consts_d = nc.dram_tensor("consts", [128, NCONST], F32, kind="ExternalInput").ap()
    zeros_t = nc.dram_tensor("zeros", [128, SLOT // 2], F32, kind="ExternalInput")
    out_d = nc.dram_tensor("out", [D, NTOK], F32, kind="ExternalOutput").ap()
    wscr = nc.dram_tensor("wscr", [DEPTH, NSLAB_ALL, 128, SLOT], BF16, kind="Internal")
    dbg = {}

    def DAP(name, offset, pat):
        return bass.AP(dram[name], offset, pat)

    def SCR(l, s):
        return bass.AP(wscr, (l * NSLAB_ALL + s) * 128 * SLOT, [[SLOT, 128], [1, SLOT]])

    with ExitStack() as es:
        S = Sched(nc, es)
        cst = S.sbuf("cst", [128, NCONST], F32)
        ident = cst[:, 0:128]
        sgumask = cst[:, 128:256]
        mult = cst[:, 256:256 + NM]
        rcfix = cst[:, 256 + NM:256 + NM + 64]
        selb = S.sbuf("selb", [128, 128], BF16)
        identb = S.sbuf("identb", [128, 128], BF16)
        onesb = S.sbuf("onesb", [128, 128], BF16)
        epsc = S.sbuf("epsc", [128, 4], F32)
        fngc = S.sbuf("fngc", [128, KT], F32)
        fst = S.sbuf("fst", [KT, 128], F32)
        AR_WORDS = (36 if mode == 'prep_test' else 24) * 1024
        AR = S.sbuf("arena", [128, AR_WORDS], F32)
        PT = mode == "prep_test"
        BIG_WORDS = NRING * 2048 + 4096 + 2048 + 3 * 1024
        BIG = S.sbuf("big", [128, 2048 if PT else BIG_WORDS], F32)
        ring = [BIG[:, i * 2048:(i + 1) * 2048].bitcast(BF16) for i in range(1 if PT else NRING)]
        _o = NRING * 2048
        xT = None if PT else BIG[:, _o:_o + 4096]
        hT = None if PT else BIG[:, _o + 4096:_o + 6144].bitcast(BF16)
        yT = None if PT else [BIG[:, _o + 6144 + k * 1024:_o + 6144 + (k + 1) * 1024].bitcast(BF16) for k in range(3)]
        banks = [S.psum(f"bank{i}", [128, 512], F32) for i in range(8)]
        L = []
        for l in range(DEPTH):
            r = {}
            r["Ec"] = S.sbuf(f"Ec{l}", [128, 16 * NQ], F32)
            r["Es"] = S.sbuf(f"Es{l}", [128, 16 * NQ], F32)
            r["r8"] = S.sbuf(f"r8_{l}", [128, 16], F32)
            r["cols"] = S.sbuf(f"cols{l}", [128, 16], F32)
            r["pw"] = S.sbuf(f"pw{l}", [128, 4 * 128], BF16)
            r["wsT"] = S.sbuf(f"wsT{l}", [128, 4 * 128], BF16)
            r["lnG"] = S.sbuf(f"lnG{l}", [128, DB], BF16)
            r["lnB"] = S.sbuf(f"lnB{l}", [128, DB], BF16)
            r["bsz"] = S.sbuf(f"bsz{l}", [128, DB], BF16)
            r["carry"] = S.sbuf(f"carry{l}", [128, 2 * 16], F32)
            r["dt2"] = S.sbuf(f"dt2_{l}", [16, 4], F32)
            r["halo"] = S.sbuf(f"halo{l}", [128, 4 * 16], F32)
            L.append(r)

        XIN_LO = 2048 + 1024 + 1024 + 10 * 512 + 1056 + 1024 + 1024
        bank_ctr = [0]

        def next_bank():
            b = banks[bank_ctr[0] % 8]
            bank_ctr[0] += 1
            return b

        ev_ctr = [0]

        def evac_copy(out_ap, in_ap, eng=None):
            ev_ctr[0] += 1
            if eng == "act" or (eng is None and ev_ctr[0] % 2):
                S.op("act", lambda e: e.activation(out_ap, in_ap, AF.Copy), reads=[in_ap], writes=[out_ap])
            else:
                S.op("dve", lambda e: e.tensor_copy(out_ap, in_ap), reads=[in_ap], writes=[out_ap])

        def TT(eng, out, a, b, op):
            S.op(eng, lambda e: e.tensor_tensor(out, a, b, op), reads=[a, b], writes=[out])

        def TS(eng, out, a, s1, s2, op0, op1=None):
            rd = [a] + [s for s in (s1, s2) if not isinstance(s, (int, float)) and s is not None]
            if op1 is None:
                S.op(eng, lambda e: e.tensor_scalar(out, a, s1, None, op0), reads=rd, writes=[out])
            else:
                S.op(eng, lambda e: e.tensor_scalar(out, a, s1, s2, op0, op1), reads=rd, writes=[out])

        def STT(out, a, s, b, op0, op1):
            rd = [a, b] + ([] if isinstance(s, (int, float)) else [s])
            S.op("dve", lambda e: e.scalar_tensor_tensor(out, a, s, b, op0, op1), reads=rd, writes=[out])

        def ACT(out, in_, func, bias=None, scale=None, accum_out=None):
            kw = {}
            rd = [in_]
            wr = [out]
            if bias is not None:
                kw["bias"] = bias
                if not isinstance(bias, (int, float)):
                    rd.append(bias)
            if scale is not None:
                kw["scale"] = scale
                if not isinstance(scale, (int, float)):
                    rd.append(scale)
            if accum_out is not None:
                kw["accum_out"] = accum_out
                wr.append(accum_out)
            S.op("act", lambda e: e.activation(out, in_, func, **kw), reads=rd, writes=wr)

        def CP(eng, out, in_):
            if eng == "act":
                ACT(out, in_, AF.Copy)
            else:
                S.op(eng, lambda e: e.tensor_copy(out, in_), reads=[in_], writes=[out])

        def MS(eng, ap, val):
            S.op(eng, lambda e: e.memset(ap, val), writes=[ap])

        def MM(out, lhsT, rhs, start, stop):
            S.op("pe", lambda e: e.matmul(out, lhsT, rhs, start=start, stop=stop), reads=[lhsT, rhs], writes=[out])

        def TR(out, in_, idn):
            S.op("pe", lambda e: e.transpose(out, in_, idn), reads=[in_, idn], writes=[out])

        def bc(ap, shape, axis):
            return ap.unsqueeze(axis).to_broadcast(list(shape))

        S.dma("sp", cst[:], consts_d, "prep")
        CP("dve", identb[:], ident)
        CP("dve", selb[:], cst[:, 256 + NM + 64:256 + NM + 64 + 128])
        MS("pool", onesb[:], 1.0)
        MS("pool", epsc[:, 0:1], RMS_EPS)
        MS("pool", epsc[:, 1:2], LN_EPS)
        MS("pool", epsc[:, 2:3], -0.5)
        S.dma("sp", fst[:], bass.AP(dram["final_norm_g"], 0, [[128, KT], [1, 128]]), "prep")
        _bf = next_bank()
        TR(_bf[:, 0:KT], fst[:], ident[0:KT, 0:KT])
        CP("dve", fngc[:], _bf[:, 0:KT])

        def emit_wconv(l):
            for s in ORDER:
                dst = SCR(l, s)
                if s < 13:
                    src = DAP("w_in", l * D * DIN + s * 512, [[DIN, 128], [128 * DIN, KT], [1, 512]])
                    d3 = bass.AP(wscr, (l * NSLAB_ALL + s) * 128 * SLOT, [[SLOT, 128], [512, KT], [1, 512]])
                elif s == 13:
                    src = DAP("s5_w_glu", l * DB * DB, [[DB, 128], [128 * DB, 4], [1, DB]])
                    d3 = bass.AP(wscr, (l * NSLAB_ALL + s) * 128 * SLOT, [[SLOT, 128], [DB, 4], [1, DB]])
                elif s < 17:
                    k = s - 14
                    src = DAP("w_branch", (l * 3 + k) * DB * D, [[D, 128], [128 * D, 4], [1, D]])
                    d3 = bass.AP(wscr, (l * NSLAB_ALL + s) * 128 * SLOT, [[SLOT, 128], [D, 4], [1, D]])
                elif s < 19:
                    hlf = s - 17
                    src = DAP("w_out", l * D * D + hlf * 512, [[D, 128], [128 * D, KT], [1, 512]])
                    d3 = bass.AP(wscr, (l * NSLAB_ALL + s) * 128 * SLOT, [[SLOT, 128], [512, KT], [1, 512]])
                else:
                    continue
                S.dma("pool", d3, src, f"wc{l}_{s}", extra_writes=[dst])

        import os
        STOP = int(os.environ.get("PREP_STOP", "999"))

        class _Stop(Exception):
            pass

        def ck(n):
            if mode == "prep_test" and n == STOP:
                raise _Stop()

        load_bufs = []

        def emit_prep_early(l):
            R = L[l]
            S.dma("sp", R["dt2"][:, 0:2], DAP("s5_log_dt", l * 32, [[2, 16], [1, 2]]), "prep")
            MS("pool", R["dt2"][:, 2:4], float(np.e))
            TT("pool", R["dt2"][:, 0:2], R["dt2"][:, 2:4], R["dt2"][:, 0:2], ALU.pow)
            MS("pool", R["bsz"][:], 0.0)
            S.dma("pool", R["bsz"][0:1, :], DAP("sgu_b", l * DB, [[DB, 1], [1, DB]]), "prep")
            S.dma("pool", R["lnG"][:], DAP("sgu_ln_g", l * DB, [[0, 128], [1, DB]]), "prep")
            S.dma("pool", R["lnB"][:], DAP("sgu_ln_b", l * DB, [[0, 128], [1, DB]]), "prep")
            S.dma("pool", R["pw"][:].rearrange("p (g d) -> p g d", d=128), DAP("pool_w", l * 65536, [[128, 128], [16384, 4], [1, 128]]), "prep")
            MS("pool", R["carry"][:], 0.0)
            MS("pool", R["halo"][:], 0.0)

        def emit_prep(l, scratch):
            R = L[l]
            C = Carver(scratch, hole=(XIN_LO, XIN_LO + 4096) if scratch is AR else None)
            st16 = C.take(3 * 128, F32, parts=16)
            ldt = C.take(2, F32, parts=16)
            P16 = C.take(48, F32)
            S.dma("sp", st16[:, 0:128], DAP("s5_lam_re", l * 2048, [[128, 16], [1, 128]]), "prep")
            S.dma("sp", st16[:, 128:256], DAP("s5_lam_im", l * 2048, [[128, 16], [1, 128]]), "prep")
            Wn = C.take(512, F32)
            S.dma("sp", Wn.rearrange("p (h s) -> p h s", s=128), DAP("sgu_w", l * 65536, [[128, 128], [16384, 4], [1, 128]]), "prep")
            colst = C.take(128, F32, parts=16)
            S.dma("sp", colst[0:8, :], DAP("norm_g", l * D, [[128, 8], [1, 128]]), "prep")
            S.dma("sp", colst[8:12, :], DAP("s5_b_glu", l * DB, [[128, 4], [1, 128]]), "prep")
            S.dma("sp", colst[12:16, :], DAP("pool_scale", l * DB, [[128, 4], [1, 128]]), "prep")
            Bre = C.take(256, F32)
            Bim = C.take(256, F32)
            S.dma("sp", Bre.rearrange("p (g c) -> p g c", c=16), DAP("s5_b_re", l * 32768, [[16, 128], [2048, 16], [1, 16]]), "prep")
            S.dma("sp", Bim.rearrange("p (g c) -> p g c", c=16), DAP("s5_b_im", l * 32768, [[16, 128], [2048, 16], [1, 16]]), "prep")
            dst32 = C.take(16, F32, parts=32)
            S.dma("sp", dst32, DAP("s5_d", l * DB, [[16, 32], [1, 16]]), "prep")
            Cs = C.take(512, F32)
            m3 = C.off
            Cn = C.take(2 * 2048, F32, parts=16)
            S.dma("sp", Cn[:, 0:2048].rearrange("p (g q) -> p g q", q=64), DAP("s5_c_re", l * 32768, [[64, 16], [1024, 32], [1, 64]]), "prep")
            S.dma("sp", Cn[:, 2048:4096].rearrange("p (g q) -> p g q", q=64), DAP("s5_c_im", l * 32768, [[64, 16], [1024, 32], [1, 64]]), "prep")
            load_bufs.extend([st16[:, 0:256], Wn, colst, Bre, Bim, dst32, Cn])
            yield "loads"
            bC = next_bank()
            for ri in range(2):
                for gp in range(16):
                    TR(bC[:, ri * 256 + gp * 16: ri * 256 + gp * 16 + 16], Cn[:, ri * 2048 + gp * 128: ri * 2048 + (gp + 1) * 128], ident[0:16, 0:16])
            CP("dve", Cs, bC[:, :])
            C.off = m3
            CP("dve", ldt, R["dt2"][:, 0:2])
            CP("dve", st16[:, 256:384].rearrange("p (a b) -> p a b", a=2), bc(ldt, [16, 2, 64], 2))
            bA = next_bank()
            for k in range(3):
                TR(bA[:, k * 16:(k + 1) * 16], st16[:, k * 128:(k + 1) * 128], ident[0:16, 0:16])
            CP("dve", P16, bA[:, 0:48])
            ck(13)
            yield
            TT("dve", Wn.rearrange("p (h s) -> p h s", s=128), Wn.rearrange("p (h s) -> p h s", s=128), bc(sgumask, [128, 4, 128], 1), ALU.mult)
            bW = next_bank()
            for h in range(4):
                TR(bW[:, h * 128:(h + 1) * 128], Wn[:, h * 128:(h + 1) * 128], ident)
            CP("dve", R["wsT"][:], bW[:, :])
            if mode == "prep_test":
                dbg.update(wsT=R["wsT"][:])
            ck(14)
            yield
            ck(15)
            yield
            bE = next_bank()
            TR(bE[:, 0:16], colst, ident[0:16, 0:16])
            CP("dve", R["cols"][:], bE[:, 0:16])
            ck(1)
            yield
            zsrc = bass.AP(zeros_t, 0, [[SLOT // 2, 128], [1, SLOT // 2]]).bitcast(BF16)
            for s_ in (20, 21, 22):
                S.dma("sp", SCR(l, s_), zsrc, f"zf{l}")
            lr, li, dt = P16[:, 0:16], P16[:, 16:32], P16[:, 32:48]
            sm = C.take(16 * 12, F32)
            smv = [sm[:, i * 16:(i + 1) * 16] for i in range(12)]
            xx, th, den, rden, nr, t0, t1, kre, kim, t2, t3, t4 = smv
            TT("dve", xx, lr, dt, ALU.mult)
            TT("dve", th, li, dt, ALU.mult)
            TT("dve", t0, lr, lr, ALU.mult)
            TT("dve", t1, li, li, ALU.mult)
            TT("dve", den, t0, t1, ALU.add)
            S.op("dve", lambda e: e.reciprocal(rden, den), reads=[den], writes=[rden])
            ck(2)
            yield
            TN = 16 * NM
            SIN = C.take(TN, F32)
            COS = C.take(TN, F32)
            m1 = C.off
            Tt = C.take(TN, F32)
            Ni = C.take(TN, I32)
            Nf = C.take(TN, F32)
            MAG = C.take(TN, F32)
            v3 = lambda a: a.rearrange("p (g m) -> p g m", m=NM)
            thB = bc(th, [128, 16, NM], 2)
            xxB = bc(xx, [128, 16, NM], 2)
            multB = bc(mult, [128, 16, NM], 1)
            STT(v3(Tt), thB, 1.0 / (2.0 * np.pi), multB, ALU.mult, ALU.mult)
            CP("dve", Ni, Tt)
            CP("dve", Nf, Ni)
            TT("dve", Nf, Tt, Nf, ALU.subtract)
            ACT(SIN, Nf, AF.Sin, scale=TWO_PI_SAFE)
            TS("dve", Tt, Tt, 0.25, None, ALU.add)
            CP("dve", Ni, Tt)
            CP("dve", Nf, Ni)
            TT("dve", Nf, Tt, Nf, ALU.subtract)
            ACT(COS, Nf, AF.Sin, scale=TWO_PI_SAFE)
            TT("dve", v3(Tt), xxB, multB, ALU.mult)
            ACT(MAG, Tt, AF.Exp)
            ck(3)
            yield
            CP("dve", R["Ec"][:].rearrange("p (g q) -> p g q", q=NQ), v3(COS)[:, :, 17:17 + NQ])
            CP("dve", R["Es"][:].rearrange("p (g q) -> p g q", q=NQ), v3(SIN)[:, :, 17:17 + NQ])
            CP("dve", R["r8"][:], v3(MAG)[:, :, 8])
            if mode == "prep_test":
                dbg.update(Ec=R["Ec"][:], Es=R["Es"][:], r8=R["r8"][:])
            Ar, Ai = COS, SIN
            TT("dve", Ar, MAG, COS, ALU.mult)
            TT("dve", Ai, MAG, SIN, ALU.mult)
            C.off = m1
            ck(4)
            yield
            TS("dve", nr, v3(Ar)[:, :, 1], -1.0, None, ALU.add)
            ni = v3(Ai)[:, :, 1]
            TT("dve", t0, nr, lr, ALU.mult)
            TT("dve", t1, ni, li, ALU.mult)
            TT("dve", t0, t0, t1, ALU.add)
            TT("dve", kre, t0, rden, ALU.mult)
            TT("dve", t2, ni, lr, ALU.mult)
            TT("dve", t3, nr, li, ALU.mult)
            TT("dve", t2, t2, t3, ALU.subtract)
            TT("dve", kim, t2, rden, ALU.mult)
            cre = C.take(128, F32)
            cim = C.take(128, F32)
            ct = C.take(128, F32)
            c3 = lambda a: a.rearrange("p (g i) -> p g i", i=8)
            ArW, AiW = v3(Ar)[:, :, 9:17], v3(Ai)[:, :, 9:17]
            kreB, kimB = bc(kre, [128, 16, 8], 2), bc(kim, [128, 16, 8], 2)
            TT("dve", c3(cre), ArW, kreB, ALU.mult)
            TT("dve", c3(ct), AiW, kimB, ALU.mult)
            TT("dve", cre, cre, ct, ALU.subtract)
            TT("dve", c3(cim), ArW, kimB, ALU.mult)
            TT("dve", c3(ct), AiW, kreB, ALU.mult)
            TT("dve", cim, cim, ct, ALU.add)
            ck(5)
            yield
            WWre = C.take(2048, F32)
            WWim = C.take(2048, F32)
            m2 = C.off
            WWt = C.take(2048, F32)
            w4 = lambda a: a.rearrange("p (g i c) -> p g i c", i=8, c=16)
            b3 = lambda a: a.rearrange("p (g c) -> p g c", c=16)
            creB, cimB = bc(c3(cre), [128, 16, 8, 16], 3), bc(c3(cim), [128, 16, 8, 16], 3)
            BreB, BimB = bc(b3(Bre), [128, 16, 8, 16], 2), bc(b3(Bim), [128, 16, 8, 16], 2)
            TT("dve", w4(WWre), creB, BreB, ALU.mult)
            TT("dve", w4(WWt), cimB, BimB, ALU.mult)
            TT("dve", WWre, WWre, WWt, ALU.subtract)
            TT("dve", w4(WWim), creB, BimB, ALU.mult)
            TT("dve", w4(WWt), cimB, BreB, ALU.mult)
            TT("dve", WWim, WWim, WWt, ALU.add)
            ck(6)
            yield
            Wfin = C.take(SLOT, BF16)
            for ri, WW in enumerate((WWre, WWim)):
                for k4 in range(4):
                    bk = next_bank()
                    for j in range(4):
                        gp = k4 * 4 + j
                        TR(bk[:, j * 128:(j + 1) * 128], WW[:, gp * 128:(gp + 1) * 128], ident)
                    dst = Wfin.rearrange("p (g r q) -> p g r q", r=2, q=64)[:, 8 * k4:8 * k4 + 8, ri, :]
                    evac_copy(dst, bk[:, :].rearrange("p (g q) -> p g q", q=64))
            S.dma("act", SCR(l, 19), Wfin, "prep")
            if mode == "prep_test":
                dbg.update(Wfin=Wfin)
            if mode != "prep_test":
                C.off = m2
            ck(7)
            yield
            Cre, Cim = b3(Cs[:, 0:256]), b3(Cs[:, 256:512])
            ck(8)
            yield
            Vcb = [C.take(16 * 9 * 16, BF16), C.take(16 * 9 * 16, BF16)]
            W7 = C.take(2 * 256, BF16)
            m4 = C.off
            VVre = C.take(16 * 9 * 16, F32)
            VVim = C.take(16 * 9 * 16, F32)
            VVt = C.take(16 * 9 * 16, F32)
            vv4 = lambda a: a.rearrange("p (g m c) -> p g m c", m=9, c=16)
            CreB, CimB = bc(Cre, [128, 16, 9, 16], 2), bc(Cim, [128, 16, 9, 16], 2)
            ArB, AiB = bc(v3(Ar)[:, :, 0:9], [128, 16, 9, 16], 3), bc(v3(Ai)[:, :, 0:9], [128, 16, 9, 16], 3)
            TT("dve", vv4(VVre), CreB, ArB, ALU.mult)
            TT("dve", vv4(VVt), CimB, AiB, ALU.mult)
            TT("dve", VVre, VVre, VVt, ALU.subtract)
            TT("dve", vv4(VVim), CreB, AiB, ALU.mult)
            TT("dve", vv4(VVt), CimB, ArB, ALU.mult)
            TT("dve", VVim, VVim, VVt, ALU.add)
            TS("dve", VVim, VVim, -1.0, None, ALU.mult)
            ck(9)
            yield
            Vc9 = []
            for ri, VV in enumerate((VVre, VVim)):
                Vc = Vcb[ri]
                CP("dve", Vc, VV)
                Vc9.append(Vc)
                Vc4 = Vc.rearrange("p (g m c) -> p g m c", m=9, c=16)
                base = (l * NSLAB_ALL + 21 + ri) * 128 * SLOT
                for g2 in range(2):
                    dst = bass.AP(wscr, base + g2 * 64 * SLOT + g2 * 128, [[SLOT, 64], [256, 16], [8 * 16, 1], [1, 128]])
                    S.dma("pool", dst, Vc4[g2 * 64:(g2 + 1) * 64, :, 1:9, :].rearrange("p g m c -> p g (m c)"), f"vp{l}{ri}", parallel=True)
            CP("dve", W7[:, 0:256].rearrange("p (g c) -> p g c", c=16), w4(WWre)[:, :, 7, :])
            CP("dve", W7[:, 256:512].rearrange("p (g c) -> p g c", c=16), w4(WWim)[:, :, 7, :])
            ck(10)
            yield
            if mode != "prep_test":
                C.off = m4
            bD = next_bank()
            TR(bD[0:16, 0:32], dst32, ident[0:32, 0:32])
            Dg = C.take(32, F32, parts=16)
            CP("dve", Dg, bD[0:16, 0:32])
            tmpD = C.take(512, F32, parts=16)
            TT("dve", tmpD.rearrange("p (g c) -> p g c", c=16), bc(ident[0:16, 0:16], [16, 32, 16], 1), bc(Dg, [16, 32, 16], 2), ALU.mult)
            tD4 = tmpD.rearrange("p (gp h c) -> p gp h c", h=2, c=16)
            Krb = C.take(32 * 128, BF16, parts=16)
            Kb4 = Krb.rearrange("p (gp h f) -> p gp h f", h=2, f=128)
            Kb5 = Krb.rearrange("p (gp h m c) -> p gp h m c", h=2, m=8, c=16)
            for g2 in range(2):
                rows = slice(g2 * 64, (g2 + 1) * 64)
                for k4 in range(4):
                    bk = next_bank()
                    for j in range(4):
                        gp = k4 * 4 + j
                        o = bk[0:16, j * 128:(j + 1) * 128]
                        MM(o, W7[rows, gp * 16:(gp + 1) * 16], Vc9[0][rows, gp * 144:gp * 144 + 128], True, False)
                        MM(o, W7[rows, 256 + gp * 16:256 + (gp + 1) * 16], Vc9[1][rows, gp * 144:gp * 144 + 128], False, True)
                    CP("act", Kb4[:, k4 * 4:(k4 + 1) * 4, g2, :], bk[0:16, :].rearrange("p (j f) -> p j f", f=128))
                    TT("dve", Kb5[:, k4 * 4:(k4 + 1) * 4, g2, 0, :], bk[0:16, :].rearrange("p (j f) -> p j f", f=128)[:, :, 0:16],
                       tD4[:, k4 * 4:(k4 + 1) * 4, g2, :], ALU.add)
            ck(11)
            yield
            if mode == "prep_test":
                dbg.update(Krow=Krb)
            K4 = Krb.rearrange("p (g m c) -> p g m c", m=8, c=16)
            base = (l * NSLAB_ALL + 20) * 128 * SLOT
            for i in range(8):
                dst = bass.AP(wscr, base + 16 * i * SLOT + i * 16, [[SLOT, 16], [128, 32], [1, (8 - i) * 16]])
                S.dma("pool", dst, K4[:, :, 0:8 - i, :].rearrange("p g m c -> p g (m c)"), f"tp{l}", parallel=True)
            if mode == "prep_test":
                dbg.update(cols=R["cols"][:])

        if mode == "prep_test":
            emit_prep_early(0)
            try:
                for _ in emit_prep(0, AR):
                    pass
            except _Stop:
                dbg["P16"] = cst[:, 0:16]
            outs = []
            for k, ap in dbg.items():
                shp = list(ap.shape)
                o = nc.dram_tensor("o_" + k, shp, ap.dtype, kind="ExternalOutput").ap()
                S.dma("sp", o, ap, "dbgout")
                outs.append("dbgout")
            S.emit(final_streams=["dbgout"])
            return nc

        nblk = (SPC * BPS) if nblocks is None else nblocks
        plan = [(b, l, s_) for b in range(nblk) for l in layers for s_ in ORDER]
        rs = {"next_load": 0, "next_use": 0, "released": set()}

        def _pump():
            while rs["next_load"] < len(plan):
                k = rs["next_load"]
                if k >= rs["next_use"] + NRING:
                    break
                if k >= NRING and (k - NRING) not in rs["released"]:
                    break
                _, l_, s_ = plan[k]
                S.dma("sp", ring[k % NRING][:], SCR(l_, s_), f"ring{k % NRING}")
                rs["next_load"] += 1

        def acquire(l_, s_):
            k = rs["next_use"]
            assert plan[k][1:] == (l_, s_), (plan[k], l_, s_)
            rs["next_use"] += 1
            _pump()
            assert rs["next_load"] > k, "ring deadlock"
            return k, ring[k % NRING]

        def release(k):
            rs["released"].add(k)
            _pump()

        def AV(off, nelem, dtype, parts=128):
            words = (nelem * mybir.dt.size(dtype) + 3) // 4
            ap = AR[0:parts, off:off + words]
            if dtype != F32:
                ap = ap.bitcast(dtype)
            return ap[:, 0:nelem]

        o = 0
        Atok = AV(o, 4096, BF16); Ysb = Atok; o += 2048
        Xim = AV(o, 2048, BF16); o += 1024
        agT = AV(o, 2048, BF16); o += 1024
        st_ = []
        for _ in range(10):
            st_.append(AV(o, 512, F32)); o += 512
        tA, tB, tC, tD, Gin_re, Gin_im, G_re, G_im, H_re, H_im = st_
        Hs = [AV(o, 16 * 65, BF16), AV(o + 528, 16 * 65, BF16)]; o += 1056
        ygT = AV(o, 2048, BF16); o += 1024
        sig = AV(o, 2048, BF16); o += 1024
        T3o = o
        bvT = AV(o, 4 * 528, F32); o += 2112
        sA = AV(o, 528, F32); o += 528
        sB = AV(o, 528, F32); o += 528
        pT = AV(o, 2048, BF16); o += 1024
        bgT = AV(o, 2048, BF16); o += 1024
        cuT = AV(o, 2048, BF16); o += 1024
        cgT = AV(o, 2048, BF16); o += 1024
        vn = AV(o, 2048, BF16); o += 1024
        lnst = AV(o, 64, F32); o += 64
        st2_ = []
        for _ in range(6):
            st2_.append(AV(o, 512, F32)); o += 512
        assert o <= AR_WORDS, o
        sq = AV(0, 4096, BF16)
        rstd = AV(2048, 512, F32)
        rstd2 = AV(2560, 512, F32)
        gk = AV(0, 4096, BF16)
        mergedF = AV(2048, 4096, F32)
        mergedT = AV(6144, 4096, BF16)
        mtmp = [AV(8192, 512, F32), AV(8704, 512, F32)]
        xtok = AV(4096, 4096, F32)
        fstat = AV(9216, 16, F32)
        junkT = AV(2048 + 1024, 1024, BF16)

        def v(ap, pat, **kw):
            return ap.rearrange(pat, **kw)

        def emit_layer(b, l, after_l1=None, x_src=None):
            R = L[l]
            PL = "dve" if (b == 0 and l == layers[0]) else "pool"
            first = (b % BPS) == 0
            gcol = R["cols"]
            S.stage = f'b{b}l{l}:L1'
            xs = xT if x_src is None else x_src
            bk = next_bank()
            for kt in range(KT):
                ACT(sq[:, kt * NB:(kt + 1) * NB], xs[:, kt * NB:(kt + 1) * NB], AF.Square)
                MM(bk[:, :], onesb[:], sq[:, kt * NB:(kt + 1) * NB], kt == 0, kt == KT - 1)
            ACT(rstd2, bk[:, :], AF.Sqrt, bias=epsc[:, 0:1], scale=1.0 / D)
            S.op("dve", lambda e: e.reciprocal(rstd, rstd2), reads=[rstd2], writes=[rstd])
            for kt in range(KT):
                STT(hT[:, kt * NB:(kt + 1) * NB], xs[:, kt * NB:(kt + 1) * NB], gcol[:, kt:kt + 1], rstd, ALU.mult, ALU.mult)

            def fm_tiles(slot, ncol_tiles, consume):
                for m in range(ncol_tiles):
                    bk_ = next_bank()
                    for kt in range(KT):
                        MM(bk_[:, :], slot[:, kt * 512 + m * 128: kt * 512 + (m + 1) * 128], hT[:, kt * NB:(kt + 1) * NB], kt == 0, kt == KT - 1)
                    consume(m, bk_)

            S.stage = f'b{b}l{l}:aval'
            k0, sl = acquire(l, 0)
            A2 = v(Atok[:, 0:2048], "p (g i c) -> p g i c", i=4, c=16)
            ab = [next_bank() for _ in range(4)]
            for kt in range(KT):
                for i0 in range(4):
                    lh = v(hT[:, kt * NB:(kt + 1) * NB], "p (m i) -> p m i", i=4)[:, :, i0]
                    MM(ab[i0][:, :], lh, sl[:, kt * 512:(kt + 1) * 512], kt == 0, kt == KT - 1)
            for i0 in range(4):
                evac_copy(A2[:, :, i0, :], v(ab[i0][:, :], "p (g c) -> p g c", c=16))
            release(k0)
            S.stage = f'b{b}l{l}:trin'
            for g8 in range(4):
                bk = next_bank()
                for j in range(8):
                    g = g8 * 8 + j
                    for h in range(2):
                        MM(bk[h * 64:(h + 1) * 64, j * NQ:(j + 1) * NQ], Atok[:, g * 64:(g + 1) * 64], selb[:, h * 64:(h + 1) * 64], True, True)
                evac_copy(Xim[:, g8 * 8 * NQ:(g8 + 1) * 8 * NQ], bk[:, 0:8 * NQ])
            if after_l1 is not None:
                after_l1()
            S.stage = f'b{b}l{l}:S'
            k1, wf = acquire(l, 19)
            Sb = []
            for hf in range(2):
                bre, bim = next_bank(), next_bank()
                for gpl in range(8):
                    gp = hf * 8 + gpl
                    for g2 in range(2):
                        g = 2 * gp + g2
                        for ri, bk in enumerate((bre, bim)):
                            MM(bk[g2 * 64:(g2 + 1) * 64, gpl * NQ:(gpl + 1) * NQ], wf[:, g * 128 + ri * 64: g * 128 + (ri + 1) * 64],
                               Xim[:, g * NQ:(g + 1) * NQ], True, True)
                Sb.append((bre, bim))
            release(k1)
            S.stage = f'b{b}l{l}:rot'
            carry = R["carry"]
            for ri in range(2):
                hs3 = v(Hs[ri], "p (g q) -> p g q", q=65)
                if first:
                    MS(PL, hs3[:, :, 0], 0.0)
                else:
                    CP(PL, hs3[:, :, 0], carry[:, ri * 16:(ri + 1) * 16])
            W_ = 8 * NQ
            pre = [(tA, tB, tC, tD, Gin_re, Gin_im), tuple(st2_)]
            for hf in range(2):
                bre, bim = Sb[hf]
                Ec = R["Ec"][:, hf * 8 * NQ:(hf + 1) * 8 * NQ]
                Es = R["Es"][:, hf * 8 * NQ:(hf + 1) * 8 * NQ]
                a_, b_, c_, d_, gr_, gi_ = pre[hf]
                TT("dve", a_, bre[:, 0:W_], Ec, ALU.mult)
                TT("dve", d_, bre[:, 0:W_], Es, ALU.mult)
                TT("dve", b_, bim[:, 0:W_], Es, ALU.mult)
                TT("dve", c_, bim[:, 0:W_], Ec, ALU.mult)
                TT(PL, gr_, a_, b_, ALU.add)
                TT(PL, gi_, c_, d_, ALU.subtract)
            for hf in range(2):
                Ec = R["Ec"][:, hf * 8 * NQ:(hf + 1) * 8 * NQ]
                Es = R["Es"][:, hf * 8 * NQ:(hf + 1) * 8 * NQ]
                for ri, (Gin, G) in enumerate(((pre[hf][4], G_re), (pre[hf][5], G_im))):
                    for gpl in range(8):
                        gp = hf * 8 + gpl
                        d0 = R["r8"][:, gp:gp + 1].to_broadcast([128, NQ])
                        init = 0.0 if first else carry[:, ri * 16 + gp: ri * 16 + gp + 1]
                        o_ = G[:, gpl * NQ:(gpl + 1) * NQ]
                        i_ = Gin[:, gpl * NQ:(gpl + 1) * NQ]
                        rd = [R["r8"][:, gp:gp + 1], i_] + ([] if first else [init])
                        S.op("dve", lambda e, o_=o_, d0=d0, i_=i_, init=init: e.tensor_tensor_scan(o_, d0, i_, init, ALU.mult, ALU.add),
                             reads=rd, writes=[o_])
                TT("dve", tA, G_re, Ec, ALU.mult)
                TT("dve", tC, G_re, Es, ALU.mult)
                TT("dve", tB, G_im, Es, ALU.mult)
                TT("dve", tD, G_im, Ec, ALU.mult)
                TT(PL, H_re, tA, tB, ALU.subtract)
                TT(PL, H_im, tC, tD, ALU.add)
                for ri, H in enumerate((H_re, H_im)):
                    hs3 = v(Hs[ri], "p (g q) -> p g q", q=65)
                    h3 = v(H, "p (g q) -> p g q", q=NQ)
                    CP("dve", hs3[:, hf * 8:(hf + 1) * 8, 1:NQ + 1], h3)
                    CP(PL, carry[:, ri * 16 + hf * 8: ri * 16 + (hf + 1) * 8], h3[:, :, NQ - 1])
            S.stage = f'b{b}l{l}:win'
            k2, sl = acquire(l, 1)
            fm_tiles(sl, 4, lambda m, bk_: ACT(agT[:, m * NB:(m + 1) * NB], bk_[:, :], AF.Silu))
            release(k2)
            bv3 = v(bvT, "p (g t) -> p g t", t=528)
            if first:
                MS(PL, bv3[:, :, 0:16], 0.0)
            else:
                CP(PL, bv3[:, :, 0:16], v(R["halo"][:], "p (g t) -> p g t", t=16))
            k7, sl = acquire(l, 2)
            fm_tiles(sl, 4, lambda m, bk_: evac_copy(bv3[:, m, 16:528], bk_[:, :], "act"))
            release(k7)
            CP(PL, v(R["halo"][:], "p (g t) -> p g t", t=16), bv3[:, :, 512:528])
            k8, sl = acquire(l, 3)
            fm_tiles(sl, 4, lambda m, bk_: ACT(bgT[:, m * NB:(m + 1) * NB], bk_[:, :], AF.Silu))
            release(k8)
            k9, sl = acquire(l, 4)
            fm_tiles(sl, 4, lambda m, bk_: evac_copy(cuT[:, m * NB:(m + 1) * NB], bk_[:, :], "act"))
            release(k9)
            k11, sl = acquire(l, 6)
            fm_tiles(sl, 4, lambda m, bk_: ACT(cgT[:, m * NB:(m + 1) * NB], bk_[:, :], AF.Silu))
            release(k11)
            S.stage = f'b{b}l{l}:poolel'
            for m in range(4):
                u = bv3[:, m, :]
                w = 2 ** (m + 1)
                cur, nxt = sA, sB
                TT(PL, cur[:, 1:528], u[:, 1:528], u[:, 0:527], ALU.add)
                sh = 2
                while sh < w:
                    lo = 2 * sh - 1
                    TT(PL, nxt[:, lo:528], cur[:, lo:528], cur[:, lo - sh:528 - sh], ALU.add)
                    cur, nxt = nxt, cur
                    sh *= 2
                STT(pT[:, m * NB:(m + 1) * NB], cur[:, 16:528], 1.0 / w, u[:, 16:528], ALU.mult, ALU.subtract)
                if first:
                    TT(PL, nxt[:, 0:16], cur[:, 16:32], rcfix[:, m * 16:(m + 1) * 16], ALU.mult)
                    TT(PL, pT[:, m * NB:m * NB + 16], nxt[:, 0:16], u[:, 16:32], ALU.subtract)
            TT("dve", cuT, cuT, cgT, ALU.mult)
            S.stage = f'b{b}l{l}:Y'
            k3, tp = acquire(l, 20)
            k4_, vre = acquire(l, 21)
            k5, vim = acquire(l, 22)
            Y5 = v(Ysb[0:NQ, :], "q (f j g c) -> q f j g c", f=4, j=8, g=8)
            for ft in range(4):
                for hb in range(2):
                    bk = next_bank()
                    for pj in range(2):
                        gp = ft * 4 + hb * 2 + pj
                        first_mm = pj == 0
                        cols = slice(pj * 256, (pj + 1) * 256)
                        MM(bk[0:NQ, cols], v(Hs[0], "p (g q) -> p g q", q=65)[:, gp, 0:NQ], vre[:, gp * 256:(gp + 1) * 256], first_mm, False)
                        MM(bk[0:NQ, cols], v(Hs[1], "p (g q) -> p g q", q=65)[:, gp, 0:NQ], vim[:, gp * 256:(gp + 1) * 256], False, False)
                        for g2 in range(2):
                            g = 2 * gp + g2
                            c0 = pj * 256 + g2 * 128
                            MM(bk[0:NQ, c0:c0 + 128], Xim[:, g * NQ:(g + 1) * NQ], tp[:, g * 128:(g + 1) * 128], False, pj == 1 and g2 == 1)
                    evac_copy(Y5[:, ft, :, hb * 4:(hb + 1) * 4, :].rearrange("q j g c -> q g j c"), v(bk[0:NQ, :], "q (g j c) -> q g j c", j=8, c=16))
            release(k3); release(k4_); release(k5)
            S.stage = f'b{b}l{l}:trout'
            for ft in range(4):
                bk = next_bank()
                bkb = bk[:, :].bitcast(BF16)
                for j in range(8):
                    TR(bkb[:, j * NQ:(j + 1) * NQ], Ysb[0:NQ, (ft * 8 + j) * 128:(ft * 8 + j + 1) * 128], identb[0:NQ, 0:NQ])
                ACT(v(ygT[:, ft * NB:(ft + 1) * NB], "p (q j) -> p j q", j=8), v(bkb[:, 0:8 * NQ], "p (j q) -> p j q", j=8), AF.Gelu_apprx_tanh)
            S.stage = f'b{b}l{l}:glu'
            k6, sl = acquire(l, 13)
            gb = [next_bank() for _ in range(4)]
            for kt in range(4):
                for m in range(4):
                    MM(gb[m][:, :], sl[:, kt * 512 + m * 128: kt * 512 + (m + 1) * 128], ygT[:, kt * NB:(kt + 1) * NB], kt == 0, kt == 3)
            for m in range(4):
                ACT(sig[:, m * NB:(m + 1) * NB], gb[m][:, :], AF.Sigmoid, bias=gcol[:, 8 + m:9 + m])
            release(k6)
            TT("dve", yT[0][:], ygT, sig, ALU.mult)
            TT("dve", yT[0][:], yT[0][:], agT, ALU.mult)
            S.stage = f'b{b}l{l}:cv'
            k10, sl = acquire(l, 5)
            for tt in range(NTT):
                bk = next_bank()
                for kt in range(KT):
                    MM(bk[:, :], hT[:, kt * NB + tt * 128: kt * NB + (tt + 1) * 128], sl[:, kt * 512:(kt + 1) * 512], kt == 0, kt == KT - 1)
                st6 = lnst[:, tt * 6:(tt + 1) * 6]
                mv = lnst[:, 24 + tt * 2: 24 + (tt + 1) * 2]
                rsd = lnst[:, 32 + tt: 33 + tt]
                S.op("dve", lambda e, st6=st6, bk=bk: e.bn_stats(st6, bk[:, :]), reads=[bk[:, :]], writes=[st6])
                S.op("dve", lambda e, st6=st6, mv=mv: e.bn_aggr(mv, st6), reads=[st6], writes=[mv])
                TS("pool", rsd, mv[:, 1:2], LN_EPS, None, ALU.add)
                TT("pool", rsd, rsd, epsc[:, 2:3], ALU.pow)
                TS("dve", vn[:, tt * DB:(tt + 1) * DB], bk[:, :], mv[:, 0:1], rsd, ALU.subtract, ALU.mult)
                TT("dve", vn[:, tt * DB:(tt + 1) * DB], vn[:, tt * DB:(tt + 1) * DB], R["lnG"][:], ALU.mult)
                TT("dve", vn[:, tt * DB:(tt + 1) * DB], vn[:, tt * DB:(tt + 1) * DB], R["lnB"][:], ALU.add)
            release(k10)
            def merge_branch(k):
                for hg in range(2):
                    kg, sl_ = acquire(l, 7 + 2 * k + hg)
                    fm_tiles(sl_, 4, lambda m, bk_, hg=hg: ACT(gk[:, (hg * 4 + m) * NB:(hg * 4 + m + 1) * NB], bk_[:, :], AF.Sigmoid))
                    release(kg)
                kb, sl_ = acquire(l, 14 + k)
                for d8 in range(8):
                    bk_ = next_bank()
                    for kt in range(4):
                        MM(bk_[:, :], sl_[:, kt * D + d8 * 128: kt * D + (d8 + 1) * 128], yT[k][:, kt * NB:(kt + 1) * NB], kt == 0, kt == 3)
                    gsl = gk[:, d8 * NB:(d8 + 1) * NB]
                    mf = mergedF[:, d8 * NB:(d8 + 1) * NB]
                    if k == 0:
                        TT("dve", mf, bk_[:, :], gsl, ALU.mult)
                    else:
                        tmp = mtmp[d8 % 2]
                        TT("dve", tmp, bk_[:, :], gsl, ALU.mult)
                        if k == 1:
                            TT(PL, mf, mf, tmp, ALU.add)
                        else:
                            TT("dve" if d8 % 2 else PL, mergedT[:, d8 * NB:(d8 + 1) * NB], mf, tmp, ALU.add)
                release(kb)

            S.stage = f'b{b}l{l}:mergeA'
            merge_branch(0)
            S.stage = f'b{b}l{l}:sgu'
            for h in range(4):
                bk = next_bank()
                for tt in range(NTT):
                    o_ = bk[:, tt * 128:(tt + 1) * 128]
                    MM(o_, vn[:, tt * DB + h * 128: tt * DB + (h + 1) * 128], R["wsT"][:, h * 128:(h + 1) * 128], tt == 0, False)
                    MM(o_, onesb[:], R["bsz"][:, h * 128:(h + 1) * 128], False, tt == NTT - 1)
                TT("dve", yT[2][:, h * NB:(h + 1) * NB], bk[:, :], cuT[:, h * NB:(h + 1) * NB], ALU.mult)
            S.stage = f'b{b}l{l}:poolmm'
            for m in range(4):
                bk = next_bank()
                MM(bk[:, :], R["pw"][:, m * 128:(m + 1) * 128], pT[:, m * NB:(m + 1) * NB], True, True)
                STT(yT[1][:, m * NB:(m + 1) * NB], bk[:, :], gcol[:, 12 + m:13 + m], bgT[:, m * NB:(m + 1) * NB], ALU.mult, ALU.mult)

            if l == layers[-1] and b + 1 < nblk:
                emit_x_load(b + 1)
            S.stage = f'b{b}l{l}:mergeB'
            merge_branch(1)
            S.stage = f'b{b}l{l}:mergeC'
            merge_branch(2)
            S.stage = f'b{b}l{l}:wout'
            ko0, sl0 = acquire(l, 17)
            ko1, sl1 = acquire(l, 18)
            ob = [next_bank() for _ in range(8)]
            for kt in range(KT):
                for d8 in range(8):
                    sl_ = sl0 if d8 < 4 else sl1
                    m = d8 % 4
                    MM(ob[d8][:, :], sl_[:, kt * 512 + m * 128: kt * 512 + (m + 1) * 128], mergedT[:, kt * NB:(kt + 1) * NB], kt == 0, kt == KT - 1)
            for d8 in range(8):
                TT("dve", xT[:, d8 * NB:(d8 + 1) * NB], xT[:, d8 * NB:(d8 + 1) * NB], ob[d8][:, :], ALU.add)
            release(ko0); release(ko1)

        assert T3o == XIN_LO, (T3o, XIN_LO)
        xin = AV(T3o, 4096, F32)

        def emit_x_load(b):
            S.dma("sp", v(xin, "p (k t) -> p k t", t=NB), bass.AP(x_d.tensor, b * NB, [[NTOK, 128], [128 * NTOK, KT], [1, NB]]), "xin")

        fsq = AV(10272, 4096, BF16)
        frs2 = AV(3072, 512, F32)
        frs = AV(3584, 512, F32)

        def emit_x_to_xT():
            S.dma("sp", xT[:, :], xin, "x2x")

        def emit_final_norm(b):
            S.stage = f'b{b}:fnorm'
            ost = xtok
            if do_final:
                bk = next_bank()
                for kt in range(KT):
                    ACT(fsq[:, kt * NB:(kt + 1) * NB], xT[:, kt * NB:(kt + 1) * NB], AF.Square)
                    MM(bk[:, :], onesb[:], fsq[:, kt * NB:(kt + 1) * NB], kt == 0, kt == KT - 1)
                ACT(frs2, bk[:, :], AF.Sqrt, bias=epsc[:, 0:1], scale=1.0 / D)
                S.op("dve", lambda e: e.reciprocal(frs, frs2), reads=[frs2], writes=[frs])
                for kt in range(KT):
                    STT(ost[:, kt * NB:(kt + 1) * NB], xT[:, kt * NB:(kt + 1) * NB], fngc[:, kt:kt + 1], frs, ALU.mult, ALU.mult)
            else:
                for kt in range(KT):
                    CP("dve", ost[:, kt * NB:(kt + 1) * NB], xT[:, kt * NB:(kt + 1) * NB])
            S.dma("sp", bass.AP(out_d.tensor, b * NB, [[NTOK, 128], [128 * NTOK, KT], [1, NB]]), v(ost, "p (k t) -> p k t", t=NB), "out0")

        for l in layers:
            emit_prep_early(l)
        gens = [emit_prep(l, AR if i == 0 else BIG) for i, l in enumerate(layers)]
        for g_ in gens:
            next(g_)
        emit_x_load(0)
        gate = S.sbuf("gate", [128, 2], F32)
        S.op("pool", lambda e: e.memset(gate[:], 0.0), reads=list(load_bufs), writes=[gate[:]])
        emit_wconv(layers[0])
        while gens:
            for g_ in list(gens):
                try:
                    next(g_)
                except StopIteration:
                    gens.remove(g_)
        for l in layers[1:]:
            emit_wconv(l)
        for b in range(nblk):
            for li, l in enumerate(layers):
                if li == 0:
                    def hook(b=b):
                        if b > 0:
                            emit_final_norm(b - 1)
                        emit_x_to_xT()
                else:
                    hook = None
                emit_layer(b, l, hook, x_src=(xin if li == 0 else None))
        emit_final_norm(nblk - 1)
        S.emit(final_streams=["out0"])
        build_program.last_sched = S
    return nc


def kernel(**inputs):
    x = np.asarray(inputs["x"], dtype=np.float32)
    nc = build_program("full")
    consts = pack_consts()
    weights = {n: np.ascontiguousarray(np.asarray(inputs[n], dtype=np.float32)) for n in WEIGHT_NAMES}
    in_maps = []
    for c in range(NCORES):
        m = dict(weights)
        m["x"] = np.ascontiguousarray(x[c * SPC:(c + 1) * SPC].reshape(SPC * SEQ, D).T)
        m["consts"] = consts
        m["zeros"] = np.zeros((128, SLOT // 2), np.float32)
        in_maps.append(m)
    res = run_bass_kernel_spmd(nc, in_maps, core_ids=list(range(NCORES)))
    out = np.concatenate([np.ascontiguousarray(np.asarray(r["out"]).T).reshape(SPC, SEQ, D) for r in res.results], axis=0)
    return out.astype(np.float32)
```

```python
import numpy as np
from contextlib import ExitStack
import concourse.bass as bass
import concourse.mybir as mybir
from concourse.bass_utils import run_bass_kernel_spmd

F32 = mybir.dt.float32
BF16 = mybir.dt.bfloat16
I32 = mybir.dt.int32
AF = mybir.ActivationFunctionType
ALU = mybir.AluOpType
AX = mybir.AxisListType

D = 1024
SEQ = 2048
BATCH = 16
DEPTH = 2
NCORES = 8
SPC = BATCH // NCORES
DB = 512
DIN = 6656
KT = D // 128
NB = 512
NQ = NB // 8
NTT = NB // 128
BPS = SEQ // NB
RMS_EPS = 1e-6
LN_EPS = 1e-5
NM = 9 + 8 + NQ
TWO_PI_SAFE = 6.283185
SLOT = 4096
NSLAB = 19


def _ap_range(ap):
    es = mybir.dt.size(ap.dtype)
    pat = ap.ap
    off = int(ap.offset)
    space = type(ap.tensor).__name__
    if space.startswith("DRam"):
        ext = 1
        for st, cnt in pat:
            ext += (cnt - 1) * abs(st)
        return (ap.tensor.name, 0, 1, off * es, (off + ext) * es)
    pstep, pcnt = pat[0]
    if pstep == 0:
        pstep = 1 << 40
    p0 = off // pstep if pstep < (1 << 40) else 0
    col = off - p0 * pstep if pstep < (1 << 40) else off
    ext = 1
    for st, cnt in pat[1:]:
        ext += (cnt - 1) * abs(st)
    return (ap.tensor.name, p0, p0 + pcnt, col * es, (col + ext) * es)


class Tok:
    _n = 0

    def __init__(self, name="tok"):
        Tok._n += 1
        self.key = (f"__tok{Tok._n}_{name}", 0, 1, 0, 1)


class Op:
    __slots__ = ("eng", "fn", "deps", "is_dma", "dsem", "dcum", "signal", "cnt", "idx", "tag", "rw")


class Sched:
    ENG = ("pe", "act", "dve", "pool", "sp")
    EMAP = {"pe": "tensor", "act": "scalar", "dve": "vector", "pool": "gpsimd", "sp": "sync"}

    def __init__(self, nc, es):
        self.nc = nc
        self.es = es
        self.ops = []
        self.recs = {}
        self.sems = {e: es.enter_context(nc.semaphore(f"s_{e}")) for e in self.ENG}
        self.dstreams = {}
        import os
        self.debug_rw = bool(os.environ.get('DEBUG_RW'))

    def sbuf(self, name, shape, dtype):
        return self.es.enter_context(self.nc.sbuf_tensor(name, list(shape), dtype))

    def psum(self, name, shape, dtype=F32):
        return self.es.enter_context(self.nc.psum_tensor(name, list(shape), dtype))

    @staticmethod
    def _is_psum(x):
        return (not isinstance(x, Tok)) and type(x.tensor).__name__.startswith("PSum")

    def _rng(self, x):
        if isinstance(x, Tok):
            return x.key
        if self._is_psum(x):
            return (x.tensor.name, 0, 128, 0, 2048)
        return _ap_range(x)

    @staticmethod
    def _remainders(r, p0, p1, b0, b1):
        rp0, rp1, rb0, rb1 = r[0], r[1], r[2], r[3]
        out = []
        if rp0 < p0:
            out.append((rp0, p0, rb0, rb1))
        if p1 < rp1:
            out.append((p1, rp1, rb0, rb1))
        q0, q1 = max(rp0, p0), min(rp1, p1)
        if rb0 < b0:
            out.append((q0, q1, rb0, b0))
        if b1 < rb1:
            out.append((q0, q1, b1, rb1))
        return out

    def _access(self, x, idx, write, deps):
        name, p0, p1, b0, b1 = self._rng(x)
        lst = self.recs.setdefault(name, [])
        keep = []
        hit = False
        for r in lst:
            if r[0] < p1 and p0 < r[1] and r[2] < b1 and b0 < r[3]:
                hit = True
                if r[4] is not None:
                    deps.add(r[4])
                if write:
                    deps.update(r[5])
                else:
                    keep.append([max(r[0], p0), min(r[1], p1), max(r[2], b0), min(r[3], b1), r[4], r[5] + [idx]])
                for (a0, a1, c0, c1) in self._remainders(r, p0, p1, b0, b1):
                    keep.append([a0, a1, c0, c1, r[4], list(r[5])])
            else:
                keep.append(r)
        if write:
            keep.append([p0, p1, b0, b1, idx, []])
        elif not hit:
            keep.append([p0, p1, b0, b1, None, [idx]])
        else:
            covered = sum((min(r[1], p1) - max(r[0], p0)) * (min(r[3], b1) - max(r[2], b0)) for r in keep
                          if r[0] < p1 and p0 < r[1] and r[2] < b1 and b0 < r[3] and idx in r[5])
            if covered < (p1 - p0) * (b1 - b0):
                keep.append([p0, p1, b0, b1, None, [idx]])
        self.recs[name] = keep

    def op(self, eng, fn, reads=(), writes=(), dstream=None):
        o = Op()
        o.eng = eng
        o.fn = fn
        o.idx = len(self.ops)
        o.tag = getattr(self, 'stage', '')
        deps = set()
        for x in reads:
            self._access(x, o.idx, self._is_psum(x), deps)
        for x in writes:
            self._access(x, o.idx, True, deps)
        deps.discard(o.idx)
        o.deps = deps
        o.rw = ([self._rng(x) for x in reads], [self._rng(x) for x in writes]) if getattr(self, 'debug_rw', False) else None
        o.is_dma = dstream is not None
        o.signal = False
        o.cnt = 0
        if o.is_dma:
            if dstream not in self.dstreams:
                self.dstreams[dstream] = [self.es.enter_context(self.nc.semaphore(f"d_{dstream}")), 0]
            st = self.dstreams[dstream]
            if len(st) > 2 and not getattr(self, "_par", False):
                o.deps.add(st[2])
            st[1] += 16
            o.dsem, o.dcum = st[0], st[1]
            if len(st) > 2:
                st[2] = o.idx
            else:
                st.append(o.idx)
            self._par = False
        self.ops.append(o)
        return o

    def dma(self, queue, out_ap, in_ap, stream, extra_reads=(), extra_writes=(), parallel=False, **kw):
        self._par = parallel
        if stream == "prep":
            self._prr = getattr(self, "_prr", 0) + 1
            stream = f"prep{self._prr % 6}"
        return self.op(queue, lambda e: e.dma_start(out=out_ap, in_=in_ap, **kw),
                       reads=[in_ap, *extra_reads], writes=[out_ap, *extra_writes], dstream=stream)

    def emit(self, final_streams=()):
        nc, ops = self.nc, self.ops
        for o in ops:
            for d in o.deps:
                od = ops[d]
                if od.is_dma:
                    continue
                if od.eng == "pe" and o.eng == "pe" and not o.is_dma:
                    continue
                od.signal = True
        cnts = {e: 0 for e in self.ENG}
        for o in ops:
            if not o.is_dma and o.signal:
                cnts[o.eng] += 1
                o.cnt = cnts[o.eng]
        self.final_counts = cnts
        with nc.Block() as block:
            for ename in self.ENG:
                def body(eng, ename=ename):
                    known = {}
                    for o in ops:
                        if o.eng != ename:
                            continue
                        need = {}
                        for d in o.deps:
                            od = ops[d]
                            if od.is_dma:
                                key, sem, val = ("d", id(od.dsem)), od.dsem, od.dcum
                            else:
                                if od.eng == "pe" and ename == "pe" and not o.is_dma:
                                    continue
                                key, sem, val = ("e", od.eng), self.sems[od.eng], od.cnt
                            if known.get(key, 0) >= val:
                                continue
                            if key not in need or need[key][1] < val:
                                need[key] = (sem, val)
                        for key, (sem, val) in need.items():
                            eng.wait_ge(sem, val)
                            known[key] = val
                        ins = o.fn(eng)
                        if o.is_dma:
                            ins.then_inc(o.dsem, 16)
                        elif o.signal:
                            ins.then_inc(self.sems[ename], 1)
                    if ename == "sp":
                        for s in final_streams:
                            st = self.dstreams[s]
                            eng.wait_ge(st[0], st[1])
                getattr(block, self.EMAP[ename])(body)


def host_consts():
    c = {}
    c["ident"] = np.eye(128, dtype=np.float32)
    t = np.arange(128)
    c["sgumask"] = ((t[None, :] // 64) <= (t[:, None] // 64)).astype(np.float32)
    mult = np.concatenate([np.arange(9), np.arange(7, -1, -1), 8 * (np.arange(NQ) + 1)]).astype(np.float32)
    c["mult"] = np.tile(mult[None, :], (128, 1))
    rc = np.zeros((128, 4, 16), np.float32)
    for gi, w in enumerate((2, 4, 8, 16)):
        rc[:, gi, :] = 1.0 / np.minimum(np.arange(1, 17), w)
    c["rcfix"] = rc.reshape(128, 64)
    sel = np.zeros((128, 2, 64), np.float32)
    for q in range(64):
        for h in range(2):
            sel[2 * q + h, h, q] = 1.0
    c["sel"] = sel.reshape(128, 128)
    return c


CONST_LAYOUT = [("ident", 128), ("sgumask", 128), ("mult", NM), ("rcfix", 64), ("sel", 128)]
NCONST = sum(w for _, w in CONST_LAYOUT)


def pack_consts():
    c = host_consts()
    return np.concatenate([c[k] for k, _ in CONST_LAYOUT], axis=1).astype(np.float32)


ORDER = [0, 19, 1, 2, 3, 4, 6, 20, 21, 22, 13, 5, 7, 8, 14, 9, 10, 15, 11, 12, 16, 17, 18]
NSLAB_ALL = 23
NRING = 5

WEIGHT_NAMES = ["norm_g", "w_in", "s5_lam_re", "s5_lam_im", "s5_log_dt", "s5_b_re", "s5_b_im", "s5_c_re",
                "s5_c_im", "s5_d", "s5_w_glu", "s5_b_glu", "pool_w", "pool_scale", "sgu_ln_g", "sgu_ln_b",
                "sgu_w", "sgu_b", "w_branch", "w_out", "final_norm_g"]
WEIGHT_SHAPES = {
    "norm_g": [DEPTH, D], "w_in": [DEPTH, D, DIN], "s5_lam_re": [DEPTH, 32, 64], "s5_lam_im": [DEPTH, 32, 64],
    "s5_log_dt": [DEPTH, 32], "s5_b_re": [DEPTH, 32, 64, 16], "s5_b_im": [DEPTH, 32, 64, 16],
    "s5_c_re": [DEPTH, 32, 16, 64], "s5_c_im": [DEPTH, 32, 16, 64], "s5_d": [DEPTH, DB],
    "s5_w_glu": [DEPTH, DB, DB], "s5_b_glu": [DEPTH, DB], "pool_w": [DEPTH, 4, 128, 128],
    "pool_scale": [DEPTH, DB], "sgu_ln_g": [DEPTH, DB], "sgu_ln_b": [DEPTH, DB], "sgu_w": [DEPTH, 4, 128, 128],
    "sgu_b": [DEPTH, 4, 128], "w_branch": [DEPTH, 3, DB, D], "w_out": [DEPTH, D, D], "final_norm_g": [D],
}


def _numel(shape):
    n = 1
    for s in shape:
        n *= s
    return n


class Carver:
    def __init__(self, tensor_f32, base=0, hole=None):
        self.t = tensor_f32
        self.off = base
        self.hole = hole

    def take(self, nelem, dtype, parts=128, p0=0):
        nbytes = nelem * mybir.dt.size(dtype)
        words = (nbytes + 3) // 4
        words = (words + 7) // 8 * 8
        if self.hole is not None and self.off < self.hole[1] and self.off + words > self.hole[0]:
            self.off = self.hole[1]
        ap = self.t[p0:p0 + parts, self.off:self.off + words]
        self.off += words
        if dtype != F32:
            ap = ap.bitcast(dtype)
        return ap[:, 0:nelem]


def build_program(mode="full", nblocks=None, layers=(0, 1), do_final=True, x_is_T=False):
    nc = bass.Bass("TRN2", target_bir_lowering=False)
    NTOK = SPC * SEQ
    dram = {}
    for n in WEIGHT_NAMES:
        dram[n] = nc.dram_tensor(n, WEIGHT_SHAPES[n], F32, kind="ExternalInput")
    x_d = nc.dram_tensor("x", [D, NTOK], F32, kind="ExternalInput").ap()
    consts_d = nc.dram_tensor("consts", [128, NCONST], F32, kind="ExternalInput").ap()
    zeros_t = nc.dram_tensor("zeros", [128, SLOT // 2], F32, kind="ExternalInput")
    out_d = nc.dram_tensor("out", [D, NTOK], F32, kind="ExternalOutput").ap()
    wscr = nc.dram_tensor("wscr", [DEPTH, NSLAB_ALL, 128, SLOT], BF16, kind="Internal")
    dbg = {}

    def DAP(name, offset, pat):
        return bass.AP(dram[name], offset, pat)

    def SCR(l, s):
        return bass.AP(wscr, (l * NSLAB_ALL + s) * 128 * SLOT, [[SLOT, 128], [1, SLOT]])

    with ExitStack() as es:
        S = Sched(nc, es)
        cst = S.sbuf("cst", [128, NCONST], F32)
        ident = cst[:, 0:128]
        sgumask = cst[:, 128:256]
        mult = cst[:, 256:256 + NM]
        rcfix = cst[:, 256 + NM:256 + NM + 64]
        selb = S.sbuf("selb", [128, 128], BF16)
        identb = S.sbuf("identb", [128, 128], BF16)
        onesb = S.sbuf("onesb", [128, 128], BF16)
        epsc = S.sbuf("epsc", [128, 4], F32)
        fngc = S.sbuf("fngc", [128, KT], F32)
        fst = S.sbuf("fst", [KT, 128], F32)
        AR_WORDS = (36 if mode == 'prep_test' else 24) * 1024
        AR = S.sbuf("arena", [128, AR_WORDS], F32)
        PT = mode == "prep_test"
        BIG_WORDS = NRING * 2048 + 4096 + 2048 + 3 * 1024
        BIG = S.sbuf("big", [128, 2048 if PT else BIG_WORDS], F32)
        ring = [BIG[:, i * 2048:(i + 1) * 2048].bitcast(BF16) for i in range(1 if PT else NRING)]
        _o = NRING * 2048
        xT = None if PT else BIG[:, _o:_o + 4096]
        hT = None if PT else BIG[:, _o + 4096:_o + 6144].bitcast(BF16)
        yT = None if PT else [BIG[:, _o + 6144 + k * 1024:_o + 6144 + (k + 1) * 1024].bitcast(BF16) for k in range(3)]
        banks = [S.psum(f"bank{i}", [128, 512], F32) for i in range(8)]
        L = []
        for l in range(DEPTH):
            r = {}
            r["Ec"] = S.sbuf(f"Ec{l}", [128, 16 * NQ], F32)
            r["Es"] = S.sbuf(f"Es{l}", [128, 16 * NQ], F32)
            r["r8"] = S.sbuf(f"r8_{l}", [128, 16], F32)
            r["cols"] = S.sbuf(f"cols{l}", [128, 16], F32)
            r["pw"] = S.sbuf(f"pw{l}", [128, 4 * 128], BF16)
            r["wsT"] = S.sbuf(f"wsT{l}", [128, 4 * 128], BF16)
            r["lnG"] = S.sbuf(f"lnG{l}", [128, DB], BF16)
            r["lnB"] = S.sbuf(f"lnB{l}", [128, DB], BF16)
            r["bsz"] = S.sbuf(f"bsz{l}", [128, DB], BF16)
            r["carry"] = S.sbuf(f"carry{l}", [128, 2 * 16], F32)
            r["dt2"] = S.sbuf(f"dt2_{l}", [16, 4], F32)
            r["halo"] = S.sbuf(f"halo{l}", [128, 4 * 16], F32)
            L.append(r)

        XIN_LO = 2048 + 1024 + 1024 + 10 * 512 + 1056 + 1024 + 1024
        bank_ctr = [0]

        def next_bank():
            b = banks[bank_ctr[0] % 8]
            bank_ctr[0] += 1
            return b

        ev_ctr = [0]

        def evac_copy(out_ap, in_ap, eng=None):
            ev_ctr[0] += 1
            if eng == "act" or (eng is None and ev_ctr[0] % 2):
                S.op("act", lambda e: e.activation(out_ap, in_ap, AF.Copy), reads=[in_ap], writes=[out_ap])
            else:
                S.op("dve", lambda e: e.tensor_copy(out_ap, in_ap), reads=[in_ap], writes=[out_ap])

        def TT(eng, out, a, b, op):
            S.op(eng, lambda e: e.tensor_tensor(out, a, b, op), reads=[a, b], writes=[out])

        def TS(eng, out, a, s1, s2, op0, op1=None):
            rd = [a] + [s for s in (s1, s2) if not isinstance(s, (int, float)) and s is not None]
            if op1 is None:
                S.op(eng, lambda e: e.tensor_scalar(out, a, s1, None, op0), reads=rd, writes=[out])
            else:
                S.op(eng, lambda e: e.tensor_scalar(out, a, s1, s2, op0, op1), reads=rd, writes=[out])

        def STT(out, a, s, b, op0, op1):
            rd = [a, b] + ([] if isinstance(s, (int, float)) else [s])
            S.op("dve", lambda e: e.scalar_tensor_tensor(out, a, s, b, op0, op1), reads=rd, writes=[out])

        def ACT(out, in_, func, bias=None, scale=None, accum_out=None):
            kw = {}
            rd = [in_]
            wr = [out]
            if bias is not None:
                kw["bias"] = bias
                if not isinstance(bias, (int, float)):
                    rd.append(bias)
            if scale is not None:
                kw["scale"] = scale
                if not isinstance(scale, (int, float)):
                    rd.append(scale)
            if accum_out is not None:
                kw["accum_out"] = accum_out
                wr.append(accum_out)
            S.op("act", lambda e: e.activation(out, in_, func, **kw), reads=rd, writes=wr)

        def CP(eng, out, in_):
            if eng == "act":
                ACT(out, in_, AF.Copy)
            else:
                S.op(eng, lambda e: e.tensor_copy(out, in_), reads=[in_], writes=[out])

        def MS(eng, ap, val):
            S.op(eng, lambda e: e.memset(ap, val), writes=[ap])

        def MM(out, lhsT, rhs, start, stop):
            S.op("pe", lambda e: e.matmul(out, lhsT, rhs, start=start, stop=stop), reads=[lhsT, rhs], writes=[out])

        def TR(out, in_, idn):
            S.op("pe", lambda e: e.transpose(out, in_, idn), reads=[in_, idn], writes=[out])

        def bc(ap, shape, axis):
            return ap.unsqueeze(axis).to_broadcast(list(shape))

        S.dma("sp", cst[:], consts_d, "prep")
        CP("dve", identb[:], ident)
        CP("dve", selb[:], cst[:, 256 + NM + 64:256 + NM + 64 + 128])
        MS("pool", onesb[:], 1.0)
        MS("pool", epsc[:, 0:1], RMS_EPS)
        MS("pool", epsc[:, 1:2], LN_EPS)
        MS("pool", epsc[:, 2:3], -0.5)
        S.dma("sp", fst[:], bass.AP(dram["final_norm_g"], 0, [[128, KT], [1, 128]]), "prep")
        _bf = next_bank()
        TR(_bf[:, 0:KT], fst[:], ident[0:KT, 0:KT])
        CP("dve", fngc[:], _bf[:, 0:KT])

        def emit_wconv(l):
            for s in ORDER:
                dst = SCR(l, s)
                if s < 13:
                    src = DAP("w_in", l * D * DIN + s * 512, [[DIN, 128], [128 * DIN, KT], [1, 512]])
                    d3 = bass.AP(wscr, (l * NSLAB_ALL + s) * 128 * SLOT, [[SLOT, 128], [512, KT], [1, 512]])
                elif s == 13:
                    src = DAP("s5_w_glu", l * DB * DB, [[DB, 128], [128 * DB, 4], [1, DB]])
                    d3 = bass.AP(wscr, (l * NSLAB_ALL + s) * 128 * SLOT, [[SLOT, 128], [DB, 4], [1, DB]])
                elif s < 17:
                    k = s - 14
                    src = DAP("w_branch", (l * 3 + k) * DB * D, [[D, 128], [128 * D, 4], [1, D]])
                    d3 = bass.AP(wscr, (l * NSLAB_ALL + s) * 128 * SLOT, [[SLOT, 128], [D, 4], [1, D]])
                elif s < 19:
                    hlf = s - 17
                    src = DAP("w_out", l * D * D + hlf * 512, [[D, 128], [128 * D, KT], [1, 512]])
                    d3 = bass.AP(wscr, (l * NSLAB_ALL + s) * 128 * SLOT, [[SLOT, 128], [512, KT], [1, 512]])
                else:
                    continue
                S.dma("pool", d3, src, f"wc{l}_{s}", extra_writes=[dst])

        import os
        STOP = int(os.environ.get("PREP_STOP", "999"))

        class _Stop(Exception):
            pass

        def ck(n):
            if mode == "prep_test" and n == STOP:
                raise _Stop()

        load_bufs = []

        def emit_prep_early(l):
            R = L[l]
            S.dma("sp", R["dt2"][:, 0:2], DAP("s5_log_dt", l * 32, [[2, 16], [1, 2]]), "prep")
            MS("pool", R["dt2"][:, 2:4], float(np.e))
            TT("pool", R["dt2"][:, 0:2], R["dt2"][:, 2:4], R["dt2"][:, 0:2], ALU.pow)
            MS("pool", R["bsz"][:], 0.0)
            S.dma("pool", R["bsz"][0:1, :], DAP("sgu_b", l * DB, [[DB, 1], [1, DB]]), "prep")
            S.dma("pool", R["lnG"][:], DAP("sgu_ln_g", l * DB, [[0, 128], [1, DB]]), "prep")
            S.dma("pool", R["lnB"][:], DAP("sgu_ln_b", l * DB, [[0, 128], [1, DB]]), "prep")
            S.dma("pool", R["pw"][:].rearrange("p (g d) -> p g d", d=128), DAP("pool_w", l * 65536, [[128, 128], [16384, 4], [1, 128]]), "prep")
            MS("pool", R["carry"][:], 0.0)
            MS("pool", R["halo"][:], 0.0)

        def emit_prep(l, scratch):
            R = L[l]
            C = Carver(scratch, hole=(XIN_LO, XIN_LO + 4096) if scratch is AR else None)
            st16 = C.take(3 * 128, F32, parts=16)
            ldt = C.take(2, F32, parts=16)
            P16 = C.take(48, F32)
            S.dma("sp", st16[:, 0:128], DAP("s5_lam_re", l * 2048, [[128, 16], [1, 128]]), "prep")
            S.dma("sp", st16[:, 128:256], DAP("s5_lam_im", l * 2048, [[128, 16], [1, 128]]), "prep")
            Wn = C.take(512, F32)
            S.dma("sp", Wn.rearrange("p (h s) -> p h s", s=128), DAP("sgu_w", l * 65536, [[128, 128], [16384, 4], [1, 128]]), "prep")
            colst = C.take(128, F32, parts=16)
            S.dma("sp", colst[0:8, :], DAP("norm_g", l * D, [[128, 8], [1, 128]]), "prep")
            S.dma("sp", colst[8:12, :], DAP("s5_b_glu", l * DB, [[128, 4], [1, 128]]), "prep")
            S.dma("sp", colst[12:16, :], DAP("pool_scale", l * DB, [[128, 4], [1, 128]]), "prep")
            Bre = C.take(256, F32)
            Bim = C.take(256, F32)
            S.dma("sp", Bre.rearrange("p (g c) -> p g c", c=16), DAP("s5_b_re", l * 32768, [[16, 128], [2048, 16], [1, 16]]), "prep")
            S.dma("sp", Bim.rearrange("p (g c) -> p g c", c=16), DAP("s5_b_im", l * 32768, [[16, 128], [2048, 16], [1, 16]]), "prep")
            dst32 = C.take(16, F32, parts=32)
            S.dma("sp", dst32, DAP("s5_d", l * DB, [[16, 32], [1, 16]]), "prep")
            Cs = C.take(512, F32)
            m3 = C.off
            Cn = C.take(2 * 2048, F32, parts=16)
            S.dma("sp", Cn[:, 0:2048].rearrange("p (g q) -> p g q", q=64), DAP("s5_c_re", l * 32768, [[64, 16], [1024, 32], [1, 64]]), "prep")
            S.dma("sp", Cn[:, 2048:4096].rearrange("p (g q) -> p g q", q=64), DAP("s5_c_im", l * 32768, [[64, 16], [1024, 32], [1, 64]]), "prep")
            load_bufs.extend([st16[:, 0:256], Wn, colst, Bre, Bim, dst32, Cn])
            yield "loads"
            bC = next_bank()
            for ri in range(2):
                for gp in range(16):
                    TR(bC[:, ri * 256 + gp * 16: ri * 256 + gp * 16 + 16], Cn[:, ri * 2048 + gp * 128: ri * 2048 + (gp + 1) * 128], ident[0:16, 0:16])
            CP("dve", Cs, bC[:, :])
            C.off = m3
            CP("dve", ldt, R["dt2"][:, 0:2])
            CP("dve", st16[:, 256:384].rearrange("p (a b) -> p a b", a=2), bc(ldt, [16, 2, 64], 2))
            bA = next_bank()
            for k in range(3):
                TR(bA[:, k * 16:(k + 1) * 16], st16[:, k * 128:(k + 1) * 128], ident[0:16, 0:16])
            CP("dve", P16, bA[:, 0:48])
            ck(13)
            yield
            TT("dve", Wn.rearrange("p (h s) -> p h s", s=128), Wn.rearrange("p (h s) -> p h s", s=128), bc(sgumask, [128, 4, 128], 1), ALU.mult)
            bW = next_bank()
            for h in range(4):
                TR(bW[:, h * 128:(h + 1) * 128], Wn[:, h * 128:(h + 1) * 128], ident)
            CP("dve", R["wsT"][:], bW[:, :])
            if mode == "prep_test":
                dbg.update(wsT=R["wsT"][:])
            ck(14)
            yield
            ck(15)
            yield
            bE = next_bank()
            TR(bE[:, 0:16], colst, ident[0:16, 0:16])
            CP("dve", R["cols"][:], bE[:, 0:16])
            ck(1)
            yield
            zsrc = bass.AP(zeros_t, 0, [[SLOT // 2, 128], [1, SLOT // 2]]).bitcast(BF16)
            for s_ in (20, 21, 22):
                S.dma("sp", SCR(l, s_), zsrc, f"zf{l}")
            lr, li, dt = P16[:, 0:16], P16[:, 16:32], P16[:, 32:48]
            sm = C.take(16 * 12, F32)
            smv = [sm[:, i * 16:(i + 1) * 16] for i in range(12)]
            xx, th, den, rden, nr, t0, t1, kre, kim, t2, t3, t4 = smv
            TT("dve", xx, lr, dt, ALU.mult)
            TT("dve", th, li, dt, ALU.mult)
            TT("dve", t0, lr, lr, ALU.mult)
            TT("dve", t1, li, li, ALU.mult)
            TT("dve", den, t0, t1, ALU.add)
            S.op("dve", lambda e: e.reciprocal(rden, den), reads=[den], writes=[rden])
            ck(2)
            yield
            TN = 16 * NM
            SIN = C.take(TN, F32)
            COS = C.take(TN, F32)
            m1 = C.off
            Tt = C.take(TN, F32)
            Ni = C.take(TN, I32)
            Nf = C.take(TN, F32)
            MAG = C.take(TN, F32)
            v3 = lambda a: a.rearrange("p (g m) -> p g m", m=NM)
            thB = bc(th, [128, 16, NM], 2)
            xxB = bc(xx, [128, 16, NM], 2)
            multB = bc(mult, [128, 16, NM], 1)
            STT(v3(Tt), thB, 1.0 / (2.0 * np.pi), multB, ALU.mult, ALU.mult)
            CP("dve", Ni, Tt)
            CP("dve", Nf, Ni)
            TT("dve", Nf, Tt, Nf, ALU.subtract)
            ACT(SIN, Nf, AF.Sin, scale=TWO_PI_SAFE)
            TS("dve", Tt, Tt, 0.25, None, ALU.add)
            CP("dve", Ni, Tt)
            CP("dve", Nf, Ni)
            TT("dve", Nf, Tt, Nf, ALU.subtract)
            ACT(COS, Nf, AF.Sin, scale=TWO_PI_SAFE)
            TT("dve", v3(Tt), xxB, multB, ALU.mult)
            ACT(MAG, Tt, AF.Exp)
            ck(3)
            yield
            CP("dve", R["Ec"][:].rearrange("p (g q) -> p g q", q=NQ), v3(COS)[:, :, 17:17 + NQ])
            CP("dve", R["Es"][:].rearrange("p (g q) -> p g q", q=NQ), v3(SIN)[:, :, 17:17 + NQ])
            CP("dve", R["r8"][:], v3(MAG)[:, :, 8])
            if mode == "prep_test":
                dbg.update(Ec=R["Ec"][:], Es=R["Es"][:], r8=R["r8"][:])
            Ar, Ai = COS, SIN
            TT("dve", Ar, MAG, COS, ALU.mult)
            TT("dve", Ai, MAG, SIN, ALU.mult)
            C.off = m1
            ck(4)
            yield
            TS("dve", nr, v3(Ar)[:, :, 1], -1.0, None, ALU.add)
            ni = v3(Ai)[:, :, 1]
            TT("dve", t0, nr, lr, ALU.mult)
            TT("dve", t1, ni, li, ALU.mult)
            TT("dve", t0, t0, t1, ALU.add)
            TT("dve", kre, t0, rden, ALU.mult)
            TT("dve", t2, ni, lr, ALU.mult)
            TT("dve", t3, nr, li, ALU.mult)
            TT("dve", t2, t2, t3, ALU.subtract)
            TT("dve", kim, t2, rden, ALU.mult)
            cre = C.take(128, F32)
            cim = C.take(128, F32)
            ct = C.take(128, F32)
            c3 = lambda a: a.rearrange("p (g i) -> p g i", i=8)
            ArW, AiW = v3(Ar)[:, :, 9:17], v3(Ai)[:, :, 9:17]
            kreB, kimB = bc(kre, [128, 16, 8], 2), bc(kim, [128, 16, 8], 2)
            TT("dve", c3(cre), ArW, kreB, ALU.mult)
            TT("dve", c3(ct), AiW, kimB, ALU.mult)
            TT("dve", cre, cre, ct, ALU.subtract)
            TT("dve", c3(cim), ArW, kimB, ALU.mult)
            TT("dve", c3(ct), AiW, kreB, ALU.mult)
            TT("dve", cim, cim, ct, ALU.add)
            ck(5)
            yield
            WWre = C.take(2048, F32)
            WWim = C.take(2048, F32)
            m2 = C.off
            WWt = C.take(2048, F32)
            w4 = lambda a: a.rearrange("p (g i c) -> p g i c", i=8, c=16)
            b3 = lambda a: a.rearrange("p (g c) -> p g c", c=16)
            creB, cimB = bc(c3(cre), [128, 16, 8, 16], 3), bc(c3(cim), [128, 16, 8, 16], 3)
            BreB, BimB = bc(b3(Bre), [128, 16, 8, 16], 2), bc(b3(Bim), [128, 16, 8, 16], 2)
            TT("dve", w4(WWre), creB, BreB, ALU.mult)
            TT("dve", w4(WWt), cimB, BimB, ALU.mult)
            TT("dve", WWre, WWre, WWt, ALU.subtract)
            TT("dve", w4(WWim), creB, BimB, ALU.mult)
            TT("dve", w4(WWt), cimB, BreB, ALU.mult)
            TT("dve", WWim, WWim, WWt, ALU.add)
            ck(6)
            yield
            Wfin = C.take(SLOT, BF16)
            for ri, WW in enumerate((WWre, WWim)):
                for k4 in range(4):
                    bk = next_bank()
                    for j in range(4):
                        gp = k4 * 4 + j
                        TR(bk[:, j * 128:(j + 1) * 128], WW[:, gp * 128:(gp + 1) * 128], ident)
                    dst = Wfin.rearrange("p (g r q) -> p g r q", r=2, q=64)[:, 8 * k4:8 * k4 + 8, ri, :]
                    evac_copy(dst, bk[:, :].rearrange("p (g q) -> p g q", q=64))
            S.dma("act", SCR(l, 19), Wfin, "prep")
            if mode == "prep_test":
                dbg.update(Wfin=Wfin)
            if mode != "prep_test":
                C.off = m2
            ck(7)
            yield
            Cre, Cim = b3(Cs[:, 0:256]), b3(Cs[:, 256:512])
            ck(8)
            yield
            Vcb = [C.take(16 * 9 * 16, BF16), C.take(16 * 9 * 16, BF16)]
            W7 = C.take(2 * 256, BF16)
            m4 = C.off
            VVre = C.take(16 * 9 * 16, F32)
            VVim = C.take(16 * 9 * 16, F32)
            VVt = C.take(16 * 9 * 16, F32)
            vv4 = lambda a: a.rearrange("p (g m c) -> p g m c", m=9, c=16)
            CreB, CimB = bc(Cre, [128, 16, 9, 16], 2), bc(Cim, [128, 16, 9, 16], 2)
            ArB, AiB = bc(v3(Ar)[:, :, 0:9], [128, 16, 9, 16], 3), bc(v3(Ai)[:, :, 0:9], [128, 16, 9, 16], 3)
            TT("dve", vv4(VVre), CreB, ArB, ALU.mult)
            TT("dve", vv4(VVt), CimB, AiB, ALU.mult)
            TT("dve", VVre, VVre, VVt, ALU.subtract)
            TT("dve", vv4(VVim), CreB, AiB, ALU.mult)
            TT("dve", vv4(VVt), CimB, ArB, ALU.mult)
            TT("dve", VVim, VVim, VVt, ALU.add)
            TS("dve", VVim, VVim, -1.0, None, ALU.mult)
            ck(9)
            yield
            Vc9 = []
            for ri, VV in enumerate((VVre, VVim)):
                Vc = Vcb[ri]
                CP("dve", Vc, VV)
                Vc9.append(Vc)
                Vc4 = Vc.rearrange("p (g m c) -> p g m c", m=9, c=16)
                base = (l * NSLAB_ALL + 21 + ri) * 128 * SLOT
                for g2 in range(2):
                    dst = bass.AP(wscr, base + g2 * 64 * SLOT + g2 * 128, [[SLOT, 64], [256, 16], [8 * 16, 1], [1, 128]])
                    S.dma("pool", dst, Vc4[g2 * 64:(g2 + 1) * 64, :, 1:9, :].rearrange("p g m c -> p g (m c)"), f"vp{l}{ri}", parallel=True)
            CP("dve", W7[:, 0:256].rearrange("p (g c) -> p g c", c=16), w4(WWre)[:, :, 7, :])
            CP("dve", W7[:, 256:512].rearrange("p (g c) -> p g c", c=16), w4(WWim)[:, :, 7, :])
            ck(10)
            yield
            if mode != "prep_test":
                C.off = m4
            bD = next_bank()
            TR(bD[0:16, 0:32], dst32, ident[0:32, 0:32])
            Dg = C.take(32, F32, parts=16)
            CP("dve", Dg, bD[0:16, 0:32])
            tmpD = C.take(512, F32, parts=16)
            TT("dve", tmpD.rearrange("p (g c) -> p g c", c=16), bc(ident[0:16, 0:16], [16, 32, 16], 1), bc(Dg, [16, 32, 16], 2), ALU.mult)
            tD4 = tmpD.rearrange("p (gp h c) -> p gp h c", h=2, c=16)
            Krb = C.take(32 * 128, BF16, parts=16)
            Kb4 = Krb.rearrange("p (gp h f) -> p gp h f", h=2, f=128)
            Kb5 = Krb.rearrange("p (gp h m c) -> p gp h m c", h=2, m=8, c=16)
            for g2 in range(2):
                rows = slice(g2 * 64, (g2 + 1) * 64)
                for k4 in range(4):
                    bk = next_bank()
                    for j in range(4):
                        gp = k4 * 4 + j
                        o = bk[0:16, j * 128:(j + 1) * 128]
                        MM(o, W7[rows, gp * 16:(gp + 1) * 16], Vc9[0][rows, gp * 144:gp * 144 + 128], True, False)
                        MM(o, W7[rows, 256 + gp * 16:256 + (gp + 1) * 16], Vc9[1][rows, gp * 144:gp * 144 + 128], False, True)
                    CP("act", Kb4[:, k4 * 4:(k4 + 1) * 4, g2, :], bk[0:16, :].rearrange("p (j f) -> p j f", f=128))
                    TT("dve", Kb5[:, k4 * 4:(k4 + 1) * 4, g2, 0, :], bk[0:16, :].rearrange("p (j f) -> p j f", f=128)[:, :, 0:16],
                       tD4[:, k4 * 4:(k4 + 1) * 4, g2, :], ALU.add)
            ck(11)
            yield
            if mode == "prep_test":
                dbg.update(Krow=Krb)
            K4 = Krb.rearrange("p (g m c) -> p g m c", m=8, c=16)
            base = (l * NSLAB_ALL + 20) * 128 * SLOT
            for i in range(8):
                dst = bass.AP(wscr, base + 16 * i * SLOT + i * 16, [[SLOT, 16], [128, 32], [1, (8 - i) * 16]])
                S.dma("pool", dst, K4[:, :, 0:8 - i, :].rearrange("p g m c -> p g (m c)"), f"tp{l}", parallel=True)
            if mode == "prep_test":
                dbg.update(cols=R["cols"][:])

        if mode == "prep_test":
            emit_prep_early(0)
            try:
                for _ in emit_prep(0, AR):
                    pass
            except _Stop:
                dbg["P16"] = cst[:, 0:16]
            outs = []
            for k, ap in dbg.items():
                shp = list(ap.shape)
                o = nc.dram_tensor("o_" + k, shp, ap.dtype, kind="ExternalOutput").ap()
                S.dma("sp", o, ap, "dbgout")
                outs.append("dbgout")
            S.emit(final_streams=["dbgout"])
            return nc

        nblk = (SPC * BPS) if nblocks is None else nblocks
        plan = [(b, l, s_) for b in range(nblk) for l in layers for s_ in ORDER]
        rs = {"next_load": 0, "next_use": 0, "released": set()}

        def _pump():
            while rs["next_load"] < len(plan):
                k = rs["next_load"]
                if k >= rs["next_use"] + NRING:
                    break
                if k >= NRING and (k - NRING) not in rs["released"]:
                    break
                _, l_, s_ = plan[k]
                S.dma("sp", ring[k % NRING][:], SCR(l_, s_), f"ring{k % NRING}")
                rs["next_load"] += 1

        def acquire(l_, s_):
            k = rs["next_use"]
            assert plan[k][1:] == (l_, s_), (plan[k], l_, s_)
            rs["next_use"] += 1
            _pump()
            assert rs["next_load"] > k, "ring deadlock"
            return k, ring[k % NRING]

        def release(k):
            rs["released"].add(k)
            _pump()

        def AV(off, nelem, dtype, parts=128):
            words = (nelem * mybir.dt.size(dtype) + 3) // 4
            ap = AR[0:parts, off:off + words]
            if dtype != F32:
                ap = ap.bitcast(dtype)
            return ap[:, 0:nelem]

        o = 0
        Atok = AV(o, 4096, BF16); Ysb = Atok; o += 2048
        Xim = AV(o, 2048, BF16); o += 1024
        agT = AV(o, 2048, BF16); o += 1024
        st_ = []
        for _ in range(10):
            st_.append(AV(o, 512, F32)); o += 512
        tA, tB, tC, tD, Gin_re, Gin_im, G_re, G_im, H_re, H_im = st_
        Hs = [AV(o, 16 * 65, BF16), AV(o + 528, 16 * 65, BF16)]; o += 1056
        ygT = AV(o, 2048, BF16); o += 1024
        sig = AV(o, 2048, BF16); o += 1024
        T3o = o
        bvT = AV(o, 4 * 528, F32); o += 2112
        sA = AV(o, 528, F32); o += 528
        sB = AV(o, 528, F32); o += 528
        pT = AV(o, 2048, BF16); o += 1024
        bgT = AV(o, 2048, BF16); o += 1024
        cuT = AV(o, 2048, BF16); o += 1024
        cgT = AV(o, 2048, BF16); o += 1024
        vn = AV(o, 2048, BF16); o += 1024
        lnst = AV(o, 64, F32); o += 64
        st2_ = []
        for _ in range(6):
            st2_.append(AV(o, 512, F32)); o += 512
        assert o <= AR_WORDS, o
        sq = AV(0, 4096, BF16)
        rstd = AV(2048, 512, F32)
        rstd2 = AV(2560, 512, F32)
        gk = AV(0, 4096, BF16)
        mergedF = AV(2048, 4096, F32)
        mergedT = AV(6144, 4096, BF16)
        mtmp = [AV(8192, 512, F32), AV(8704, 512, F32)]
        xtok = AV(4096, 4096, F32)
        fstat = AV(9216, 16, F32)
        junkT = AV(2048 + 1024, 1024, BF16)

        def v(ap, pat, **kw):
            return ap.rearrange(pat, **kw)

        def emit_L1(b, l, xs):
            R = L[l]
            gcol = R["cols"]
            S.stage = f'b{b}l{l}:L1'
            bk = next_bank()
            for kt in range(KT):
                ACT(sq[:, kt * NB:(kt + 1) * NB], xs[:, kt * NB:(kt + 1) * NB], AF.Square)
                MM(bk[:, :], onesb[:], sq[:, kt * NB:(kt + 1) * NB], kt == 0, kt == KT - 1)
            ACT(rstd2, bk[:, :], AF.Sqrt, bias=epsc[:, 0:1], scale=1.0 / D)
            S.op("dve", lambda e: e.reciprocal(rstd, rstd2), reads=[rstd2], writes=[rstd])
            for kt in range(KT):
                STT(hT[:, kt * NB:(kt + 1) * NB], xs[:, kt * NB:(kt + 1) * NB], gcol[:, kt:kt + 1], rstd, ALU.mult, ALU.mult)

        def emit_layer(b, l, after_l1=None, x_src=None, skip_l1=False, pre_wout=None):
            R = L[l]
            PL = "dve" if (b == 0 and l == layers[0]) else "pool"
            first = (b % BPS) == 0
            gcol = R["cols"]
            S.stage = f'b{b}l{l}:L1'
            if not skip_l1:
                emit_L1(b, l, xT if x_src is None else x_src)

            def fm_tiles(slot, ncol_tiles, consume):
                for m in range(ncol_tiles):
                    bk_ = next_bank()
                    for kt in range(KT):
                        MM(bk_[:, :], slot[:, kt * 512 + m * 128: kt * 512 + (m + 1) * 128], hT[:, kt * NB:(kt + 1) * NB], kt == 0, kt == KT - 1)
                    consume(m, bk_)

            S.stage = f'b{b}l{l}:aval'
            k0, sl = acquire(l, 0)
            A2 = v(Atok[:, 0:2048], "p (g i c) -> p g i c", i=4, c=16)
            ab = [next_bank() for _ in range(4)]
            for kt in range(KT):
                for i0 in range(4):
                    lh = v(hT[:, kt * NB:(kt + 1) * NB], "p (m i) -> p m i", i=4)[:, :, i0]
                    MM(ab[i0][:, :], lh, sl[:, kt * 512:(kt + 1) * 512], kt == 0, kt == KT - 1)
            for i0 in range(4):
                evac_copy(A2[:, :, i0, :], v(ab[i0][:, :], "p (g c) -> p g c", c=16))
            release(k0)
            S.stage = f'b{b}l{l}:trin'
            for g8 in range(4):
                bk = next_bank()
                for j in range(8):
                    g = g8 * 8 + j
                    for h in range(2):
                        MM(bk[h * 64:(h + 1) * 64, j * NQ:(j + 1) * NQ], Atok[:, g * 64:(g + 1) * 64], selb[:, h * 64:(h + 1) * 64], True, True)
                evac_copy(Xim[:, g8 * 8 * NQ:(g8 + 1) * 8 * NQ], bk[:, 0:8 * NQ])
            if after_l1 is not None:
                after_l1()
            S.stage = f'b{b}l{l}:S'
            k1, wf = acquire(l, 19)
            Sb = []
            for hf in range(2):
                bre, bim = next_bank(), next_bank()
                for gpl in range(8):
                    gp = hf * 8 + gpl
                    for g2 in range(2):
                        g = 2 * gp + g2
                        for ri, bk in enumerate((bre, bim)):
                            MM(bk[g2 * 64:(g2 + 1) * 64, gpl * NQ:(gpl + 1) * NQ], wf[:, g * 128 + ri * 64: g * 128 + (ri + 1) * 64],
                               Xim[:, g * NQ:(g + 1) * NQ], True, True)
                Sb.append((bre, bim))
            release(k1)
            S.stage = f'b{b}l{l}:rot'
            carry = R["carry"]
            for ri in range(2):
                hs3 = v(Hs[ri], "p (g q) -> p g q", q=65)
                if first:
                    MS(PL, hs3[:, :, 0], 0.0)
                else:
                    CP(PL, hs3[:, :, 0], carry[:, ri * 16:(ri + 1) * 16])
            W_ = 8 * NQ
            pre = [(tA, tB, tC, tD, Gin_re, Gin_im), tuple(st2_)]
            for hf in range(2):
                bre, bim = Sb[hf]
                Ec = R["Ec"][:, hf * 8 * NQ:(hf + 1) * 8 * NQ]
                Es = R["Es"][:, hf * 8 * NQ:(hf + 1) * 8 * NQ]
                a_, b_, c_, d_, gr_, gi_ = pre[hf]
                TT("dve", a_, bre[:, 0:W_], Ec, ALU.mult)
                TT("dve", d_, bre[:, 0:W_], Es, ALU.mult)
                TT("dve", b_, bim[:, 0:W_], Es, ALU.mult)
                TT("dve", c_, bim[:, 0:W_], Ec, ALU.mult)
                TT(PL, gr_, a_, b_, ALU.add)
                TT(PL, gi_, c_, d_, ALU.subtract)
            for hf in range(2):
                Ec = R["Ec"][:, hf * 8 * NQ:(hf + 1) * 8 * NQ]
                Es = R["Es"][:, hf * 8 * NQ:(hf + 1) * 8 * NQ]
                for ri, (Gin, G) in enumerate(((pre[hf][4], G_re), (pre[hf][5], G_im))):
                    for gpl in range(8):
                        gp = hf * 8 + gpl
                        d0 = R["r8"][:, gp:gp + 1].to_broadcast([128, NQ])
                        init = 0.0 if first else carry[:, ri * 16 + gp: ri * 16 + gp + 1]
                        o_ = G[:, gpl * NQ:(gpl + 1) * NQ]
                        i_ = Gin[:, gpl * NQ:(gpl + 1) * NQ]
                        rd = [R["r8"][:, gp:gp + 1], i_] + ([] if first else [init])
                        S.op("dve", lambda e, o_=o_, d0=d0, i_=i_, init=init: e.tensor_tensor_scan(o_, d0, i_, init, ALU.mult, ALU.add),
                             reads=rd, writes=[o_])
                TT("dve", tA, G_re, Ec, ALU.mult)
                TT("dve", tC, G_re, Es, ALU.mult)
                TT("dve", tB, G_im, Es, ALU.mult)
                TT("dve", tD, G_im, Ec, ALU.mult)
                TT(PL, H_re, tA, tB, ALU.subtract)
                TT(PL, H_im, tC, tD, ALU.add)
                for ri, H in enumerate((H_re, H_im)):
                    hs3 = v(Hs[ri], "p (g q) -> p g q", q=65)
                    h3 = v(H, "p (g q) -> p g q", q=NQ)
                    CP("dve", hs3[:, hf * 8:(hf + 1) * 8, 1:NQ + 1], h3)
                    CP(PL, carry[:, ri * 16 + hf * 8: ri * 16 + (hf + 1) * 8], h3[:, :, NQ - 1])
            S.stage = f'b{b}l{l}:win'
            k2, sl = acquire(l, 1)
            fm_tiles(sl, 4, lambda m, bk_: ACT(agT[:, m * NB:(m + 1) * NB], bk_[:, :], AF.Silu))
            release(k2)
            bv3 = v(bvT, "p (g t) -> p g t", t=528)
            if first:
                MS(PL, bv3[:, :, 0:16], 0.0)
            else:
                CP(PL, bv3[:, :, 0:16], v(R["halo"][:], "p (g t) -> p g t", t=16))
            k7, sl = acquire(l, 2)
            fm_tiles(sl, 4, lambda m, bk_: evac_copy(bv3[:, m, 16:528], bk_[:, :], "act"))
            release(k7)
            CP(PL, v(R["halo"][:], "p (g t) -> p g t", t=16), bv3[:, :, 512:528])
            k8, sl = acquire(l, 3)
            fm_tiles(sl, 4, lambda m, bk_: ACT(bgT[:, m * NB:(m + 1) * NB], bk_[:, :], AF.Silu))
            release(k8)
            k9, sl = acquire(l, 4)
            fm_tiles(sl, 4, lambda m, bk_: evac_copy(cuT[:, m * NB:(m + 1) * NB], bk_[:, :], "act"))
            release(k9)
            k11, sl = acquire(l, 6)
            fm_tiles(sl, 4, lambda m, bk_: ACT(cgT[:, m * NB:(m + 1) * NB], bk_[:, :], AF.Silu))
            release(k11)
            S.stage = f'b{b}l{l}:poolel'
            for m in range(4):
                u = bv3[:, m, :]
                w = 2 ** (m + 1)
                cur, nxt = sA, sB
                TT(PL, cur[:, 1:528], u[:, 1:528], u[:, 0:527], ALU.add)
                sh = 2
                while sh < w:
                    lo = 2 * sh - 1
                    TT(PL, nxt[:, lo:528], cur[:, lo:528], cur[:, lo - sh:528 - sh], ALU.add)
                    cur, nxt = nxt, cur
                    sh *= 2
                STT(pT[:, m * NB:(m + 1) * NB], cur[:, 16:528], 1.0 / w, u[:, 16:528], ALU.mult, ALU.subtract)
                if first:
                    TT(PL, nxt[:, 0:16], cur[:, 16:32], rcfix[:, m * 16:(m + 1) * 16], ALU.mult)
                    TT(PL, pT[:, m * NB:m * NB + 16], nxt[:, 0:16], u[:, 16:32], ALU.subtract)
            TT("dve", cuT, cuT, cgT, ALU.mult)
            S.stage = f'b{b}l{l}:Y'
            k3, tp = acquire(l, 20)
            k4_, vre = acquire(l, 21)
            k5, vim = acquire(l, 22)
            Y5 = v(Ysb[0:NQ, :], "q (f j g c) -> q f j g c", f=4, j=8, g=8)
            for ft in range(4):
                for hb in range(2):
                    bk = next_bank()
                    for pj in range(2):
                        gp = ft * 4 + hb * 2 + pj
                        first_mm = pj == 0
                        cols = slice(pj * 256, (pj + 1) * 256)
                        MM(bk[0:NQ, cols], v(Hs[0], "p (g q) -> p g q", q=65)[:, gp, 0:NQ], vre[:, gp * 256:(gp + 1) * 256], first_mm, False)
                        MM(bk[0:NQ, cols], v(Hs[1], "p (g q) -> p g q", q=65)[:, gp, 0:NQ], vim[:, gp * 256:(gp + 1) * 256], False, False)
                        for g2 in range(2):
                            g = 2 * gp + g2
                            c0 = pj * 256 + g2 * 128
                            MM(bk[0:NQ, c0:c0 + 128], Xim[:, g * NQ:(g + 1) * NQ], tp[:, g * 128:(g + 1) * 128], False, pj == 1 and g2 == 1)
                    evac_copy(Y5[:, ft, :, hb * 4:(hb + 1) * 4, :].rearrange("q j g c -> q g j c"), v(bk[0:NQ, :], "q (g j c) -> q g j c", j=8, c=16))
            release(k3); release(k4_); release(k5)
            S.stage = f'b{b}l{l}:trout'
            for ft in range(4):
                bk = next_bank()
                bkb = bk[:, :].bitcast(BF16)
                for j in range(8):
                    TR(bkb[:, j * NQ:(j + 1) * NQ], Ysb[0:NQ, (ft * 8 + j) * 128:(ft * 8 + j + 1) * 128], identb[0:NQ, 0:NQ])
                ACT(v(ygT[:, ft * NB:(ft + 1) * NB], "p (q j) -> p j q", j=8), v(bkb[:, 0:8 * NQ], "p (j q) -> p j q", j=8), AF.Gelu_apprx_tanh)
            S.stage = f'b{b}l{l}:glu'
            k6, sl = acquire(l, 13)
            gb = [next_bank() for _ in range(4)]
            for kt in range(4):
                for m in range(4):
                    MM(gb[m][:, :], sl[:, kt * 512 + m * 128: kt * 512 + (m + 1) * 128], ygT[:, kt * NB:(kt + 1) * NB], kt == 0, kt == 3)
            for m in range(4):
                ACT(sig[:, m * NB:(m + 1) * NB], gb[m][:, :], AF.Sigmoid, bias=gcol[:, 8 + m:9 + m])
            release(k6)
            TT("dve", yT[0][:], ygT, sig, ALU.mult)
            TT("dve", yT[0][:], yT[0][:], agT, ALU.mult)
            S.stage = f'b{b}l{l}:cv'
            k10, sl = acquire(l, 5)
            for tt in range(NTT):
                bk = next_bank()
                for kt in range(KT):
                    MM(bk[:, :], hT[:, kt * NB + tt * 128: kt * NB + (tt + 1) * 128], sl[:, kt * 512:(kt + 1) * 512], kt == 0, kt == KT - 1)
                st6 = lnst[:, tt * 6:(tt + 1) * 6]
                mv = lnst[:, 24 + tt * 2: 24 + (tt + 1) * 2]
                rsd = lnst[:, 32 + tt: 33 + tt]
                S.op("dve", lambda e, st6=st6, bk=bk: e.bn_stats(st6, bk[:, :]), reads=[bk[:, :]], writes=[st6])
                S.op("dve", lambda e, st6=st6, mv=mv: e.bn_aggr(mv, st6), reads=[st6], writes=[mv])
                TS("pool", rsd, mv[:, 1:2], LN_EPS, None, ALU.add)
                TT("pool", rsd, rsd, epsc[:, 2:3], ALU.pow)
                TS("dve", vn[:, tt * DB:(tt + 1) * DB], bk[:, :], mv[:, 0:1], rsd, ALU.subtract, ALU.mult)
                TT("dve", vn[:, tt * DB:(tt + 1) * DB], vn[:, tt * DB:(tt + 1) * DB], R["lnG"][:], ALU.mult)
                TT("dve", vn[:, tt * DB:(tt + 1) * DB], vn[:, tt * DB:(tt + 1) * DB], R["lnB"][:], ALU.add)
            release(k10)
            def merge_branch(k):
                for hg in range(2):
                    kg, sl_ = acquire(l, 7 + 2 * k + hg)
                    fm_tiles(sl_, 4, lambda m, bk_, hg=hg: ACT(gk[:, (hg * 4 + m) * NB:(hg * 4 + m + 1) * NB], bk_[:, :], AF.Sigmoid))
                    release(kg)
                kb, sl_ = acquire(l, 14 + k)
                for d8 in range(8):
                    bk_ = next_bank()
                    for kt in range(4):
                        MM(bk_[:, :], sl_[:, kt * D + d8 * 128: kt * D + (d8 + 1) * 128], yT[k][:, kt * NB:(kt + 1) * NB], kt == 0, kt == 3)
                    gsl = gk[:, d8 * NB:(d8 + 1) * NB]
                    mf = mergedF[:, d8 * NB:(d8 + 1) * NB]
                    if k == 0:
                        TT("dve", mf, bk_[:, :], gsl, ALU.mult)
                    else:
                        tmp = mtmp[d8 % 2]
                        TT("dve", tmp, bk_[:, :], gsl, ALU.mult)
                        if k == 1:
                            TT(PL, mf, mf, tmp, ALU.add)
                        else:
                            TT("dve" if d8 % 2 else PL, mergedT[:, d8 * NB:(d8 + 1) * NB], mf, tmp, ALU.add)
                release(kb)

            S.stage = f'b{b}l{l}:mergeA'
            merge_branch(0)
            S.stage = f'b{b}l{l}:sgu'
            for h in range(4):
                bk = next_bank()
                for tt in range(NTT):
                    o_ = bk[:, tt * 128:(tt + 1) * 128]
                    MM(o_, vn[:, tt * DB + h * 128: tt * DB + (h + 1) * 128], R["wsT"][:, h * 128:(h + 1) * 128], tt == 0, False)
                    MM(o_, onesb[:], R["bsz"][:, h * 128:(h + 1) * 128], False, tt == NTT - 1)
                TT("dve", yT[2][:, h * NB:(h + 1) * NB], bk[:, :], cuT[:, h * NB:(h + 1) * NB], ALU.mult)
            S.stage = f'b{b}l{l}:poolmm'
            for m in range(4):
                bk = next_bank()
                MM(bk[:, :], R["pw"][:, m * 128:(m + 1) * 128], pT[:, m * NB:(m + 1) * NB], True, True)
                STT(yT[1][:, m * NB:(m + 1) * NB], bk[:, :], gcol[:, 12 + m:13 + m], bgT[:, m * NB:(m + 1) * NB], ALU.mult, ALU.mult)

            if l == layers[-1] and b + 1 < nblk:
                emit_x_load(b + 1)
            S.stage = f'b{b}l{l}:mergeB'
            merge_branch(1)
            S.stage = f'b{b}l{l}:mergeC'
            merge_branch(2)
            S.stage = f'b{b}l{l}:wout'
            if pre_wout is not None:
                pre_wout()
            ko0, sl0 = acquire(l, 17)
            ko1, sl1 = acquire(l, 18)
            ob = [next_bank() for _ in range(8)]
            for kt in range(KT):
                for d8 in range(8):
                    sl_ = sl0 if d8 < 4 else sl1
                    m = d8 % 4
                    MM(ob[d8][:, :], sl_[:, kt * 512 + m * 128: kt * 512 + (m + 1) * 128], mergedT[:, kt * NB:(kt + 1) * NB], kt == 0, kt == KT - 1)
            for d8 in range(8):
                TT("dve", xT[:, d8 * NB:(d8 + 1) * NB], xT[:, d8 * NB:(d8 + 1) * NB], ob[d8][:, :], ALU.add)
            release(ko0); release(ko1)

        assert T3o == XIN_LO, (T3o, XIN_LO)
        xin = AV(T3o, 4096, F32)

        def emit_x_load(b):
            S.dma("sp", v(xin, "p (k t) -> p k t", t=NB), bass.AP(x_d.tensor, b * NB, [[NTOK, 128], [128 * NTOK, KT], [1, NB]]), "xin")

        fsq = AV(10272, 4096, BF16)
        frs2 = AV(3072, 512, F32)
        frs = AV(3584, 512, F32)

        def emit_x_to_xT():
            S.dma("sp", xT[:, :], xin, "x2x")

        def emit_final_norm(b):
            S.stage = f'b{b}:fnorm'
            ost = xtok
            if do_final:
                bk = next_bank()
                for kt in range(KT):
                    ACT(fsq[:, kt * NB:(kt + 1) * NB], xT[:, kt * NB:(kt + 1) * NB], AF.Square)
                    MM(bk[:, :], onesb[:], fsq[:, kt * NB:(kt + 1) * NB], kt == 0, kt == KT - 1)
                ACT(frs2, bk[:, :], AF.Sqrt, bias=epsc[:, 0:1], scale=1.0 / D)
                S.op("dve", lambda e: e.reciprocal(frs, frs2), reads=[frs2], writes=[frs])
                for kt in range(KT):
                    STT(ost[:, kt * NB:(kt + 1) * NB], xT[:, kt * NB:(kt + 1) * NB], fngc[:, kt:kt + 1], frs, ALU.mult, ALU.mult)
            else:
                for kt in range(KT):
                    CP("dve", ost[:, kt * NB:(kt + 1) * NB], xT[:, kt * NB:(kt + 1) * NB])
            S.dma("sp", bass.AP(out_d.tensor, b * NB, [[NTOK, 128], [128 * NTOK, KT], [1, NB]]), v(ost, "p (k t) -> p k t", t=NB), "out0")

        for l in layers:
            emit_prep_early(l)
        gens = [emit_prep(l, AR if i == 0 else BIG) for i, l in enumerate(layers)]
        for g_ in gens:
            next(g_)
        emit_x_load(0)
        gate = S.sbuf("gate", [128, 2], F32)
        S.op("pool", lambda e: e.memset(gate[:], 0.0), reads=list(load_bufs), writes=[gate[:]])
        emit_wconv(layers[0])
        while gens:
            for g_ in list(gens):
                try:
                    next(g_)
                except StopIteration:
                    gens.remove(g_)
        for l in layers[1:]:
            emit_wconv(l)
        emit_L1(0, layers[0], xin)
        emit_x_to_xT()
        for b in range(nblk):
            for li, l in enumerate(layers):
                last = li == len(layers) - 1
                pre = None
                if last and b + 1 < nblk:
                    pre = (lambda b=b: emit_L1(b + 1, layers[0], xin))
                emit_layer(b, l, None, skip_l1=(li == 0), pre_wout=pre)
            if b + 1 < nblk:
                emit_final_norm(b)
                emit_x_to_xT()
        emit_final_norm(nblk - 1)
        S.emit(final_streams=["out0"])
        build_program.last_sched = S
    return nc


def kernel(**inputs):
    x = np.asarray(inputs["x"], dtype=np.float32)
    nc = build_program("full")
    consts = pack_consts()
    weights = {n: np.ascontiguousarray(np.asarray(inputs[n], dtype=np.float32)) for n in WEIGHT_NAMES}
    in_maps = []
    for c in range(NCORES):
        m = dict(weights)
        m["x"] = np.ascontiguousarray(x[c * SPC:(c + 1) * SPC].reshape(SPC * SEQ, D).T)
        m["consts"] = consts
        m["zeros"] = np.zeros((128, SLOT // 2), np.float32)
        in_maps.append(m)
    res = run_bass_kernel_spmd(nc, in_maps, core_ids=list(range(NCORES)))
    out = np.concatenate([np.ascontiguousarray(np.asarray(r["out"]).T).reshape(SPC, SEQ, D) for r in res.results], axis=0)
    return out.astype(np.float32)
```

```python
import numpy as np
from contextlib import ExitStack
import concourse.bass as bass
import concourse.mybir as mybir
from concourse.bass_utils import run_bass_kernel_spmd

F32 = mybir.dt.float32
BF16 = mybir.dt.bfloat16
I32 = mybir.dt.int32
AF = mybir.ActivationFunctionType
ALU = mybir.AluOpType
AX = mybir.AxisListType

D = 1024
SEQ = 2048
BATCH = 16
DEPTH = 2
NCORES = 8
SPC = BATCH // NCORES
DB = 512
DIN = 6656
KT = D // 128
NB = 512
NQ = NB // 8
NTT = NB // 128
BPS = SEQ // NB
RMS_EPS = 1e-6
LN_EPS = 1e-5
NM = 9 + 8 + NQ
TWO_PI_SAFE = 6.283185
SLOT = 4096
NSLAB = 19


def _ap_range(ap):
    es = mybir.dt.size(ap.dtype)
    pat = ap.ap
    off = int(ap.offset)
    space = type(ap.tensor).__name__
    if space.startswith("DRam"):
        ext = 1
        for st, cnt in pat:
            ext += (cnt - 1) * abs(st)
        return (ap.tensor.name, 0, 1, off * es, (off + ext) * es)
    pstep, pcnt = pat[0]
    if pstep == 0:
        pstep = 1 << 40
    p0 = off // pstep if pstep < (1 << 40) else 0
    col = off - p0 * pstep if pstep < (1 << 40) else off
    ext = 1
    for st, cnt in pat[1:]:
        ext += (cnt - 1) * abs(st)
    return (ap.tensor.name, p0, p0 + pcnt, col * es, (col + ext) * es)


class Tok:
    _n = 0

    def __init__(self, name="tok"):
        Tok._n += 1
        self.key = (f"__tok{Tok._n}_{name}", 0, 1, 0, 1)


class Op:
    __slots__ = ("eng", "fn", "deps", "is_dma", "dsem", "dcum", "signal", "cnt", "idx", "tag", "rw")


class Sched:
    ENG = ("pe", "act", "dve", "pool", "sp")
    EMAP = {"pe": "tensor", "act": "scalar", "dve": "vector", "pool": "gpsimd", "sp": "sync"}

    def __init__(self, nc, es):
        self.nc = nc
        self.es = es
        self.ops = []
        self.recs = {}
        self.sems = {e: es.enter_context(nc.semaphore(f"s_{e}")) for e in self.ENG}
        self.dstreams = {}
        import os
        self.debug_rw = bool(os.environ.get('DEBUG_RW'))

    def sbuf(self, name, shape, dtype):
        return self.es.enter_context(self.nc.sbuf_tensor(name, list(shape), dtype))

    def psum(self, name, shape, dtype=F32):
        return self.es.enter_context(self.nc.psum_tensor(name, list(shape), dtype))

    @staticmethod
    def _is_psum(x):
        return (not isinstance(x, Tok)) and type(x.tensor).__name__.startswith("PSum")

    def _rng(self, x):
        if isinstance(x, Tok):
            return x.key
        if self._is_psum(x):
            return (x.tensor.name, 0, 128, 0, 2048)
        return _ap_range(x)

    @staticmethod
    def _remainders(r, p0, p1, b0, b1):
        rp0, rp1, rb0, rb1 = r[0], r[1], r[2], r[3]
        out = []
        if rp0 < p0:
            out.append((rp0, p0, rb0, rb1))
        if p1 < rp1:
            out.append((p1, rp1, rb0, rb1))
        q0, q1 = max(rp0, p0), min(rp1, p1)
        if rb0 < b0:
            out.append((q0, q1, rb0, b0))
        if b1 < rb1:
            out.append((q0, q1, b1, rb1))
        return out

    def _access(self, x, idx, write, deps):
        name, p0, p1, b0, b1 = self._rng(x)
        lst = self.recs.setdefault(name, [])
        keep = []
        hit = False
        for r in lst:
            if r[0] < p1 and p0 < r[1] and r[2] < b1 and b0 < r[3]:
                hit = True
                if r[4] is not None:
                    deps.add(r[4])
                if write:
                    deps.update(r[5])
                else:
                    keep.append([max(r[0], p0), min(r[1], p1), max(r[2], b0), min(r[3], b1), r[4], r[5] + [idx]])
                for (a0, a1, c0, c1) in self._remainders(r, p0, p1, b0, b1):
                    keep.append([a0, a1, c0, c1, r[4], list(r[5])])
            else:
                keep.append(r)
        if write:
            keep.append([p0, p1, b0, b1, idx, []])
        elif not hit:
            keep.append([p0, p1, b0, b1, None, [idx]])
        else:
            covered = sum((min(r[1], p1) - max(r[0], p0)) * (min(r[3], b1) - max(r[2], b0)) for r in keep
                          if r[0] < p1 and p0 < r[1] and r[2] < b1 and b0 < r[3] and idx in r[5])
            if covered < (p1 - p0) * (b1 - b0):
                keep.append([p0, p1, b0, b1, None, [idx]])
        self.recs[name] = keep

    def op(self, eng, fn, reads=(), writes=(), dstream=None):
        o = Op()
        o.eng = eng
        o.fn = fn
        o.idx = len(self.ops)
        o.tag = getattr(self, 'stage', '')
        deps = set()
        for x in reads:
            self._access(x, o.idx, self._is_psum(x), deps)
        for x in writes:
            self._access(x, o.idx, True, deps)
        deps.discard(o.idx)
        o.deps = deps
        o.rw = ([self._rng(x) for x in reads], [self._rng(x) for x in writes]) if getattr(self, 'debug_rw', False) else None
        o.is_dma = dstream is not None
        o.signal = False
        o.cnt = 0
        if o.is_dma:
            if dstream not in self.dstreams:
                self.dstreams[dstream] = [self.es.enter_context(self.nc.semaphore(f"d_{dstream}")), 0]
            st = self.dstreams[dstream]
            if len(st) > 2 and not getattr(self, "_par", False):
                o.deps.add(st[2])
            st[1] += 16
            o.dsem, o.dcum = st[0], st[1]
            if len(st) > 2:
                st[2] = o.idx
            else:
                st.append(o.idx)
            self._par = False
        self.ops.append(o)
        return o

    def dma(self, queue, out_ap, in_ap, stream, extra_reads=(), extra_writes=(), parallel=False, **kw):
        self._par = parallel
        if stream == "prep":
            self._prr = getattr(self, "_prr", 0) + 1
            stream = f"prep{self._prr % 6}"
        return self.op(queue, lambda e: e.dma_start(out=out_ap, in_=in_ap, **kw),
                       reads=[in_ap, *extra_reads], writes=[out_ap, *extra_writes], dstream=stream)

    def emit(self, final_streams=()):
        nc, ops = self.nc, self.ops
        for o in ops:
            for d in o.deps:
                od = ops[d]
                if od.is_dma:
                    continue
                if od.eng == "pe" and o.eng == "pe" and not o.is_dma:
                    continue
                od.signal = True
        cnts = {e: 0 for e in self.ENG}
        for o in ops:
            if not o.is_dma and o.signal:
                cnts[o.eng] += 1
                o.cnt = cnts[o.eng]
        self.final_counts = cnts
        with nc.Block() as block:
            for ename in self.ENG:
                def body(eng, ename=ename):
                    known = {}
                    for o in ops:
                        if o.eng != ename:
                            continue
                        need = {}
                        for d in o.deps:
                            od = ops[d]
                            if od.is_dma:
                                key, sem, val = ("d", id(od.dsem)), od.dsem, od.dcum
                            else:
                                if od.eng == "pe" and ename == "pe" and not o.is_dma:
                                    continue
                                key, sem, val = ("e", od.eng), self.sems[od.eng], od.cnt
                            if known.get(key, 0) >= val:
                                continue
                            if key not in need or need[key][1] < val:
                                need[key] = (sem, val)
                        for key, (sem, val) in need.items():
                            eng.wait_ge(sem, val)
                            known[key] = val
                        ins = o.fn(eng)
                        if o.is_dma:
                            ins.then_inc(o.dsem, 16)
                        elif o.signal:
                            ins.then_inc(self.sems[ename], 1)
                    if ename == "sp":
                        for s in final_streams:
                            st = self.dstreams[s]
                            eng.wait_ge(st[0], st[1])
                getattr(block, self.EMAP[ename])(body)


def host_consts():
    c = {}
    c["ident"] = np.eye(128, dtype=np.float32)
    t = np.arange(128)
    c["sgumask"] = ((t[None, :] // 64) <= (t[:, None] // 64)).astype(np.float32)
    mult = np.concatenate([np.arange(9), np.arange(7, -1, -1), 8 * (np.arange(NQ) + 1)]).astype(np.float32)
    c["mult"] = np.tile(mult[None, :], (128, 1))
    rc = np.zeros((128, 4, 16), np.float32)
    for gi, w in enumerate((2, 4, 8, 16)):
        rc[:, gi, :] = 1.0 / np.minimum(np.arange(1, 17), w)
    c["rcfix"] = rc.reshape(128, 64)
    sel = np.zeros((128, 2, 64), np.float32)
    for q in range(64):
        for h in range(2):
            sel[2 * q + h, h, q] = 1.0
    c["sel"] = sel.reshape(128, 128)
    return c


CONST_LAYOUT = [("ident", 128), ("sgumask", 128), ("mult", NM), ("rcfix", 64), ("sel", 128)]
NCONST = sum(w for _, w in CONST_LAYOUT)


def pack_consts():
    c = host_consts()
    return np.concatenate([c[k] for k, _ in CONST_LAYOUT], axis=1).astype(np.float32)


ORDER = [0, 19, 1, 2, 3, 4, 6, 20, 21, 22, 5, 13, 7, 8, 14, 9, 10, 15, 11, 12, 16, 17, 18]
NSLAB_ALL = 23
NRING = 5

WEIGHT_NAMES = ["norm_g", "w_in", "s5_lam_re", "s5_lam_im", "s5_log_dt", "s5_b_re", "s5_b_im", "s5_c_re",
                "s5_c_im", "s5_d", "s5_w_glu", "s5_b_glu", "pool_w", "pool_scale", "sgu_ln_g", "sgu_ln_b",
                "sgu_w", "sgu_b", "w_branch", "w_out", "final_norm_g"]
WEIGHT_SHAPES = {
    "norm_g": [DEPTH, D], "w_in": [DEPTH, D, DIN], "s5_lam_re": [DEPTH, 32, 64], "s5_lam_im": [DEPTH, 32, 64],
    "s5_log_dt": [DEPTH, 32], "s5_b_re": [DEPTH, 32, 64, 16], "s5_b_im": [DEPTH, 32, 64, 16],
    "s5_c_re": [DEPTH, 32, 16, 64], "s5_c_im": [DEPTH, 32, 16, 64], "s5_d": [DEPTH, DB],
    "s5_w_glu": [DEPTH, DB, DB], "s5_b_glu": [DEPTH, DB], "pool_w": [DEPTH, 4, 128, 128],
    "pool_scale": [DEPTH, DB], "sgu_ln_g": [DEPTH, DB], "sgu_ln_b": [DEPTH, DB], "sgu_w": [DEPTH, 4, 128, 128],
    "sgu_b": [DEPTH, 4, 128], "w_branch": [DEPTH, 3, DB, D], "w_out": [DEPTH, D, D], "final_norm_g": [D],
}


def _numel(shape):
    n = 1
    for s in shape:
        n *= s
    return n


class Carver:
    def __init__(self, tensor_f32, base=0, hole=None):
        self.t = tensor_f32
        self.off = base
        self.hole = hole

    def take(self, nelem, dtype, parts=128, p0=0):
        nbytes = nelem * mybir.dt.size(dtype)
        words = (nbytes + 3) // 4
        words = (words + 7) // 8 * 8
        if self.hole is not None and self.off < self.hole[1] and self.off + words > self.hole[0]:
            self.off = self.hole[1]
        ap = self.t[p0:p0 + parts, self.off:self.off + words]
        self.off += words
        if dtype != F32:
            ap = ap.bitcast(dtype)
        return ap[:, 0:nelem]


def build_program(mode="full", nblocks=None, layers=(0, 1), do_final=True, x_is_T=False):
    nc = bass.Bass("TRN2", target_bir_lowering=False)
    NTOK = SPC * SEQ
    dram = {}
    for n in WEIGHT_NAMES:
        dram[n] = nc.dram_tensor(n, WEIGHT_SHAPES[n], F32, kind="ExternalInput")
    x_d = nc.dram_tensor("x", [D, NTOK], F32, kind="ExternalInput").ap()
    consts_d = nc.dram_tensor("consts", [128, NCONST], F32, kind="ExternalInput").ap()
    zeros_t = nc.dram_tensor("zeros", [128, SLOT // 2], F32, kind="ExternalInput")
    out_d = nc.dram_tensor("out", [D, NTOK], F32, kind="ExternalOutput").ap()
    wscr = nc.dram_tensor("wscr", [DEPTH, NSLAB_ALL, 128, SLOT], BF16, kind="Internal")
    dbg = {}

    def DAP(name, offset, pat):
        return bass.AP(dram[name], offset, pat)

    def SCR(l, s):
        return bass.AP(wscr, (l * NSLAB_ALL + s) * 128 * SLOT, [[SLOT, 128], [1, SLOT]])

    with ExitStack() as es:
        S = Sched(nc, es)
        cst = S.sbuf("cst", [128, NCONST], F32)
        ident = cst[:, 0:128]
        sgumask = cst[:, 128:256]
        mult = cst[:, 256:256 + NM]
        rcfix = cst[:, 256 + NM:256 + NM + 64]
        selb = S.sbuf("selb", [128, 128], BF16)
        identb = S.sbuf("identb", [128, 128], BF16)
        onesb = S.sbuf("onesb", [128, 128], BF16)
        epsc = S.sbuf("epsc", [128, 4], F32)
        fngc = S.sbuf("fngc", [128, KT], F32)
        fst = S.sbuf("fst", [KT, 128], F32)
        AR_WORDS = (36 if mode == 'prep_test' else 24) * 1024
        AR = S.sbuf("arena", [128, AR_WORDS], F32)
        PT = mode == "prep_test"
        BIG_WORDS = NRING * 2048 + 4096 + 2048 + 3 * 1024
        BIG = S.sbuf("big", [128, 2048 if PT else BIG_WORDS], F32)
        ring = [BIG[:, i * 2048:(i + 1) * 2048].bitcast(BF16) for i in range(1 if PT else NRING)]
        _o = NRING * 2048
        xT = None if PT else BIG[:, _o:_o + 4096]
        hT = None if PT else BIG[:, _o + 4096:_o + 6144].bitcast(BF16)
        yT = None if PT else [BIG[:, _o + 6144 + k * 1024:_o + 6144 + (k + 1) * 1024].bitcast(BF16) for k in range(3)]
        banks = [S.psum(f"bank{i}", [128, 512], F32) for i in range(8)]
        L = []
        for l in range(DEPTH):
            r = {}
            r["Ec"] = S.sbuf(f"Ec{l}", [128, 16 * NQ], F32)
            r["Es"] = S.sbuf(f"Es{l}", [128, 16 * NQ], F32)
            r["r8"] = S.sbuf(f"r8_{l}", [128, 16], F32)
            r["cols"] = S.sbuf(f"cols{l}", [128, 16], F32)
            r["pw"] = S.sbuf(f"pw{l}", [128, 4 * 128], BF16)
            r["wsT"] = S.sbuf(f"wsT{l}", [128, 4 * 128], BF16)
            r["lnG"] = S.sbuf(f"lnG{l}", [128, DB], BF16)
            r["lnB"] = S.sbuf(f"lnB{l}", [128, DB], BF16)
            r["bsz"] = S.sbuf(f"bsz{l}", [128, DB], BF16)
            r["carry"] = S.sbuf(f"carry{l}", [128, 2 * 16], F32)
            r["dt2"] = S.sbuf(f"dt2_{l}", [16, 4], F32)
            r["halo"] = S.sbuf(f"halo{l}", [128, 4 * 16], F32)
            L.append(r)

        XIN_LO = 2048 + 1024 + 1024 + 10 * 512 + 1056 + 1024 + 1024
        bank_ctr = [0]

        def next_bank():
            b = banks[bank_ctr[0] % 8]
            bank_ctr[0] += 1
            return b

        ev_ctr = [0]

        def evac_copy(out_ap, in_ap, eng=None):
            ev_ctr[0] += 1
            if eng == "act" or (eng is None and ev_ctr[0] % 2):
                S.op("act", lambda e: e.activation(out_ap, in_ap, AF.Copy), reads=[in_ap], writes=[out_ap])
            else:
                S.op("dve", lambda e: e.tensor_copy(out_ap, in_ap), reads=[in_ap], writes=[out_ap])

        def TT(eng, out, a, b, op):
            S.op(eng, lambda e: e.tensor_tensor(out, a, b, op), reads=[a, b], writes=[out])

        def TS(eng, out, a, s1, s2, op0, op1=None):
            rd = [a] + [s for s in (s1, s2) if not isinstance(s, (int, float)) and s is not None]
            if op1 is None:
                S.op(eng, lambda e: e.tensor_scalar(out, a, s1, None, op0), reads=rd, writes=[out])
            else:
                S.op(eng, lambda e: e.tensor_scalar(out, a, s1, s2, op0, op1), reads=rd, writes=[out])

        def STT(out, a, s, b, op0, op1):
            rd = [a, b] + ([] if isinstance(s, (int, float)) else [s])
            S.op("dve", lambda e: e.scalar_tensor_tensor(out, a, s, b, op0, op1), reads=rd, writes=[out])

        def ACT(out, in_, func, bias=None, scale=None, accum_out=None):
            kw = {}
            rd = [in_]
            wr = [out]
            if bias is not None:
                kw["bias"] = bias
                if not isinstance(bias, (int, float)):
                    rd.append(bias)
            if scale is not None:
                kw["scale"] = scale
                if not isinstance(scale, (int, float)):
                    rd.append(scale)
            if accum_out is not None:
                kw["accum_out"] = accum_out
                wr.append(accum_out)
            S.op("act", lambda e: e.activation(out, in_, func, **kw), reads=rd, writes=wr)

        def CP(eng, out, in_):
            if eng == "act":
                ACT(out, in_, AF.Copy)
            else:
                S.op(eng, lambda e: e.tensor_copy(out, in_), reads=[in_], writes=[out])

        def MS(eng, ap, val):
            S.op(eng, lambda e: e.memset(ap, val), writes=[ap])

        def MM(out, lhsT, rhs, start, stop):
            S.op("pe", lambda e: e.matmul(out, lhsT, rhs, start=start, stop=stop), reads=[lhsT, rhs], writes=[out])

        def TR(out, in_, idn):
            S.op("pe", lambda e: e.transpose(out, in_, idn), reads=[in_, idn], writes=[out])

        def bc(ap, shape, axis):
            return ap.unsqueeze(axis).to_broadcast(list(shape))

        S.dma("sp", cst[:], consts_d, "prep")
        CP("dve", identb[:], ident)
        CP("dve", selb[:], cst[:, 256 + NM + 64:256 + NM + 64 + 128])
        MS("pool", onesb[:], 1.0)
        MS("pool", epsc[:, 0:1], RMS_EPS)
        MS("pool", epsc[:, 1:2], LN_EPS)
        MS("pool", epsc[:, 2:3], -0.5)
        S.dma("sp", fst[:], bass.AP(dram["final_norm_g"], 0, [[128, KT], [1, 128]]), "prep")
        _bf = next_bank()
        TR(_bf[:, 0:KT], fst[:], ident[0:KT, 0:KT])
        CP("dve", fngc[:], _bf[:, 0:KT])

        def emit_wconv(l):
            for s in ORDER:
                dst = SCR(l, s)
                if s < 13:
                    src = DAP("w_in", l * D * DIN + s * 512, [[DIN, 128], [128 * DIN, KT], [1, 512]])
                    d3 = bass.AP(wscr, (l * NSLAB_ALL + s) * 128 * SLOT, [[SLOT, 128], [512, KT], [1, 512]])
                elif s == 13:
                    src = DAP("s5_w_glu", l * DB * DB, [[DB, 128], [128 * DB, 4], [1, DB]])
                    d3 = bass.AP(wscr, (l * NSLAB_ALL + s) * 128 * SLOT, [[SLOT, 128], [DB, 4], [1, DB]])
                elif s < 17:
                    k = s - 14
                    src = DAP("w_branch", (l * 3 + k) * DB * D, [[D, 128], [128 * D, 4], [1, D]])
                    d3 = bass.AP(wscr, (l * NSLAB_ALL + s) * 128 * SLOT, [[SLOT, 128], [D, 4], [1, D]])
                elif s < 19:
                    hlf = s - 17
                    src = DAP("w_out", l * D * D + hlf * 512, [[D, 128], [128 * D, KT], [1, 512]])
                    d3 = bass.AP(wscr, (l * NSLAB_ALL + s) * 128 * SLOT, [[SLOT, 128], [512, KT], [1, 512]])
                else:
                    continue
                S.dma("pool", d3, src, f"wc{l}_{s}", extra_writes=[dst])

        import os
        STOP = int(os.environ.get("PREP_STOP", "999"))

        class _Stop(Exception):
            pass

        def ck(n):
            if mode == "prep_test" and n == STOP:
                raise _Stop()

        load_bufs = []

        def emit_prep_early(l):
            R = L[l]
            S.dma("sp", R["dt2"][:, 0:2], DAP("s5_log_dt", l * 32, [[2, 16], [1, 2]]), "prep")
            MS("pool", R["dt2"][:, 2:4], float(np.e))
            TT("pool", R["dt2"][:, 0:2], R["dt2"][:, 2:4], R["dt2"][:, 0:2], ALU.pow)
            MS("pool", R["bsz"][:], 0.0)
            S.dma("pool", R["bsz"][0:1, :], DAP("sgu_b", l * DB, [[DB, 1], [1, DB]]), "prep")
            S.dma("pool", R["lnG"][:], DAP("sgu_ln_g", l * DB, [[0, 128], [1, DB]]), "prep")
            S.dma("pool", R["lnB"][:], DAP("sgu_ln_b", l * DB, [[0, 128], [1, DB]]), "prep")
            S.dma("pool", R["pw"][:].rearrange("p (g d) -> p g d", d=128), DAP("pool_w", l * 65536, [[128, 128], [16384, 4], [1, 128]]), "prep")
            MS("pool", R["carry"][:], 0.0)
            MS("pool", R["halo"][:], 0.0)

        def emit_prep(l, scratch):
            R = L[l]
            C = Carver(scratch, hole=(XIN_LO, XIN_LO + 4096) if scratch is AR else None)
            st16 = C.take(3 * 128, F32, parts=16)
            ldt = C.take(2, F32, parts=16)
            P16 = C.take(48, F32)
            S.dma("sp", st16[:, 0:128], DAP("s5_lam_re", l * 2048, [[128, 16], [1, 128]]), "prep")
            S.dma("sp", st16[:, 128:256], DAP("s5_lam_im", l * 2048, [[128, 16], [1, 128]]), "prep")
            Wn = C.take(512, F32)
            S.dma("sp", Wn.rearrange("p (h s) -> p h s", s=128), DAP("sgu_w", l * 65536, [[128, 128], [16384, 4], [1, 128]]), "prep")
            colst = C.take(128, F32, parts=16)
            S.dma("sp", colst[0:8, :], DAP("norm_g", l * D, [[128, 8], [1, 128]]), "prep")
            S.dma("sp", colst[8:12, :], DAP("s5_b_glu", l * DB, [[128, 4], [1, 128]]), "prep")
            S.dma("sp", colst[12:16, :], DAP("pool_scale", l * DB, [[128, 4], [1, 128]]), "prep")
            Bre = C.take(256, F32)
            Bim = C.take(256, F32)
            S.dma("sp", Bre.rearrange("p (g c) -> p g c", c=16), DAP("s5_b_re", l * 32768, [[16, 128], [2048, 16], [1, 16]]), "prep")
            S.dma("sp", Bim.rearrange("p (g c) -> p g c", c=16), DAP("s5_b_im", l * 32768, [[16, 128], [2048, 16], [1, 16]]), "prep")
            dst32 = C.take(16, F32, parts=32)
            S.dma("sp", dst32, DAP("s5_d", l * DB, [[16, 32], [1, 16]]), "prep")
            Cs = C.take(512, F32)
            m3 = C.off
            Cn = C.take(2 * 2048, F32, parts=16)
            S.dma("sp", Cn[:, 0:2048].rearrange("p (g q) -> p g q", q=64), DAP("s5_c_re", l * 32768, [[64, 16], [1024, 32], [1, 64]]), "prep")
            S.dma("sp", Cn[:, 2048:4096].rearrange("p (g q) -> p g q", q=64), DAP("s5_c_im", l * 32768, [[64, 16], [1024, 32], [1, 64]]), "prep")
            load_bufs.extend([st16[:, 0:256], Wn, colst, Bre, Bim, dst32, Cn])
            yield "loads"
            bC = next_bank()
            for ri in range(2):
                for gp in range(16):
                    TR(bC[:, ri * 256 + gp * 16: ri * 256 + gp * 16 + 16], Cn[:, ri * 2048 + gp * 128: ri * 2048 + (gp + 1) * 128], ident[0:16, 0:16])
            CP("dve", Cs, bC[:, :])
            C.off = m3
            CP("dve", ldt, R["dt2"][:, 0:2])
            CP("dve", st16[:, 256:384].rearrange("p (a b) -> p a b", a=2), bc(ldt, [16, 2, 64], 2))
            bA = next_bank()
            for k in range(3):
                TR(bA[:, k * 16:(k + 1) * 16], st16[:, k * 128:(k + 1) * 128], ident[0:16, 0:16])
            CP("dve", P16, bA[:, 0:48])
            ck(13)
            yield
            TT("dve", Wn.rearrange("p (h s) -> p h s", s=128), Wn.rearrange("p (h s) -> p h s", s=128), bc(sgumask, [128, 4, 128], 1), ALU.mult)
            bW = next_bank()
            for h in range(4):
                TR(bW[:, h * 128:(h + 1) * 128], Wn[:, h * 128:(h + 1) * 128], ident)
            CP("dve", R["wsT"][:], bW[:, :])
            if mode == "prep_test":
                dbg.update(wsT=R["wsT"][:])
            ck(14)
            yield
            ck(15)
            yield
            bE = next_bank()
            TR(bE[:, 0:16], colst, ident[0:16, 0:16])
            CP("dve", R["cols"][:], bE[:, 0:16])
            ck(1)
            yield
            zsrc = bass.AP(zeros_t, 0, [[SLOT // 2, 128], [1, SLOT // 2]]).bitcast(BF16)
            for s_ in (20, 21, 22):
                S.dma("sp", SCR(l, s_), zsrc, f"zf{l}")
            lr, li, dt = P16[:, 0:16], P16[:, 16:32], P16[:, 32:48]
            sm = C.take(16 * 12, F32)
            smv = [sm[:, i * 16:(i + 1) * 16] for i in range(12)]
            xx, th, den, rden, nr, t0, t1, kre, kim, t2, t3, t4 = smv
            TT("dve", xx, lr, dt, ALU.mult)
            TT("dve", th, li, dt, ALU.mult)
            TT("dve", t0, lr, lr, ALU.mult)
            TT("dve", t1, li, li, ALU.mult)
            TT("dve", den, t0, t1, ALU.add)
            S.op("dve", lambda e: e.reciprocal(rden, den), reads=[den], writes=[rden])
            ck(2)
            yield
            TN = 16 * NM
            SIN = C.take(TN, F32)
            COS = C.take(TN, F32)
            m1 = C.off
            Tt = C.take(TN, F32)
            Ni = C.take(TN, I32)
            Nf = C.take(TN, F32)
            MAG = C.take(TN, F32)
            v3 = lambda a: a.rearrange("p (g m) -> p g m", m=NM)
            thB = bc(th, [128, 16, NM], 2)
            xxB = bc(xx, [128, 16, NM], 2)
            multB = bc(mult, [128, 16, NM], 1)
            STT(v3(Tt), thB, 1.0 / (2.0 * np.pi), multB, ALU.mult, ALU.mult)
            CP("dve", Ni, Tt)
            CP("dve", Nf, Ni)
            TT("dve", Nf, Tt, Nf, ALU.subtract)
            ACT(SIN, Nf, AF.Sin, scale=TWO_PI_SAFE)
            TS("dve", Tt, Tt, 0.25, None, ALU.add)
            CP("dve", Ni, Tt)
            CP("dve", Nf, Ni)
            TT("dve", Nf, Tt, Nf, ALU.subtract)
            ACT(COS, Nf, AF.Sin, scale=TWO_PI_SAFE)
            TT("dve", v3(Tt), xxB, multB, ALU.mult)
            ACT(MAG, Tt, AF.Exp)
            ck(3)
            yield
            CP("dve", R["Ec"][:].rearrange("p (g q) -> p g q", q=NQ), v3(COS)[:, :, 17:17 + NQ])
            CP("dve", R["Es"][:].rearrange("p (g q) -> p g q", q=NQ), v3(SIN)[:, :, 17:17 + NQ])
            CP("dve", R["r8"][:], v3(MAG)[:, :, 8])
            if mode == "prep_test":
                dbg.update(Ec=R["Ec"][:], Es=R["Es"][:], r8=R["r8"][:])
            Ar, Ai = COS, SIN
            TT("dve", Ar, MAG, COS, ALU.mult)
            TT("dve", Ai, MAG, SIN, ALU.mult)
            C.off = m1
            ck(4)
            yield
            TS("dve", nr, v3(Ar)[:, :, 1], -1.0, None, ALU.add)
            ni = v3(Ai)[:, :, 1]
            TT("dve", t0, nr, lr, ALU.mult)
            TT("dve", t1, ni, li, ALU.mult)
            TT("dve", t0, t0, t1, ALU.add)
            TT("dve", kre, t0, rden, ALU.mult)
            TT("dve", t2, ni, lr, ALU.mult)
            TT("dve", t3, nr, li, ALU.mult)
            TT("dve", t2, t2, t3, ALU.subtract)
            TT("dve", kim, t2, rden, ALU.mult)
            cre = C.take(128, F32)
            cim = C.take(128, F32)
            ct = C.take(128, F32)
            c3 = lambda a: a.rearrange("p (g i) -> p g i", i=8)
            ArW, AiW = v3(Ar)[:, :, 9:17], v3(Ai)[:, :, 9:17]
            kreB, kimB = bc(kre, [128, 16, 8], 2), bc(kim, [128, 16, 8], 2)
            TT("dve", c3(cre), ArW, kreB, ALU.mult)
            TT("dve", c3(ct), AiW, kimB, ALU.mult)
            TT("dve", cre, cre, ct, ALU.subtract)
            TT("dve", c3(cim), ArW, kimB, ALU.mult)
            TT("dve", c3(ct), AiW, kreB, ALU.mult)
            TT("dve", cim, cim, ct, ALU.add)
            ck(5)
            yield
            WWre = C.take(2048, F32)
            WWim = C.take(2048, F32)
            m2 = C.off
            WWt = C.take(2048, F32)
            w4 = lambda a: a.rearrange("p (g i c) -> p g i c", i=8, c=16)
            b3 = lambda a: a.rearrange("p (g c) -> p g c", c=16)
            creB, cimB = bc(c3(cre), [128, 16, 8, 16], 3), bc(c3(cim), [128, 16, 8, 16], 3)
            BreB, BimB = bc(b3(Bre), [128, 16, 8, 16], 2), bc(b3(Bim), [128, 16, 8, 16], 2)
            TT("dve", w4(WWre), creB, BreB, ALU.mult)
            TT("dve", w4(WWt), cimB, BimB, ALU.mult)
            TT("dve", WWre, WWre, WWt, ALU.subtract)
            TT("dve", w4(WWim), creB, BimB, ALU.mult)
            TT("dve", w4(WWt), cimB, BreB, ALU.mult)
            TT("dve", WWim, WWim, WWt, ALU.add)
            ck(6)
            yield
            Wfin = C.take(SLOT, BF16)
            for ri, WW in enumerate((WWre, WWim)):
                for k4 in range(4):
                    bk = next_bank()
                    for j in range(4):
                        gp = k4 * 4 + j
                        TR(bk[:, j * 128:(j + 1) * 128], WW[:, gp * 128:(gp + 1) * 128], ident)
                    dst = Wfin.rearrange("p (g r q) -> p g r q", r=2, q=64)[:, 8 * k4:8 * k4 + 8, ri, :]
                    evac_copy(dst, bk[:, :].rearrange("p (g q) -> p g q", q=64))
            S.dma("act", SCR(l, 19), Wfin, "prep")
            if mode == "prep_test":
                dbg.update(Wfin=Wfin)
            if mode != "prep_test":
                C.off = m2
            ck(7)
            yield
            Cre, Cim = b3(Cs[:, 0:256]), b3(Cs[:, 256:512])
            ck(8)
            yield
            Vcb = [C.take(16 * 9 * 16, BF16), C.take(16 * 9 * 16, BF16)]
            W7 = C.take(2 * 256, BF16)
            m4 = C.off
            VVre = C.take(16 * 9 * 16, F32)
            VVim = C.take(16 * 9 * 16, F32)
            VVt = C.take(16 * 9 * 16, F32)
            vv4 = lambda a: a.rearrange("p (g m c) -> p g m c", m=9, c=16)
            CreB, CimB = bc(Cre, [128, 16, 9, 16], 2), bc(Cim, [128, 16, 9, 16], 2)
            ArB, AiB = bc(v3(Ar)[:, :, 0:9], [128, 16, 9, 16], 3), bc(v3(Ai)[:, :, 0:9], [128, 16, 9, 16], 3)
            TT("dve", vv4(VVre), CreB, ArB, ALU.mult)
            TT("dve", vv4(VVt), CimB, AiB, ALU.mult)
            TT("dve", VVre, VVre, VVt, ALU.subtract)
            TT("dve", vv4(VVim), CreB, AiB, ALU.mult)
            TT("dve", vv4(VVt), CimB, ArB, ALU.mult)
            TT("dve", VVim, VVim, VVt, ALU.add)
            TS("dve", VVim, VVim, -1.0, None, ALU.mult)
            ck(9)
            yield
            Vc9 = []
            for ri, VV in enumerate((VVre, VVim)):
                Vc = Vcb[ri]
                CP("dve", Vc, VV)
                Vc9.append(Vc)
                Vc4 = Vc.rearrange("p (g m c) -> p g m c", m=9, c=16)
                base = (l * NSLAB_ALL + 21 + ri) * 128 * SLOT
                for g2 in range(2):
                    dst = bass.AP(wscr, base + g2 * 64 * SLOT + g2 * 128, [[SLOT, 64], [256, 16], [8 * 16, 1], [1, 128]])
                    S.dma("pool", dst, Vc4[g2 * 64:(g2 + 1) * 64, :, 1:9, :].rearrange("p g m c -> p g (m c)"), f"vp{l}{ri}", parallel=True)
            CP("dve", W7[:, 0:256].rearrange("p (g c) -> p g c", c=16), w4(WWre)[:, :, 7, :])
            CP("dve", W7[:, 256:512].rearrange("p (g c) -> p g c", c=16), w4(WWim)[:, :, 7, :])
            ck(10)
            yield
            if mode != "prep_test":
                C.off = m4
            bD = next_bank()
            TR(bD[0:16, 0:32], dst32, ident[0:32, 0:32])
            Dg = C.take(32, F32, parts=16)
            CP("dve", Dg, bD[0:16, 0:32])
            tmpD = C.take(512, F32, parts=16)
            TT("dve", tmpD.rearrange("p (g c) -> p g c", c=16), bc(ident[0:16, 0:16], [16, 32, 16], 1), bc(Dg, [16, 32, 16], 2), ALU.mult)
            tD4 = tmpD.rearrange("p (gp h c) -> p gp h c", h=2, c=16)
            Krb = C.take(32 * 128, BF16, parts=16)
            Kb4 = Krb.rearrange("p (gp h f) -> p gp h f", h=2, f=128)
            Kb5 = Krb.rearrange("p (gp h m c) -> p gp h m c", h=2, m=8, c=16)
            for g2 in range(2):
                rows = slice(g2 * 64, (g2 + 1) * 64)
                for k4 in range(4):
                    bk = next_bank()
                    for j in range(4):
                        gp = k4 * 4 + j
                        o = bk[0:16, j * 128:(j + 1) * 128]
                        MM(o, W7[rows, gp * 16:(gp + 1) * 16], Vc9[0][rows, gp * 144:gp * 144 + 128], True, False)
                        MM(o, W7[rows, 256 + gp * 16:256 + (gp + 1) * 16], Vc9[1][rows, gp * 144:gp * 144 + 128], False, True)
                    CP("act", Kb4[:, k4 * 4:(k4 + 1) * 4, g2, :], bk[0:16, :].rearrange("p (j f) -> p j f", f=128))
                    TT("dve", Kb5[:, k4 * 4:(k4 + 1) * 4, g2, 0, :], bk[0:16, :].rearrange("p (j f) -> p j f", f=128)[:, :, 0:16],
                       tD4[:, k4 * 4:(k4 + 1) * 4, g2, :], ALU.add)
            ck(11)
            yield
            if mode == "prep_test":
                dbg.update(Krow=Krb)
            K4 = Krb.rearrange("p (g m c) -> p g m c", m=8, c=16)
            base = (l * NSLAB_ALL + 20) * 128 * SLOT
            for i in range(8):
                dst = bass.AP(wscr, base + 16 * i * SLOT + i * 16, [[SLOT, 16], [128, 32], [1, (8 - i) * 16]])
                S.dma("pool", dst, K4[:, :, 0:8 - i, :].rearrange("p g m c -> p g (m c)"), f"tp{l}", parallel=True)
            if mode == "prep_test":
                dbg.update(cols=R["cols"][:])

        if mode == "prep_test":
            emit_prep_early(0)
            try:
                for _ in emit_prep(0, AR):
                    pass
            except _Stop:
                dbg["P16"] = cst[:, 0:16]
            outs = []
            for k, ap in dbg.items():
                shp = list(ap.shape)
                o = nc.dram_tensor("o_" + k, shp, ap.dtype, kind="ExternalOutput").ap()
                S.dma("sp", o, ap, "dbgout")
                outs.append("dbgout")
            S.emit(final_streams=["dbgout"])
            return nc

        nblk = (SPC * BPS) if nblocks is None else nblocks
        plan = [(b, l, s_) for b in range(nblk) for l in layers for s_ in ORDER]
        rs = {"next_load": 0, "next_use": 0, "released": set()}

        def _pump():
            while rs["next_load"] < len(plan):
                k = rs["next_load"]
                if k >= rs["next_use"] + NRING:
                    break
                if k >= NRING and (k - NRING) not in rs["released"]:
                    break
                _, l_, s_ = plan[k]
                S.dma("sp", ring[k % NRING][:], SCR(l_, s_), f"ring{k % NRING}")
                rs["next_load"] += 1

        def acquire(l_, s_):
            k = rs["next_use"]
            assert plan[k][1:] == (l_, s_), (plan[k], l_, s_)
            rs["next_use"] += 1
            _pump()
            assert rs["next_load"] > k, "ring deadlock"
            return k, ring[k % NRING]

        def release(k):
            rs["released"].add(k)
            _pump()

        def AV(off, nelem, dtype, parts=128):
            words = (nelem * mybir.dt.size(dtype) + 3) // 4
            ap = AR[0:parts, off:off + words]
            if dtype != F32:
                ap = ap.bitcast(dtype)
            return ap[:, 0:nelem]

        o = 0
        Atok = AV(o, 4096, BF16); Ysb = Atok; o += 2048
        Xim = AV(o, 2048, BF16); o += 1024
        agT = AV(o, 2048, BF16); o += 1024
        st_ = []
        for _ in range(10):
            st_.append(AV(o, 512, F32)); o += 512
        tA, tB, tC, tD, Gin_re, Gin_im, G_re, G_im, H_re, H_im = st_
        Hs = [AV(o, 16 * 65, BF16), AV(o + 528, 16 * 65, BF16)]; o += 1056
        ygT = AV(o, 2048, BF16); o += 1024
        sig = AV(o, 2048, BF16); o += 1024
        T3o = o
        bvT = AV(o, 4 * 528, F32); o += 2112
        sA = AV(o, 528, F32); o += 528
        sB = AV(o, 528, F32); o += 528
        pT = AV(o, 2048, BF16); o += 1024
        bgT = AV(o, 2048, BF16); o += 1024
        cuT = AV(o, 2048, BF16); o += 1024
        cgT = AV(o, 2048, BF16); o += 1024
        vn = AV(o, 2048, BF16); o += 1024
        lnst = AV(o, 64, F32); o += 64
        st2_ = []
        for _ in range(6):
            st2_.append(AV(o, 512, F32)); o += 512
        assert o <= AR_WORDS, o
        sq = AV(0, 4096, BF16)
        rstd = AV(2048, 512, F32)
        rstd2 = AV(2560, 512, F32)
        gk = AV(0, 4096, BF16)
        mergedF = AV(2048, 4096, F32)
        mergedT = AV(6144, 4096, BF16)
        mtmp = [AV(8192, 512, F32), AV(8704, 512, F32)]
        xtok = AV(4096, 4096, F32)
        fstat = AV(9216, 16, F32)
        junkT = AV(2048 + 1024, 1024, BF16)

        def v(ap, pat, **kw):
            return ap.rearrange(pat, **kw)

        def emit_L1(b, l, xs):
            R = L[l]
            gcol = R["cols"]
            S.stage = f'b{b}l{l}:L1'
            bk = next_bank()
            for kt in range(KT):
                ACT(sq[:, kt * NB:(kt + 1) * NB], xs[:, kt * NB:(kt + 1) * NB], AF.Square)
                MM(bk[:, :], onesb[:], sq[:, kt * NB:(kt + 1) * NB], kt == 0, kt == KT - 1)
            ACT(rstd2, bk[:, :], AF.Sqrt, bias=epsc[:, 0:1], scale=1.0 / D)
            S.op("dve", lambda e: e.reciprocal(rstd, rstd2), reads=[rstd2], writes=[rstd])
            for kt in range(KT):
                STT(hT[:, kt * NB:(kt + 1) * NB], xs[:, kt * NB:(kt + 1) * NB], gcol[:, kt:kt + 1], rstd, ALU.mult, ALU.mult)

        def emit_layer(b, l, after_l1=None, x_src=None, skip_l1=False, pre_wout=None):
            R = L[l]
            PL = "dve" if (b == 0 and l == layers[0]) else "pool"
            first = (b % BPS) == 0
            gcol = R["cols"]
            S.stage = f'b{b}l{l}:L1'
            if not skip_l1:
                emit_L1(b, l, xT if x_src is None else x_src)

            def fm_tiles(slot, ncol_tiles, consume):
                for m in range(ncol_tiles):
                    bk_ = next_bank()
                    for kt in range(KT):
                        MM(bk_[:, :], slot[:, kt * 512 + m * 128: kt * 512 + (m + 1) * 128], hT[:, kt * NB:(kt + 1) * NB], kt == 0, kt == KT - 1)
                    consume(m, bk_)

            S.stage = f'b{b}l{l}:aval'
            k0, sl = acquire(l, 0)
            A2 = v(Atok[:, 0:2048], "p (g i c) -> p g i c", i=4, c=16)
            ab = [next_bank() for _ in range(4)]
            for kt in range(KT):
                for i0 in range(4):
                    lh = v(hT[:, kt * NB:(kt + 1) * NB], "p (m i) -> p m i", i=4)[:, :, i0]
                    MM(ab[i0][:, :], lh, sl[:, kt * 512:(kt + 1) * 512], kt == 0, kt == KT - 1)
            for i0 in range(4):
                evac_copy(A2[:, :, i0, :], v(ab[i0][:, :], "p (g c) -> p g c", c=16))
            release(k0)
            S.stage = f'b{b}l{l}:trin'
            for g8 in range(4):
                bk = next_bank()
                for j in range(8):
                    g = g8 * 8 + j
                    for h in range(2):
                        MM(bk[h * 64:(h + 1) * 64, j * NQ:(j + 1) * NQ], Atok[:, g * 64:(g + 1) * 64], selb[:, h * 64:(h + 1) * 64], True, True)
                evac_copy(Xim[:, g8 * 8 * NQ:(g8 + 1) * 8 * NQ], bk[:, 0:8 * NQ])
            if after_l1 is not None:
                after_l1()
            S.stage = f'b{b}l{l}:S'
            k1, wf = acquire(l, 19)
            Sb = []
            for hf in range(2):
                bre, bim = next_bank(), next_bank()
                for gpl in range(8):
                    gp = hf * 8 + gpl
                    for g2 in range(2):
                        g = 2 * gp + g2
                        for ri, bk in enumerate((bre, bim)):
                            MM(bk[g2 * 64:(g2 + 1) * 64, gpl * NQ:(gpl + 1) * NQ], wf[:, g * 128 + ri * 64: g * 128 + (ri + 1) * 64],
                               Xim[:, g * NQ:(g + 1) * NQ], True, True)
                Sb.append((bre, bim))
            release(k1)
            S.stage = f'b{b}l{l}:rot'
            carry = R["carry"]
            for ri in range(2):
                hs3 = v(Hs[ri], "p (g q) -> p g q", q=65)
                if first:
                    MS(PL, hs3[:, :, 0], 0.0)
                else:
                    CP(PL, hs3[:, :, 0], carry[:, ri * 16:(ri + 1) * 16])
            W_ = 8 * NQ
            pre = [(tA, tB, tC, tD, Gin_re, Gin_im), tuple(st2_)]
            for hf in range(2):
                bre, bim = Sb[hf]
                Ec = R["Ec"][:, hf * 8 * NQ:(hf + 1) * 8 * NQ]
                Es = R["Es"][:, hf * 8 * NQ:(hf + 1) * 8 * NQ]
                a_, b_, c_, d_, gr_, gi_ = pre[hf]
                TT("dve", a_, bre[:, 0:W_], Ec, ALU.mult)
                TT("dve", d_, bre[:, 0:W_], Es, ALU.mult)
                TT("dve", b_, bim[:, 0:W_], Es, ALU.mult)
                TT("dve", c_, bim[:, 0:W_], Ec, ALU.mult)
                TT(PL, gr_, a_, b_, ALU.add)
                TT(PL, gi_, c_, d_, ALU.subtract)
            for hf in range(2):
                Ec = R["Ec"][:, hf * 8 * NQ:(hf + 1) * 8 * NQ]
                Es = R["Es"][:, hf * 8 * NQ:(hf + 1) * 8 * NQ]
                for ri, (Gin, G) in enumerate(((pre[hf][4], G_re), (pre[hf][5], G_im))):
                    for gpl in range(8):
                        gp = hf * 8 + gpl
                        d0 = R["r8"][:, gp:gp + 1].to_broadcast([128, NQ])
                        init = 0.0 if first else carry[:, ri * 16 + gp: ri * 16 + gp + 1]
                        o_ = G[:, gpl * NQ:(gpl + 1) * NQ]
                        i_ = Gin[:, gpl * NQ:(gpl + 1) * NQ]
                        rd = [R["r8"][:, gp:gp + 1], i_] + ([] if first else [init])
                        S.op("dve", lambda e, o_=o_, d0=d0, i_=i_, init=init: e.tensor_tensor_scan(o_, d0, i_, init, ALU.mult, ALU.add),
                             reads=rd, writes=[o_])
                TT("dve", tA, G_re, Ec, ALU.mult)
                TT("dve", tC, G_re, Es, ALU.mult)
                TT("dve", tB, G_im, Es, ALU.mult)
                TT("dve", tD, G_im, Ec, ALU.mult)
                TT(PL, H_re, tA, tB, ALU.subtract)
                TT(PL, H_im, tC, tD, ALU.add)
                for ri, H in enumerate((H_re, H_im)):
                    hs3 = v(Hs[ri], "p (g q) -> p g q", q=65)
                    h3 = v(H, "p (g q) -> p g q", q=NQ)
                    CP("dve", hs3[:, hf * 8:(hf + 1) * 8, 1:NQ + 1], h3)
                    CP(PL, carry[:, ri * 16 + hf * 8: ri * 16 + (hf + 1) * 8], h3[:, :, NQ - 1])
            S.stage = f'b{b}l{l}:win'
            k2, sl = acquire(l, 1)
            fm_tiles(sl, 4, lambda m, bk_: ACT(agT[:, m * NB:(m + 1) * NB], bk_[:, :], AF.Silu))
            release(k2)
            bv3 = v(bvT, "p (g t) -> p g t", t=528)
            if first:
                MS(PL, bv3[:, :, 0:16], 0.0)
            else:
                CP(PL, bv3[:, :, 0:16], v(R["halo"][:], "p (g t) -> p g t", t=16))
            k7, sl = acquire(l, 2)
            fm_tiles(sl, 4, lambda m, bk_: evac_copy(bv3[:, m, 16:528], bk_[:, :], "act"))
            release(k7)
            CP(PL, v(R["halo"][:], "p (g t) -> p g t", t=16), bv3[:, :, 512:528])
            k8, sl = acquire(l, 3)
            fm_tiles(sl, 4, lambda m, bk_: ACT(bgT[:, m * NB:(m + 1) * NB], bk_[:, :], AF.Silu))
            release(k8)
            k9, sl = acquire(l, 4)
            fm_tiles(sl, 4, lambda m, bk_: evac_copy(cuT[:, m * NB:(m + 1) * NB], bk_[:, :], "act"))
            release(k9)
            k11, sl = acquire(l, 6)
            fm_tiles(sl, 4, lambda m, bk_: ACT(cgT[:, m * NB:(m + 1) * NB], bk_[:, :], AF.Silu))
            release(k11)
            S.stage = f'b{b}l{l}:poolel'
            for m in range(4):
                u = bv3[:, m, :]
                w = 2 ** (m + 1)
                cur, nxt = sA, sB
                TT(PL, cur[:, 1:528], u[:, 1:528], u[:, 0:527], ALU.add)
                sh = 2
                while sh < w:
                    lo = 2 * sh - 1
                    TT(PL, nxt[:, lo:528], cur[:, lo:528], cur[:, lo - sh:528 - sh], ALU.add)
                    cur, nxt = nxt, cur
                    sh *= 2
                STT(pT[:, m * NB:(m + 1) * NB], cur[:, 16:528], 1.0 / w, u[:, 16:528], ALU.mult, ALU.subtract)
                if first:
                    TT(PL, nxt[:, 0:16], cur[:, 16:32], rcfix[:, m * 16:(m + 1) * 16], ALU.mult)
                    TT(PL, pT[:, m * NB:m * NB + 16], nxt[:, 0:16], u[:, 16:32], ALU.subtract)
            TT("dve", cuT, cuT, cgT, ALU.mult)
            S.stage = f'b{b}l{l}:Y'
            k3, tp = acquire(l, 20)
            k4_, vre = acquire(l, 21)
            k5, vim = acquire(l, 22)
            Y5 = v(Ysb[0:NQ, :], "q (f j g c) -> q f j g c", f=4, j=8, g=8)
            for ft in range(4):
                for hb in range(2):
                    bk = next_bank()
                    for pj in range(2):
                        gp = ft * 4 + hb * 2 + pj
                        first_mm = pj == 0
                        cols = slice(pj * 256, (pj + 1) * 256)
                        MM(bk[0:NQ, cols], v(Hs[0], "p (g q) -> p g q", q=65)[:, gp, 0:NQ], vre[:, gp * 256:(gp + 1) * 256], first_mm, False)
                        MM(bk[0:NQ, cols], v(Hs[1], "p (g q) -> p g q", q=65)[:, gp, 0:NQ], vim[:, gp * 256:(gp + 1) * 256], False, False)
                        for g2 in range(2):
                            g = 2 * gp + g2
                            c0 = pj * 256 + g2 * 128
                            MM(bk[0:NQ, c0:c0 + 128], Xim[:, g * NQ:(g + 1) * NQ], tp[:, g * 128:(g + 1) * 128], False, pj == 1 and g2 == 1)
                    evac_copy(Y5[:, ft, :, hb * 4:(hb + 1) * 4, :].rearrange("q j g c -> q g j c"), v(bk[0:NQ, :], "q (g j c) -> q g j c", j=8, c=16))
            release(k3); release(k4_); release(k5)
            S.stage = f'b{b}l{l}:trout'
            for ft in range(4):
                bk = next_bank()
                bkb = bk[:, :].bitcast(BF16)
                for j in range(8):
                    TR(bkb[:, j * NQ:(j + 1) * NQ], Ysb[0:NQ, (ft * 8 + j) * 128:(ft * 8 + j + 1) * 128], identb[0:NQ, 0:NQ])
                ACT(v(ygT[:, ft * NB:(ft + 1) * NB], "p (q j) -> p j q", j=8), v(bkb[:, 0:8 * NQ], "p (j q) -> p j q", j=8), AF.Gelu_apprx_tanh)
            S.stage = f'b{b}l{l}:cv'
            k10, sl = acquire(l, 5)
            for tt in range(NTT):
                bk = next_bank()
                for kt in range(KT):
                    MM(bk[:, :], hT[:, kt * NB + tt * 128: kt * NB + (tt + 1) * 128], sl[:, kt * 512:(kt + 1) * 512], kt == 0, kt == KT - 1)
                st6 = lnst[:, tt * 6:(tt + 1) * 6]
                mv = lnst[:, 24 + tt * 2: 24 + (tt + 1) * 2]
                rsd = lnst[:, 32 + tt: 33 + tt]
                S.op("dve", lambda e, st6=st6, bk=bk: e.bn_stats(st6, bk[:, :]), reads=[bk[:, :]], writes=[st6])
                S.op("dve", lambda e, st6=st6, mv=mv: e.bn_aggr(mv, st6), reads=[st6], writes=[mv])
                TS("pool", rsd, mv[:, 1:2], LN_EPS, None, ALU.add)
                TT("pool", rsd, rsd, epsc[:, 2:3], ALU.pow)
                TS("dve", vn[:, tt * DB:(tt + 1) * DB], bk[:, :], mv[:, 0:1], rsd, ALU.subtract, ALU.mult)
                TT("dve", vn[:, tt * DB:(tt + 1) * DB], vn[:, tt * DB:(tt + 1) * DB], R["lnG"][:], ALU.mult)
                TT("dve", vn[:, tt * DB:(tt + 1) * DB], vn[:, tt * DB:(tt + 1) * DB], R["lnB"][:], ALU.add)
            release(k10)
            S.stage = f'b{b}l{l}:glu'
            k6, sl = acquire(l, 13)
            gb = [next_bank() for _ in range(4)]
            for kt in range(4):
                for m in range(4):
                    MM(gb[m][:, :], sl[:, kt * 512 + m * 128: kt * 512 + (m + 1) * 128], ygT[:, kt * NB:(kt + 1) * NB], kt == 0, kt == 3)
            for m in range(4):
                ACT(sig[:, m * NB:(m + 1) * NB], gb[m][:, :], AF.Sigmoid, bias=gcol[:, 8 + m:9 + m])
            release(k6)
            TT("dve", yT[0][:], ygT, sig, ALU.mult)
            TT("dve", yT[0][:], yT[0][:], agT, ALU.mult)
            def merge_branch(k):
                for hg in range(2):
                    kg, sl_ = acquire(l, 7 + 2 * k + hg)
                    fm_tiles(sl_, 4, lambda m, bk_, hg=hg: ACT(gk[:, (hg * 4 + m) * NB:(hg * 4 + m + 1) * NB], bk_[:, :], AF.Sigmoid))
                    release(kg)
                kb, sl_ = acquire(l, 14 + k)
                for d8 in range(8):
                    bk_ = next_bank()
                    for kt in range(4):
                        MM(bk_[:, :], sl_[:, kt * D + d8 * 128: kt * D + (d8 + 1) * 128], yT[k][:, kt * NB:(kt + 1) * NB], kt == 0, kt == 3)
                    gsl = gk[:, d8 * NB:(d8 + 1) * NB]
                    mf = mergedF[:, d8 * NB:(d8 + 1) * NB]
                    if k == 0:
                        TT("dve", mf, bk_[:, :], gsl, ALU.mult)
                    else:
                        tmp = mtmp[d8 % 2]
                        TT("dve", tmp, bk_[:, :], gsl, ALU.mult)
                        if k == 1:
                            TT(PL, mf, mf, tmp, ALU.add)
                        else:
                            TT("dve" if d8 % 2 else PL, mergedT[:, d8 * NB:(d8 + 1) * NB], mf, tmp, ALU.add)
                release(kb)

            S.stage = f'b{b}l{l}:mergeA'
            merge_branch(0)
            S.stage = f'b{b}l{l}:sgu'
            for h in range(4):
                bk = next_bank()
                for tt in range(NTT):
                    o_ = bk[:, tt * 128:(tt + 1) * 128]
                    MM(o_, vn[:, tt * DB + h * 128: tt * DB + (h + 1) * 128], R["wsT"][:, h * 128:(h + 1) * 128], tt == 0, False)
                    MM(o_, onesb[:], R["bsz"][:, h * 128:(h + 1) * 128], False, tt == NTT - 1)
                TT("dve", yT[2][:, h * NB:(h + 1) * NB], bk[:, :], cuT[:, h * NB:(h + 1) * NB], ALU.mult)
            S.stage = f'b{b}l{l}:poolmm'
            for m in range(4):
                bk = next_bank()
                MM(bk[:, :], R["pw"][:, m * 128:(m + 1) * 128], pT[:, m * NB:(m + 1) * NB], True, True)
                STT(yT[1][:, m * NB:(m + 1) * NB], bk[:, :], gcol[:, 12 + m:13 + m], bgT[:, m * NB:(m + 1) * NB], ALU.mult, ALU.mult)

            if l == layers[-1] and b + 1 < nblk:
                emit_x_load(b + 1)
            S.stage = f'b{b}l{l}:mergeB'
            merge_branch(1)
            S.stage = f'b{b}l{l}:mergeC'
            merge_branch(2)
            S.stage = f'b{b}l{l}:wout'
            if pre_wout is not None:
                pre_wout()
            ko0, sl0 = acquire(l, 17)
            ko1, sl1 = acquire(l, 18)
            ob = [next_bank() for _ in range(8)]
            for kt in range(KT):
                for d8 in range(8):
                    sl_ = sl0 if d8 < 4 else sl1
                    m = d8 % 4
                    MM(ob[d8][:, :], sl_[:, kt * 512 + m * 128: kt * 512 + (m + 1) * 128], mergedT[:, kt * NB:(kt + 1) * NB], kt == 0, kt == KT - 1)
            for d8 in range(8):
                TT("dve", xT[:, d8 * NB:(d8 + 1) * NB], xT[:, d8 * NB:(d8 + 1) * NB], ob[d8][:, :], ALU.add)
            release(ko0); release(ko1)

        assert T3o == XIN_LO, (T3o, XIN_LO)
        xin = AV(T3o, 4096, F32)

        def emit_x_load(b):
            S.dma("sp", v(xin, "p (k t) -> p k t", t=NB), bass.AP(x_d.tensor, b * NB, [[NTOK, 128], [128 * NTOK, KT], [1, NB]]), "xin")

        fsq = AV(10272, 4096, BF16)
        frs2 = AV(3072, 512, F32)
        frs = AV(3584, 512, F32)

        def emit_x_to_xT():
            S.dma("sp", xT[:, :], xin, "x2x")

        def emit_final_norm(b):
            S.stage = f'b{b}:fnorm'
            ost = xtok
            if do_final:
                bk = next_bank()
                for kt in range(KT):
                    ACT(fsq[:, kt * NB:(kt + 1) * NB], xT[:, kt * NB:(kt + 1) * NB], AF.Square)
                    MM(bk[:, :], onesb[:], fsq[:, kt * NB:(kt + 1) * NB], kt == 0, kt == KT - 1)
                ACT(frs2, bk[:, :], AF.Sqrt, bias=epsc[:, 0:1], scale=1.0 / D)
                S.op("dve", lambda e: e.reciprocal(frs, frs2), reads=[frs2], writes=[frs])
                for kt in range(KT):
                    STT(ost[:, kt * NB:(kt + 1) * NB], xT[:, kt * NB:(kt + 1) * NB], fngc[:, kt:kt + 1], frs, ALU.mult, ALU.mult)
            else:
                for kt in range(KT):
                    CP("dve", ost[:, kt * NB:(kt + 1) * NB], xT[:, kt * NB:(kt + 1) * NB])
            S.dma("sp", bass.AP(out_d.tensor, b * NB, [[NTOK, 128], [128 * NTOK, KT], [1, NB]]), v(ost, "p (k t) -> p k t", t=NB), "out0")

        for l in layers:
            emit_prep_early(l)
        gens = [emit_prep(l, AR if i == 0 else BIG) for i, l in enumerate(layers)]
        for g_ in gens:
            next(g_)
        emit_x_load(0)
        gate = S.sbuf("gate", [128, 2], F32)
        S.op("pool", lambda e: e.memset(gate[:], 0.0), reads=list(load_bufs), writes=[gate[:]])
        emit_wconv(layers[0])
        while gens:
            for g_ in list(gens):
                try:
                    next(g_)
                except StopIteration:
                    gens.remove(g_)
        for l in layers[1:]:
            emit_wconv(l)
        emit_L1(0, layers[0], xin)
        emit_x_to_xT()
        for b in range(nblk):
            for li, l in enumerate(layers):
                last = li == len(layers) - 1
                pre = None
                if last and b + 1 < nblk:
                    pre = (lambda b=b: emit_L1(b + 1, layers[0], xin))
                emit_layer(b, l, None, skip_l1=(li == 0), pre_wout=pre)
            if b + 1 < nblk:
                emit_final_norm(b)
                emit_x_to_xT()
        emit_final_norm(nblk - 1)
        S.emit(final_streams=["out0"])
        build_program.last_sched = S
    return nc


def kernel(**inputs):
    x = np.asarray(inputs["x"], dtype=np.float32)
    nc = build_program("full")
    consts = pack_consts()
    weights = {n: np.ascontiguousarray(np.asarray(inputs[n], dtype=np.float32)) for n in WEIGHT_NAMES}
    in_maps = []
    for c in range(NCORES):
        m = dict(weights)
        m["x"] = np.ascontiguousarray(x[c * SPC:(c + 1) * SPC].reshape(SPC * SEQ, D).T)
        m["consts"] = consts
        m["zeros"] = np.zeros((128, SLOT // 2), np.float32)
        in_maps.append(m)
    res = run_bass_kernel_spmd(nc, in_maps, core_ids=list(range(NCORES)))
    out = np.concatenate([np.ascontiguousarray(np.asarray(r["out"]).T).reshape(SPC, SEQ, D) for r in res.results], axis=0)
    return out.astype(np.float32)
```

```python
import numpy as np
from contextlib import ExitStack
import concourse.bass as bass
import concourse.mybir as mybir
from concourse.bass_utils import run_bass_kernel_spmd

F32 = mybir.dt.float32
BF16 = mybir.dt.bfloat16
I32 = mybir.dt.int32
AF = mybir.ActivationFunctionType
ALU = mybir.AluOpType
AX = mybir.AxisListType

D = 1024
SEQ = 2048
BATCH = 16
DEPTH = 2
NCORES = 8
SPC = BATCH // NCORES
DB = 512
DIN = 6656
KT = D // 128
NB = 512
NQ = NB // 8
NTT = NB // 128
BPS = SEQ // NB
RMS_EPS = 1e-6
LN_EPS = 1e-5
NM = 9 + 8 + NQ
TWO_PI_SAFE = 6.283185
SLOT = 4096
NSLAB = 19


def _ap_range(ap):
    es = mybir.dt.size(ap.dtype)
    pat = ap.ap
    off = int(ap.offset)
    space = type(ap.tensor).__name__
    if space.startswith("DRam"):
        ext = 1
        for st, cnt in pat:
            ext += (cnt - 1) * abs(st)
        return (ap.tensor.name, 0, 1, off * es, (off + ext) * es)
    pstep, pcnt = pat[0]
    if pstep == 0:
        pstep = 1 << 40
    p0 = off // pstep if pstep < (1 << 40) else 0
    col = off - p0 * pstep if pstep < (1 << 40) else off
    ext = 1
    for st, cnt in pat[1:]:
        ext += (cnt - 1) * abs(st)
    return (ap.tensor.name, p0, p0 + pcnt, col * es, (col + ext) * es)


def _ap_ranges(ap, max_pieces=40):
    name, p0, p1, b0, b1 = _ap_range(ap)
    if type(ap.tensor).__name__.startswith("DRam"):
        return [(name, p0, p1, b0, b1)]
    es = mybir.dt.size(ap.dtype)
    dims = [(st, cnt) for st, cnt in ap.ap[1:] if cnt > 1 and st != 0]
    if not dims or any(st < 0 for st, _ in dims):
        return [(name, p0, p1, b0, b1)]
    run = 1
    while dims and dims[-1][0] == run:
        run *= dims[-1][1]
        dims.pop()
    npieces = 1
    for _, cnt in dims:
        npieces *= cnt
    if not dims or npieces > max_pieces:
        return [(name, p0, p1, b0, b1)]
    offs = [0]
    for st, cnt in dims:
        offs = [o + k * st for o in offs for k in range(cnt)]
    return [(name, p0, p1, b0 + o * es, b0 + (o + run) * es) for o in offs]


class Tok:
    _n = 0

    def __init__(self, name="tok"):
        Tok._n += 1
        self.key = (f"__tok{Tok._n}_{name}", 0, 1, 0, 1)


class Op:
    __slots__ = ("eng", "fn", "deps", "is_dma", "dsem", "dcum", "signal", "cnt", "idx", "tag", "rw")


class Sched:
    ENG = ("pe", "act", "dve", "pool", "sp")
    EMAP = {"pe": "tensor", "act": "scalar", "dve": "vector", "pool": "gpsimd", "sp": "sync"}

    def __init__(self, nc, es):
        self.nc = nc
        self.es = es
        self.ops = []
        self.recs = {}
        self.sems = {e: es.enter_context(nc.semaphore(f"s_{e}")) for e in self.ENG}
        self.dstreams = {}
        import os
        self.debug_rw = bool(os.environ.get('DEBUG_RW'))

    def sbuf(self, name, shape, dtype):
        return self.es.enter_context(self.nc.sbuf_tensor(name, list(shape), dtype))

    def psum(self, name, shape, dtype=F32):
        return self.es.enter_context(self.nc.psum_tensor(name, list(shape), dtype))

    @staticmethod
    def _is_psum(x):
        return (not isinstance(x, Tok)) and type(x.tensor).__name__.startswith("PSum")

    def _rng(self, x):
        if isinstance(x, Tok):
            return x.key
        if self._is_psum(x):
            return (x.tensor.name, 0, 128, 0, 2048)
        return _ap_range(x)

    def _rngs(self, x):
        if isinstance(x, Tok) or self._is_psum(x):
            return [self._rng(x)]
        return _ap_ranges(x)

    @staticmethod
    def _remainders(r, p0, p1, b0, b1):
        rp0, rp1, rb0, rb1 = r[0], r[1], r[2], r[3]
        out = []
        if rp0 < p0:
            out.append((rp0, p0, rb0, rb1))
        if p1 < rp1:
            out.append((p1, rp1, rb0, rb1))
        q0, q1 = max(rp0, p0), min(rp1, p1)
        if rb0 < b0:
            out.append((q0, q1, rb0, b0))
        if b1 < rb1:
            out.append((q0, q1, b1, rb1))
        return out

    def _access(self, x, idx, write, deps):
        for rng in self._rngs(x):
            self._access1(rng, idx, write, deps)

    def _access1(self, rng, idx, write, deps):
        name, p0, p1, b0, b1 = rng
        lst = self.recs.setdefault(name, [])
        keep = []
        hit = False
        for r in lst:
            if r[0] < p1 and p0 < r[1] and r[2] < b1 and b0 < r[3]:
                hit = True
                if r[4] is not None:
                    deps.add(r[4])
                if write:
                    deps.update(r[5])
                else:
                    keep.append([max(r[0], p0), min(r[1], p1), max(r[2], b0), min(r[3], b1), r[4], r[5] + [idx]])
                for (a0, a1, c0, c1) in self._remainders(r, p0, p1, b0, b1):
                    keep.append([a0, a1, c0, c1, r[4], list(r[5])])
            else:
                keep.append(r)
        if write:
            keep.append([p0, p1, b0, b1, idx, []])
        elif not hit:
            keep.append([p0, p1, b0, b1, None, [idx]])
        else:
            covered = sum((min(r[1], p1) - max(r[0], p0)) * (min(r[3], b1) - max(r[2], b0)) for r in keep
                          if r[0] < p1 and p0 < r[1] and r[2] < b1 and b0 < r[3] and idx in r[5])
            if covered < (p1 - p0) * (b1 - b0):
                keep.append([p0, p1, b0, b1, None, [idx]])
        self.recs[name] = keep

    def op(self, eng, fn, reads=(), writes=(), dstream=None):
        o = Op()
        o.eng = eng
        o.fn = fn
        o.idx = len(self.ops)
        o.tag = getattr(self, 'stage', '')
        deps = set()
        for x in reads:
            self._access(x, o.idx, self._is_psum(x), deps)
        for x in writes:
            self._access(x, o.idx, True, deps)
        deps.discard(o.idx)
        o.deps = deps
        o.rw = ([self._rng(x) for x in reads], [self._rng(x) for x in writes]) if getattr(self, 'debug_rw', False) else None
        o.is_dma = dstream is not None
        o.signal = False
        o.cnt = 0
        if o.is_dma:
            if dstream not in self.dstreams:
                self.dstreams[dstream] = [self.es.enter_context(self.nc.semaphore(f"d_{dstream}")), 0]
            st = self.dstreams[dstream]
            if len(st) > 2 and not getattr(self, "_par", False):
                o.deps.add(st[2])
            st[1] += 16
            o.dsem, o.dcum = st[0], st[1]
            if len(st) > 2:
                st[2] = o.idx
            else:
                st.append(o.idx)
            self._par = False
        self.ops.append(o)
        return o

    def dma(self, queue, out_ap, in_ap, stream, extra_reads=(), extra_writes=(), parallel=False, **kw):
        self._par = parallel
        if stream == "prep":
            self._prr = getattr(self, "_prr", 0) + 1
            stream = f"prep{self._prr % 6}"
        return self.op(queue, lambda e: e.dma_start(out=out_ap, in_=in_ap, **kw),
                       reads=[in_ap, *extra_reads], writes=[out_ap, *extra_writes], dstream=stream)

    def emit(self, final_streams=()):
        nc, ops = self.nc, self.ops
        for o in ops:
            for d in o.deps:
                od = ops[d]
                if od.is_dma:
                    continue
                if od.eng == "pe" and o.eng == "pe" and not o.is_dma:
                    continue
                od.signal = True
        cnts = {e: 0 for e in self.ENG}
        for o in ops:
            if not o.is_dma and o.signal:
                cnts[o.eng] += 1
                o.cnt = cnts[o.eng]
        self.final_counts = cnts
        with nc.Block() as block:
            for ename in self.ENG:
                def body(eng, ename=ename):
                    known = {}
                    for o in ops:
                        if o.eng != ename:
                            continue
                        need = {}
                        for d in o.deps:
                            od = ops[d]
                            if od.is_dma:
                                key, sem, val = ("d", id(od.dsem)), od.dsem, od.dcum
                            else:
                                if od.eng == "pe" and ename == "pe" and not o.is_dma:
                                    continue
                                key, sem, val = ("e", od.eng), self.sems[od.eng], od.cnt
                            if known.get(key, 0) >= val:
                                continue
                            if key not in need or need[key][1] < val:
                                need[key] = (sem, val)
                        for key, (sem, val) in need.items():
                            eng.wait_ge(sem, val)
                            known[key] = val
                        ins = o.fn(eng)
                        if o.is_dma:
                            ins.then_inc(o.dsem, 16)
                        elif o.signal:
                            ins.then_inc(self.sems[ename], 1)
                    if ename == "sp":
                        for s in final_streams:
                            st = self.dstreams[s]
                            eng.wait_ge(st[0], st[1])
                getattr(block, self.EMAP[ename])(body)


def host_consts():
    c = {}
    c["ident"] = np.eye(128, dtype=np.float32)
    t = np.arange(128)
    c["sgumask"] = ((t[None, :] // 64) <= (t[:, None] // 64)).astype(np.float32)
    mult = np.concatenate([np.arange(9), np.arange(7, -1, -1), 8 * (np.arange(NQ) + 1)]).astype(np.float32)
    c["mult"] = np.tile(mult[None, :], (128, 1))
    rc = np.zeros((128, 4, 16), np.float32)
    for gi, w in enumerate((2, 4, 8, 16)):
        rc[:, gi, :] = 1.0 / np.minimum(np.arange(1, 17), w)
    c["rcfix"] = rc.reshape(128, 64)
    sel = np.zeros((128, 2, 64), np.float32)
    for q in range(64):
        for h in range(2):
            sel[2 * q + h, h, q] = 1.0
    c["sel"] = sel.reshape(128, 128)
    return c


CONST_LAYOUT = [("ident", 128), ("sgumask", 128), ("mult", NM), ("rcfix", 64), ("sel", 128)]
NCONST = sum(w for _, w in CONST_LAYOUT)


def pack_consts():
    c = host_consts()
    return np.concatenate([c[k] for k, _ in CONST_LAYOUT], axis=1).astype(np.float32)


ORDER = [0, 19, 1, 2, 3, 4, 6, 20, 21, 22, 5, 13, 7, 8, 14, 9, 10, 15, 11, 12, 16, 17, 18]
NSLAB_ALL = 23
NRING = 5

WEIGHT_NAMES = ["norm_g", "w_in", "s5_lam_re", "s5_lam_im", "s5_log_dt", "s5_b_re", "s5_b_im", "s5_c_re",
                "s5_c_im", "s5_d", "s5_w_glu", "s5_b_glu", "pool_w", "pool_scale", "sgu_ln_g", "sgu_ln_b",
                "sgu_w", "sgu_b", "w_branch", "w_out", "final_norm_g"]
WEIGHT_SHAPES = {
    "norm_g": [DEPTH, D], "w_in": [DEPTH, D, DIN], "s5_lam_re": [DEPTH, 32, 64], "s5_lam_im": [DEPTH, 32, 64],
    "s5_log_dt": [DEPTH, 32], "s5_b_re": [DEPTH, 32, 64, 16], "s5_b_im": [DEPTH, 32, 64, 16],
    "s5_c_re": [DEPTH, 32, 16, 64], "s5_c_im": [DEPTH, 32, 16, 64], "s5_d": [DEPTH, DB],
    "s5_w_glu": [DEPTH, DB, DB], "s5_b_glu": [DEPTH, DB], "pool_w": [DEPTH, 4, 128, 128],
    "pool_scale": [DEPTH, DB], "sgu_ln_g": [DEPTH, DB], "sgu_ln_b": [DEPTH, DB], "sgu_w": [DEPTH, 4, 128, 128],
    "sgu_b": [DEPTH, 4, 128], "w_branch": [DEPTH, 3, DB, D], "w_out": [DEPTH, D, D], "final_norm_g": [D],
}


def _numel(shape):
    n = 1
    for s in shape:
        n *= s
    return n


class Carver:
    def __init__(self, tensor_f32, base=0, hole=None):
        self.t = tensor_f32
        self.off = base
        self.hole = hole

    def take(self, nelem, dtype, parts=128, p0=0):
        nbytes = nelem * mybir.dt.size(dtype)
        words = (nbytes + 3) // 4
        words = (words + 7) // 8 * 8
        if self.hole is not None and self.off < self.hole[1] and self.off + words > self.hole[0]:
            self.off = self.hole[1]
        ap = self.t[p0:p0 + parts, self.off:self.off + words]
        self.off += words
        if dtype != F32:
            ap = ap.bitcast(dtype)
        return ap[:, 0:nelem]


def build_program(mode="full", nblocks=None, layers=(0, 1), do_final=True, x_is_T=False):
    nc = bass.Bass("TRN2", target_bir_lowering=False)
    NTOK = SPC * SEQ
    dram = {}
    for n in WEIGHT_NAMES:
        dram[n] = nc.dram_tensor(n, WEIGHT_SHAPES[n], F32, kind="ExternalInput")
    x_d = nc.dram_tensor("x", [D, NTOK], F32, kind="ExternalInput").ap()
    consts_d = nc.dram_tensor("consts", [128, NCONST], F32, kind="ExternalInput").ap()
    zeros_t = nc.dram_tensor("zeros", [128, SLOT // 2], F32, kind="ExternalInput")
    out_d = nc.dram_tensor("out", [D, NTOK], F32, kind="ExternalOutput").ap()
    wscr = nc.dram_tensor("wscr", [DEPTH, NSLAB_ALL, 128, SLOT], BF16, kind="Internal")
    dbg = {}

    def DAP(name, offset, pat):
        return bass.AP(dram[name], offset, pat)

    def SCR(l, s):
        return bass.AP(wscr, (l * NSLAB_ALL + s) * 128 * SLOT, [[SLOT, 128], [1, SLOT]])

    with ExitStack() as es:
        S = Sched(nc, es)
        cst = S.sbuf("cst", [128, NCONST], F32)
        ident = cst[:, 0:128]
        sgumask = cst[:, 128:256]
        mult = cst[:, 256:256 + NM]
        rcfix = cst[:, 256 + NM:256 + NM + 64]
        selb = S.sbuf("selb", [128, 128], BF16)
        identb = S.sbuf("identb", [128, 128], BF16)
        onesb = S.sbuf("onesb", [128, 128], BF16)
        epsc = S.sbuf("epsc", [128, 4], F32)
        fngc = S.sbuf("fngc", [128, KT], F32)
        fst = S.sbuf("fst", [KT, 128], F32)
        AR_WORDS = (36 if mode == 'prep_test' else 24) * 1024
        AR = S.sbuf("arena", [128, AR_WORDS], F32)
        PT = mode == "prep_test"
        BIG_WORDS = NRING * 2048 + 4096 + 2048 + 3 * 1024
        BIG = S.sbuf("big", [128, 2048 if PT else BIG_WORDS], F32)
        ring = [BIG[:, i * 2048:(i + 1) * 2048].bitcast(BF16) for i in range(1 if PT else NRING)]
        _o = NRING * 2048
        xT = None if PT else BIG[:, _o:_o + 4096]
        hT = None if PT else BIG[:, _o + 4096:_o + 6144].bitcast(BF16)
        yT = None if PT else [BIG[:, _o + 6144 + k * 1024:_o + 6144 + (k + 1) * 1024].bitcast(BF16) for k in range(3)]
        banks = [S.psum(f"bank{i}", [128, 512], F32) for i in range(8)]
        L = []
        for l in range(DEPTH):
            r = {}
            r["Ec"] = S.sbuf(f"Ec{l}", [128, 16 * NQ], F32)
            r["Es"] = S.sbuf(f"Es{l}", [128, 16 * NQ], F32)
            r["r8"] = S.sbuf(f"r8_{l}", [128, 16], F32)
            r["cols"] = S.sbuf(f"cols{l}", [128, 16], F32)
            r["pw"] = S.sbuf(f"pw{l}", [128, 4 * 128], BF16)
            r["wsT"] = S.sbuf(f"wsT{l}", [128, 4 * 128], BF16)
            r["lnG"] = S.sbuf(f"lnG{l}", [128, DB], BF16)
            r["lnB"] = S.sbuf(f"lnB{l}", [128, DB], BF16)
            r["bsz"] = S.sbuf(f"bsz{l}", [128, DB], BF16)
            r["carry"] = S.sbuf(f"carry{l}", [128, 2 * 16], F32)
            r["dt2"] = S.sbuf(f"dt2_{l}", [16, 4], F32)
            r["halo"] = S.sbuf(f"halo{l}", [128, 4 * 16], F32)
            L.append(r)

        XIN_LO = 2048 + 1024 + 1024 + 10 * 512 + 1056 + 1024 + 1024
        bank_ctr = [0]

        def next_bank():
            b = banks[bank_ctr[0] % 8]
            bank_ctr[0] += 1
            return b

        ev_ctr = [0]

        def evac_copy(out_ap, in_ap, eng=None):
            ev_ctr[0] += 1
            if eng == "act" or (eng is None and ev_ctr[0] % 2):
                S.op("act", lambda e: e.activation(out_ap, in_ap, AF.Copy), reads=[in_ap], writes=[out_ap])
            else:
                S.op("dve", lambda e: e.tensor_copy(out_ap, in_ap), reads=[in_ap], writes=[out_ap])

        def TT(eng, out, a, b, op):
            S.op(eng, lambda e: e.tensor_tensor(out, a, b, op), reads=[a, b], writes=[out])

        def TS(eng, out, a, s1, s2, op0, op1=None):
            rd = [a] + [s for s in (s1, s2) if not isinstance(s, (int, float)) and s is not None]
            if op1 is None:
                S.op(eng, lambda e: e.tensor_scalar(out, a, s1, None, op0), reads=rd, writes=[out])
            else:
                S.op(eng, lambda e: e.tensor_scalar(out, a, s1, s2, op0, op1), reads=rd, writes=[out])

        def STT(out, a, s, b, op0, op1):
            rd = [a, b] + ([] if isinstance(s, (int, float)) else [s])
            S.op("dve", lambda e: e.scalar_tensor_tensor(out, a, s, b, op0, op1), reads=rd, writes=[out])

        def ACT(out, in_, func, bias=None, scale=None, accum_out=None):
            kw = {}
            rd = [in_]
            wr = [out]
            if bias is not None:
                kw["bias"] = bias
                if not isinstance(bias, (int, float)):
                    rd.append(bias)
            if scale is not None:
                kw["scale"] = scale
                if not isinstance(scale, (int, float)):
                    rd.append(scale)
            if accum_out is not None:
                kw["accum_out"] = accum_out
                wr.append(accum_out)
            S.op("act", lambda e: e.activation(out, in_, func, **kw), reads=rd, writes=wr)

        def CP(eng, out, in_):
            if eng == "act":
                ACT(out, in_, AF.Copy)
            else:
                S.op(eng, lambda e: e.tensor_copy(out, in_), reads=[in_], writes=[out])

        def MS(eng, ap, val):
            S.op(eng, lambda e: e.memset(ap, val), writes=[ap])

        def MM(out, lhsT, rhs, start, stop):
            S.op("pe", lambda e: e.matmul(out, lhsT, rhs, start=start, stop=stop), reads=[lhsT, rhs], writes=[out])

        def TR(out, in_, idn):
            S.op("pe", lambda e: e.transpose(out, in_, idn), reads=[in_, idn], writes=[out])

        def bc(ap, shape, axis):
            return ap.unsqueeze(axis).to_broadcast(list(shape))

        S.dma("sp", cst[:], consts_d, "prep")
        CP("dve", identb[:], ident)
        CP("dve", selb[:], cst[:, 256 + NM + 64:256 + NM + 64 + 128])
        MS("pool", onesb[:], 1.0)
        MS("pool", epsc[:, 0:1], RMS_EPS)
        MS("pool", epsc[:, 1:2], LN_EPS)
        MS("pool", epsc[:, 2:3], -0.5)
        S.dma("sp", fst[:], bass.AP(dram["final_norm_g"], 0, [[128, KT], [1, 128]]), "prep")
        _bf = next_bank()
        TR(_bf[:, 0:KT], fst[:], ident[0:KT, 0:KT])
        CP("dve", fngc[:], _bf[:, 0:KT])

        def emit_wconv(l):
            for s in ORDER:
                dst = SCR(l, s)
                if s < 13:
                    src = DAP("w_in", l * D * DIN + s * 512, [[DIN, 128], [128 * DIN, KT], [1, 512]])
                    d3 = bass.AP(wscr, (l * NSLAB_ALL + s) * 128 * SLOT, [[SLOT, 128], [512, KT], [1, 512]])
                elif s == 13:
                    src = DAP("s5_w_glu", l * DB * DB, [[DB, 128], [128 * DB, 4], [1, DB]])
                    d3 = bass.AP(wscr, (l * NSLAB_ALL + s) * 128 * SLOT, [[SLOT, 128], [DB, 4], [1, DB]])
                elif s < 17:
                    k = s - 14
                    src = DAP("w_branch", (l * 3 + k) * DB * D, [[D, 128], [128 * D, 4], [1, D]])
                    d3 = bass.AP(wscr, (l * NSLAB_ALL + s) * 128 * SLOT, [[SLOT, 128], [D, 4], [1, D]])
                elif s < 19:
                    hlf = s - 17
                    src = DAP("w_out", l * D * D + hlf * 512, [[D, 128], [128 * D, KT], [1, 512]])
                    d3 = bass.AP(wscr, (l * NSLAB_ALL + s) * 128 * SLOT, [[SLOT, 128], [512, KT], [1, 512]])
                else:
                    continue
                S.dma("pool", d3, src, f"wc{l}_{s}", extra_writes=[dst])

        import os
        STOP = int(os.environ.get("PREP_STOP", "999"))

        class _Stop(Exception):
            pass

        def ck(n):
            if mode == "prep_test" and n == STOP:
                raise _Stop()

        load_bufs = []

        def emit_prep_early(l):
            R = L[l]
            S.dma("sp", R["dt2"][:, 0:2], DAP("s5_log_dt", l * 32, [[2, 16], [1, 2]]), "prep")
            MS("pool", R["dt2"][:, 2:4], float(np.e))
            TT("pool", R["dt2"][:, 0:2], R["dt2"][:, 2:4], R["dt2"][:, 0:2], ALU.pow)
            MS("pool", R["bsz"][:], 0.0)
            S.dma("pool", R["bsz"][0:1, :], DAP("sgu_b", l * DB, [[DB, 1], [1, DB]]), "prep")
            S.dma("pool", R["lnG"][:], DAP("sgu_ln_g", l * DB, [[0, 128], [1, DB]]), "prep")
            S.dma("pool", R["lnB"][:], DAP("sgu_ln_b", l * DB, [[0, 128], [1, DB]]), "prep")
            S.dma("pool", R["pw"][:].rearrange("p (g d) -> p g d", d=128), DAP("pool_w", l * 65536, [[128, 128], [16384, 4], [1, 128]]), "prep")
            MS("pool", R["carry"][:], 0.0)
            MS("pool", R["halo"][:], 0.0)

        def emit_prep(l, scratch):
            R = L[l]
            C = Carver(scratch, hole=(XIN_LO, XIN_LO + 4096) if scratch is AR else None)
            st16 = C.take(3 * 128, F32, parts=16)
            ldt = C.take(2, F32, parts=16)
            P16 = C.take(48, F32)
            S.dma("sp", st16[:, 0:128], DAP("s5_lam_re", l * 2048, [[128, 16], [1, 128]]), "prep")
            S.dma("sp", st16[:, 128:256], DAP("s5_lam_im", l * 2048, [[128, 16], [1, 128]]), "prep")
            Wn = C.take(512, F32)
            S.dma("sp", Wn.rearrange("p (h s) -> p h s", s=128), DAP("sgu_w", l * 65536, [[128, 128], [16384, 4], [1, 128]]), "prep")
            colst = C.take(128, F32, parts=16)
            S.dma("sp", colst[0:8, :], DAP("norm_g", l * D, [[128, 8], [1, 128]]), "prep")
            S.dma("sp", colst[8:12, :], DAP("s5_b_glu", l * DB, [[128, 4], [1, 128]]), "prep")
            S.dma("sp", colst[12:16, :], DAP("pool_scale", l * DB, [[128, 4], [1, 128]]), "prep")
            Bre = C.take(256, F32)
            Bim = C.take(256, F32)
            S.dma("sp", Bre.rearrange("p (g c) -> p g c", c=16), DAP("s5_b_re", l * 32768, [[16, 128], [2048, 16], [1, 16]]), "prep")
            S.dma("sp", Bim.rearrange("p (g c) -> p g c", c=16), DAP("s5_b_im", l * 32768, [[16, 128], [2048, 16], [1, 16]]), "prep")
            dst32 = C.take(16, F32, parts=32)
            S.dma("sp", dst32, DAP("s5_d", l * DB, [[16, 32], [1, 16]]), "prep")
            Cs = C.take(512, F32)
            m3 = C.off
            Cn = C.take(2 * 2048, F32, parts=16)
            S.dma("sp", Cn[:, 0:2048].rearrange("p (g q) -> p g q", q=64), DAP("s5_c_re", l * 32768, [[64, 16], [1024, 32], [1, 64]]), "prep")
            S.dma("sp", Cn[:, 2048:4096].rearrange("p (g q) -> p g q", q=64), DAP("s5_c_im", l * 32768, [[64, 16], [1024, 32], [1, 64]]), "prep")
            load_bufs.extend([st16[:, 0:256], Wn, colst, Bre, Bim, dst32, Cn])
            yield "loads"
            bC = next_bank()
            for ri in range(2):
                for gp in range(16):
                    TR(bC[:, ri * 256 + gp * 16: ri * 256 + gp * 16 + 16], Cn[:, ri * 2048 + gp * 128: ri * 2048 + (gp + 1) * 128], ident[0:16, 0:16])
            CP("dve", Cs, bC[:, :])
            C.off = m3
            CP("dve", ldt, R["dt2"][:, 0:2])
            CP("dve", st16[:, 256:384].rearrange("p (a b) -> p a b", a=2), bc(ldt, [16, 2, 64], 2))
            bA = next_bank()
            for k in range(3):
                TR(bA[:, k * 16:(k + 1) * 16], st16[:, k * 128:(k + 1) * 128], ident[0:16, 0:16])
            CP("dve", P16, bA[:, 0:48])
            ck(13)
            yield
            TT("dve", Wn.rearrange("p (h s) -> p h s", s=128), Wn.rearrange("p (h s) -> p h s", s=128), bc(sgumask, [128, 4, 128], 1), ALU.mult)
            bW = next_bank()
            for h in range(4):
                TR(bW[:, h * 128:(h + 1) * 128], Wn[:, h * 128:(h + 1) * 128], ident)
            CP("dve", R["wsT"][:], bW[:, :])
            if mode == "prep_test":
                dbg.update(wsT=R["wsT"][:])
            ck(14)
            yield
            ck(15)
            yield
            bE = next_bank()
            TR(bE[:, 0:16], colst, ident[0:16, 0:16])
            CP("dve", R["cols"][:], bE[:, 0:16])
            ck(1)
            yield
            zsrc = bass.AP(zeros_t, 0, [[SLOT // 2, 128], [1, SLOT // 2]]).bitcast(BF16)
            for s_ in (20, 21, 22):
                S.dma("sp", SCR(l, s_), zsrc, f"zf{l}")
            lr, li, dt = P16[:, 0:16], P16[:, 16:32], P16[:, 32:48]
            sm = C.take(16 * 12, F32)
            smv = [sm[:, i * 16:(i + 1) * 16] for i in range(12)]
            xx, th, den, rden, nr, t0, t1, kre, kim, t2, t3, t4 = smv
            TT("dve", xx, lr, dt, ALU.mult)
            TT("dve", th, li, dt, ALU.mult)
            TT("dve", t0, lr, lr, ALU.mult)
            TT("dve", t1, li, li, ALU.mult)
            TT("dve", den, t0, t1, ALU.add)
            S.op("dve", lambda e: e.reciprocal(rden, den), reads=[den], writes=[rden])
            ck(2)
            yield
            TN = 16 * NM
            SIN = C.take(TN, F32)
            COS = C.take(TN, F32)
            m1 = C.off
            Tt = C.take(TN, F32)
            Ni = C.take(TN, I32)
            Nf = C.take(TN, F32)
            MAG = C.take(TN, F32)
            v3 = lambda a: a.rearrange("p (g m) -> p g m", m=NM)
            thB = bc(th, [128, 16, NM], 2)
            xxB = bc(xx, [128, 16, NM], 2)
            multB = bc(mult, [128, 16, NM], 1)
            STT(v3(Tt), thB, 1.0 / (2.0 * np.pi), multB, ALU.mult, ALU.mult)
            CP("dve", Ni, Tt)
            CP("dve", Nf, Ni)
            TT("dve", Nf, Tt, Nf, ALU.subtract)
            ACT(SIN, Nf, AF.Sin, scale=TWO_PI_SAFE)
            TS("dve", Tt, Tt, 0.25, None, ALU.add)
            CP("dve", Ni, Tt)
            CP("dve", Nf, Ni)
            TT("dve", Nf, Tt, Nf, ALU.subtract)
            ACT(COS, Nf, AF.Sin, scale=TWO_PI_SAFE)
            TT("dve", v3(Tt), xxB, multB, ALU.mult)
            ACT(MAG, Tt, AF.Exp)
            ck(3)
            yield
            CP("dve", R["Ec"][:].rearrange("p (g q) -> p g q", q=NQ), v3(COS)[:, :, 17:17 + NQ])
            CP("dve", R["Es"][:].rearrange("p (g q) -> p g q", q=NQ), v3(SIN)[:, :, 17:17 + NQ])
            CP("dve", R["r8"][:], v3(MAG)[:, :, 8])
            if mode == "prep_test":
                dbg.update(Ec=R["Ec"][:], Es=R["Es"][:], r8=R["r8"][:])
            Ar, Ai = COS, SIN
            TT("dve", Ar, MAG, COS, ALU.mult)
            TT("dve", Ai, MAG, SIN, ALU.mult)
            C.off = m1
            ck(4)
            yield
            TS("dve", nr, v3(Ar)[:, :, 1], -1.0, None, ALU.add)
            ni = v3(Ai)[:, :, 1]
            TT("dve", t0, nr, lr, ALU.mult)
            TT("dve", t1, ni, li, ALU.mult)
            TT("dve", t0, t0, t1, ALU.add)
            TT("dve", kre, t0, rden, ALU.mult)
            TT("dve", t2, ni, lr, ALU.mult)
            TT("dve", t3, nr, li, ALU.mult)
            TT("dve", t2, t2, t3, ALU.subtract)
            TT("dve", kim, t2, rden, ALU.mult)
            cre = C.take(128, F32)
            cim = C.take(128, F32)
            ct = C.take(128, F32)
            c3 = lambda a: a.rearrange("p (g i) -> p g i", i=8)
            ArW, AiW = v3(Ar)[:, :, 9:17], v3(Ai)[:, :, 9:17]
            kreB, kimB = bc(kre, [128, 16, 8], 2), bc(kim, [128, 16, 8], 2)
            TT("dve", c3(cre), ArW, kreB, ALU.mult)
            TT("dve", c3(ct), AiW, kimB, ALU.mult)
            TT("dve", cre, cre, ct, ALU.subtract)
            TT("dve", c3(cim), ArW, kimB, ALU.mult)
            TT("dve", c3(ct), AiW, kreB, ALU.mult)
            TT("dve", cim, cim, ct, ALU.add)
            ck(5)
            yield
            WWre = C.take(2048, F32)
            WWim = C.take(2048, F32)
            m2 = C.off
            WWt = C.take(2048, F32)
            w4 = lambda a: a.rearrange("p (g i c) -> p g i c", i=8, c=16)
            b3 = lambda a: a.rearrange("p (g c) -> p g c", c=16)
            creB, cimB = bc(c3(cre), [128, 16, 8, 16], 3), bc(c3(cim), [128, 16, 8, 16], 3)
            BreB, BimB = bc(b3(Bre), [128, 16, 8, 16], 2), bc(b3(Bim), [128, 16, 8, 16], 2)
            TT("dve", w4(WWre), creB, BreB, ALU.mult)
            TT("dve", w4(WWt), cimB, BimB, ALU.mult)
            TT("dve", WWre, WWre, WWt, ALU.subtract)
            TT("dve", w4(WWim), creB, BimB, ALU.mult)
            TT("dve", w4(WWt), cimB, BreB, ALU.mult)
            TT("dve", WWim, WWim, WWt, ALU.add)
            ck(6)
            yield
            Wfin = C.take(SLOT, BF16)
            for ri, WW in enumerate((WWre, WWim)):
                for k4 in range(4):
                    bk = next_bank()
                    for j in range(4):
                        gp = k4 * 4 + j
                        TR(bk[:, j * 128:(j + 1) * 128], WW[:, gp * 128:(gp + 1) * 128], ident)
                    dst = Wfin.rearrange("p (g r q) -> p g r q", r=2, q=64)[:, 8 * k4:8 * k4 + 8, ri, :]
                    evac_copy(dst, bk[:, :].rearrange("p (g q) -> p g q", q=64))
            S.dma("act", SCR(l, 19), Wfin, "prep")
            if mode == "prep_test":
                dbg.update(Wfin=Wfin)
            if mode != "prep_test":
                C.off = m2
            ck(7)
            yield
            Cre, Cim = b3(Cs[:, 0:256]), b3(Cs[:, 256:512])
            ck(8)
            yield
            Vcb = [C.take(16 * 9 * 16, BF16), C.take(16 * 9 * 16, BF16)]
            W7 = C.take(2 * 256, BF16)
            m4 = C.off
            VVre = C.take(16 * 9 * 16, F32)
            VVim = C.take(16 * 9 * 16, F32)
            VVt = C.take(16 * 9 * 16, F32)
            vv4 = lambda a: a.rearrange("p (g m c) -> p g m c", m=9, c=16)
            CreB, CimB = bc(Cre, [128, 16, 9, 16], 2), bc(Cim, [128, 16, 9, 16], 2)
            ArB, AiB = bc(v3(Ar)[:, :, 0:9], [128, 16, 9, 16], 3), bc(v3(Ai)[:, :, 0:9], [128, 16, 9, 16], 3)
            TT("dve", vv4(VVre), CreB, ArB, ALU.mult)
            TT("dve", vv4(VVt), CimB, AiB, ALU.mult)
            TT("dve", VVre, VVre, VVt, ALU.subtract)
            TT("dve", vv4(VVim), CreB, AiB, ALU.mult)
            TT("dve", vv4(VVt), CimB, ArB, ALU.mult)
            TT("dve", VVim, VVim, VVt, ALU.add)
            TS("dve", VVim, VVim, -1.0, None, ALU.mult)
            ck(9)
            yield
            Vc9 = []
            for ri, VV in enumerate((VVre, VVim)):
                Vc = Vcb[ri]
                CP("dve", Vc, VV)
                Vc9.append(Vc)
                Vc4 = Vc.rearrange("p (g m c) -> p g m c", m=9, c=16)
                base = (l * NSLAB_ALL + 21 + ri) * 128 * SLOT
                for g2 in range(2):
                    dst = bass.AP(wscr, base + g2 * 64 * SLOT + g2 * 128, [[SLOT, 64], [256, 16], [8 * 16, 1], [1, 128]])
                    S.dma("pool", dst, Vc4[g2 * 64:(g2 + 1) * 64, :, 1:9, :].rearrange("p g m c -> p g (m c)"), f"vp{l}{ri}", parallel=True)
            CP("dve", W7[:, 0:256].rearrange("p (g c) -> p g c", c=16), w4(WWre)[:, :, 7, :])
            CP("dve", W7[:, 256:512].rearrange("p (g c) -> p g c", c=16), w4(WWim)[:, :, 7, :])
            ck(10)
            yield
            if mode != "prep_test":
                C.off = m4
            bD = next_bank()
            TR(bD[0:16, 0:32], dst32, ident[0:32, 0:32])
            Dg = C.take(32, F32, parts=16)
            CP("dve", Dg, bD[0:16, 0:32])
            tmpD = C.take(512, F32, parts=16)
            TT("dve", tmpD.rearrange("p (g c) -> p g c", c=16), bc(ident[0:16, 0:16], [16, 32, 16], 1), bc(Dg, [16, 32, 16], 2), ALU.mult)
            tD4 = tmpD.rearrange("p (gp h c) -> p gp h c", h=2, c=16)
            Krb = C.take(32 * 128, BF16, parts=16)
            Kb4 = Krb.rearrange("p (gp h f) -> p gp h f", h=2, f=128)
            Kb5 = Krb.rearrange("p (gp h m c) -> p gp h m c", h=2, m=8, c=16)
            for g2 in range(2):
                rows = slice(g2 * 64, (g2 + 1) * 64)
                for k4 in range(4):
                    bk = next_bank()
                    for j in range(4):
                        gp = k4 * 4 + j
                        o = bk[0:16, j * 128:(j + 1) * 128]
                        MM(o, W7[rows, gp * 16:(gp + 1) * 16], Vc9[0][rows, gp * 144:gp * 144 + 128], True, False)
                        MM(o, W7[rows, 256 + gp * 16:256 + (gp + 1) * 16], Vc9[1][rows, gp * 144:gp * 144 + 128], False, True)
                    CP("act", Kb4[:, k4 * 4:(k4 + 1) * 4, g2, :], bk[0:16, :].rearrange("p (j f) -> p j f", f=128))
                    TT("dve", Kb5[:, k4 * 4:(k4 + 1) * 4, g2, 0, :], bk[0:16, :].rearrange("p (j f) -> p j f", f=128)[:, :, 0:16],
                       tD4[:, k4 * 4:(k4 + 1) * 4, g2, :], ALU.add)
            ck(11)
            yield
            if mode == "prep_test":
                dbg.update(Krow=Krb)
            K4 = Krb.rearrange("p (g m c) -> p g m c", m=8, c=16)
            base = (l * NSLAB_ALL + 20) * 128 * SLOT
            for i in range(8):
                dst = bass.AP(wscr, base + 16 * i * SLOT + i * 16, [[SLOT, 16], [128, 32], [1, (8 - i) * 16]])
                S.dma("pool", dst, K4[:, :, 0:8 - i, :].rearrange("p g m c -> p g (m c)"), f"tp{l}", parallel=True)
            if mode == "prep_test":
                dbg.update(cols=R["cols"][:])

        if mode == "prep_test":
            emit_prep_early(0)
            try:
                for _ in emit_prep(0, AR):
                    pass
            except _Stop:
                dbg["P16"] = cst[:, 0:16]
            outs = []
            for k, ap in dbg.items():
                shp = list(ap.shape)
                o = nc.dram_tensor("o_" + k, shp, ap.dtype, kind="ExternalOutput").ap()
                S.dma("sp", o, ap, "dbgout")
                outs.append("dbgout")
            S.emit(final_streams=["dbgout"])
            return nc

        nblk = (SPC * BPS) if nblocks is None else nblocks
        plan = [(b, l, s_) for b in range(nblk) for l in layers for s_ in ORDER]
        rs = {"next_load": 0, "next_use": 0, "released": set()}

        def _pump():
            while rs["next_load"] < len(plan):
                k = rs["next_load"]
                if k >= rs["next_use"] + NRING:
                    break
                if k >= NRING and (k - NRING) not in rs["released"]:
                    break
                _, l_, s_ = plan[k]
                S.dma("sp", ring[k % NRING][:], SCR(l_, s_), f"ring{k % NRING}")
                rs["next_load"] += 1

        def acquire(l_, s_):
            k = rs["next_use"]
            assert plan[k][1:] == (l_, s_), (plan[k], l_, s_)
            rs["next_use"] += 1
            _pump()
            assert rs["next_load"] > k, "ring deadlock"
            return k, ring[k % NRING]

        def release(k):
            rs["released"].add(k)
            _pump()

        def AV(off, nelem, dtype, parts=128):
            words = (nelem * mybir.dt.size(dtype) + 3) // 4
            ap = AR[0:parts, off:off + words]
            if dtype != F32:
                ap = ap.bitcast(dtype)
            return ap[:, 0:nelem]

        o = 0
        Atok = AV(o, 4096, BF16); Ysb = Atok; o += 2048
        Xim = AV(o, 2048, BF16); o += 1024
        agT = AV(o, 2048, BF16); o += 1024
        st_ = []
        for _ in range(10):
            st_.append(AV(o, 512, F32)); o += 512
        tA, tB, tC, tD, Gin_re, Gin_im, G_re, G_im, H_re, H_im = st_
        Hs = [AV(o, 16 * 65, BF16), AV(o + 528, 16 * 65, BF16)]; o += 1056
        ygT = AV(o, 2048, BF16); o += 1024
        sig = AV(o, 2048, BF16); o += 1024
        T3o = o
        bvT = AV(o, 4 * 528, F32); o += 2112
        sA = AV(o, 528, F32); o += 528
        sB = AV(o, 528, F32); o += 528
        pT = AV(o, 2048, BF16); o += 1024
        bgT = AV(o, 2048, BF16); o += 1024
        cuT = AV(o, 2048, BF16); o += 1024
        cgT = AV(o, 2048, BF16); o += 1024
        vn = AV(o, 2048, BF16); o += 1024
        lnst = AV(o, 64, F32); o += 64
        st2_ = []
        for _ in range(6):
            st2_.append(AV(o, 512, F32)); o += 512
        assert o <= AR_WORDS, o
        sq = AV(0, 4096, BF16)
        rstd = AV(2048, 512, F32)
        rstd2 = AV(2560, 512, F32)
        gk = AV(0, 4096, BF16)
        mergedF = AV(2048, 4096, F32)
        mergedT = AV(6144, 4096, BF16)
        mtmp = [AV(8192, 512, F32), AV(8704, 512, F32)]
        xtok = AV(4096, 4096, F32)
        fstat = AV(9216, 16, F32)
        junkT = AV(2048 + 1024, 1024, BF16)

        def v(ap, pat, **kw):
            return ap.rearrange(pat, **kw)

        def emit_L1(b, l, xs):
            R = L[l]
            gcol = R["cols"]
            S.stage = f'b{b}l{l}:L1'
            bk = next_bank()
            for kt in range(KT):
                ACT(sq[:, kt * NB:(kt + 1) * NB], xs[:, kt * NB:(kt + 1) * NB], AF.Square)
                MM(bk[:, :], onesb[:], sq[:, kt * NB:(kt + 1) * NB], kt == 0, kt == KT - 1)
            ACT(rstd2, bk[:, :], AF.Sqrt, bias=epsc[:, 0:1], scale=1.0 / D)
            S.op("dve", lambda e: e.reciprocal(rstd, rstd2), reads=[rstd2], writes=[rstd])
            for kt in range(KT):
                STT(hT[:, kt * NB:(kt + 1) * NB], xs[:, kt * NB:(kt + 1) * NB], gcol[:, kt:kt + 1], rstd, ALU.mult, ALU.mult)

        def emit_layer(b, l, after_l1=None, x_src=None, skip_l1=False, pre_wout=None):
            R = L[l]
            PL = "dve" if (b == 0 and l == layers[0]) else "pool"
            first = (b % BPS) == 0
            gcol = R["cols"]
            S.stage = f'b{b}l{l}:L1'
            if not skip_l1:
                emit_L1(b, l, xT if x_src is None else x_src)

            def fm_tiles(slot, ncol_tiles, consume):
                for m in range(ncol_tiles):
                    bk_ = next_bank()
                    for kt in range(KT):
                        MM(bk_[:, :], slot[:, kt * 512 + m * 128: kt * 512 + (m + 1) * 128], hT[:, kt * NB:(kt + 1) * NB], kt == 0, kt == KT - 1)
                    consume(m, bk_)

            S.stage = f'b{b}l{l}:aval'
            k0, sl = acquire(l, 0)
            A2 = v(Atok[:, 0:2048], "p (g i c) -> p g i c", i=4, c=16)
            ab = [next_bank() for _ in range(4)]
            for kt in range(KT):
                for i0 in range(4):
                    lh = v(hT[:, kt * NB:(kt + 1) * NB], "p (m i) -> p m i", i=4)[:, :, i0]
                    MM(ab[i0][:, :], lh, sl[:, kt * 512:(kt + 1) * 512], kt == 0, kt == KT - 1)
            for i0 in range(4):
                evac_copy(A2[:, :, i0, :], v(ab[i0][:, :], "p (g c) -> p g c", c=16))
            release(k0)
            S.stage = f'b{b}l{l}:trin'
            for g8 in range(4):
                bk = next_bank()
                for j in range(8):
                    g = g8 * 8 + j
                    for h in range(2):
                        MM(bk[h * 64:(h + 1) * 64, j * NQ:(j + 1) * NQ], Atok[:, g * 64:(g + 1) * 64], selb[:, h * 64:(h + 1) * 64], True, True)
                evac_copy(Xim[:, g8 * 8 * NQ:(g8 + 1) * 8 * NQ], bk[:, 0:8 * NQ])
            if after_l1 is not None:
                after_l1()
            S.stage = f'b{b}l{l}:S'
            k1, wf = acquire(l, 19)
            Sb = []
            for hf in range(2):
                bre, bim = next_bank(), next_bank()
                for gpl in range(8):
                    gp = hf * 8 + gpl
                    for g2 in range(2):
                        g = 2 * gp + g2
                        for ri, bk in enumerate((bre, bim)):
                            MM(bk[g2 * 64:(g2 + 1) * 64, gpl * NQ:(gpl + 1) * NQ], wf[:, g * 128 + ri * 64: g * 128 + (ri + 1) * 64],
                               Xim[:, g * NQ:(g + 1) * NQ], True, True)
                Sb.append((bre, bim))
            release(k1)
            S.stage = f'b{b}l{l}:rot'
            carry = R["carry"]
            for ri in range(2):
                hs3 = v(Hs[ri], "p (g q) -> p g q", q=65)
                if first:
                    MS(PL, hs3[:, :, 0], 0.0)
                else:
                    CP(PL, hs3[:, :, 0], carry[:, ri * 16:(ri + 1) * 16])
            W_ = 8 * NQ
            pre = [(tA, tB, tC, tD, Gin_re, Gin_im), tuple(st2_)]
            for hf in range(2):
                bre, bim = Sb[hf]
                Ec = R["Ec"][:, hf * 8 * NQ:(hf + 1) * 8 * NQ]
                Es = R["Es"][:, hf * 8 * NQ:(hf + 1) * 8 * NQ]
                a_, b_, c_, d_, gr_, gi_ = pre[hf]
                TT("dve", a_, bre[:, 0:W_], Ec, ALU.mult)
                TT("dve", d_, bre[:, 0:W_], Es, ALU.mult)
                TT("dve", b_, bim[:, 0:W_], Es, ALU.mult)
                TT("dve", c_, bim[:, 0:W_], Ec, ALU.mult)
                TT(PL, gr_, a_, b_, ALU.add)
                TT(PL, gi_, c_, d_, ALU.subtract)
            for hf in range(2):
                Ec = R["Ec"][:, hf * 8 * NQ:(hf + 1) * 8 * NQ]
                Es = R["Es"][:, hf * 8 * NQ:(hf + 1) * 8 * NQ]
                for ri, (Gin, G) in enumerate(((pre[hf][4], G_re), (pre[hf][5], G_im))):
                    for gpl in range(8):
                        gp = hf * 8 + gpl
                        d0 = R["r8"][:, gp:gp + 1].to_broadcast([128, NQ])
                        init = 0.0 if first else carry[:, ri * 16 + gp: ri * 16 + gp + 1]
                        o_ = G[:, gpl * NQ:(gpl + 1) * NQ]
                        i_ = Gin[:, gpl * NQ:(gpl + 1) * NQ]
                        rd = [R["r8"][:, gp:gp + 1], i_] + ([] if first else [init])
                        S.op("dve", lambda e, o_=o_, d0=d0, i_=i_, init=init: e.tensor_tensor_scan(o_, d0, i_, init, ALU.mult, ALU.add),
                             reads=rd, writes=[o_])
                TT("dve", tA, G_re, Ec, ALU.mult)
                TT("dve", tC, G_re, Es, ALU.mult)
                TT("dve", tB, G_im, Es, ALU.mult)
                TT("dve", tD, G_im, Ec, ALU.mult)
                TT(PL, H_re, tA, tB, ALU.subtract)
                TT(PL, H_im, tC, tD, ALU.add)
                for ri, H in enumerate((H_re, H_im)):
                    hs3 = v(Hs[ri], "p (g q) -> p g q", q=65)
                    h3 = v(H, "p (g q) -> p g q", q=NQ)
                    CP("dve", hs3[:, hf * 8:(hf + 1) * 8, 1:NQ + 1], h3)
                    CP(PL, carry[:, ri * 16 + hf * 8: ri * 16 + (hf + 1) * 8], h3[:, :, NQ - 1])
            S.stage = f'b{b}l{l}:win'
            k2, sl = acquire(l, 1)
            fm_tiles(sl, 4, lambda m, bk_: ACT(agT[:, m * NB:(m + 1) * NB], bk_[:, :], AF.Silu))
            release(k2)
            bv3 = v(bvT, "p (g t) -> p g t", t=528)
            if first:
                MS(PL, bv3[:, :, 0:16], 0.0)
            else:
                CP(PL, bv3[:, :, 0:16], v(R["halo"][:], "p (g t) -> p g t", t=16))
            k7, sl = acquire(l, 2)
            fm_tiles(sl, 4, lambda m, bk_: evac_copy(bv3[:, m, 16:528], bk_[:, :], "act"))
            release(k7)
            CP(PL, v(R["halo"][:], "p (g t) -> p g t", t=16), bv3[:, :, 512:528])
            k8, sl = acquire(l, 3)
            fm_tiles(sl, 4, lambda m, bk_: ACT(bgT[:, m * NB:(m + 1) * NB], bk_[:, :], AF.Silu))
            release(k8)
            k9, sl = acquire(l, 4)
            fm_tiles(sl, 4, lambda m, bk_: evac_copy(cuT[:, m * NB:(m + 1) * NB], bk_[:, :], "act"))
            release(k9)
            k11, sl = acquire(l, 6)
            fm_tiles(sl, 4, lambda m, bk_: ACT(cgT[:, m * NB:(m + 1) * NB], bk_[:, :], AF.Silu))
            release(k11)
            S.stage = f'b{b}l{l}:poolel'
            for m in range(4):
                u = bv3[:, m, :]
                w = 2 ** (m + 1)
                cur, nxt = sA, sB
                TT(PL, cur[:, 1:528], u[:, 1:528], u[:, 0:527], ALU.add)
                sh = 2
                while sh < w:
                    lo = 2 * sh - 1
                    TT(PL, nxt[:, lo:528], cur[:, lo:528], cur[:, lo - sh:528 - sh], ALU.add)
                    cur, nxt = nxt, cur
                    sh *= 2
                STT(pT[:, m * NB:(m + 1) * NB], cur[:, 16:528], 1.0 / w, u[:, 16:528], ALU.mult, ALU.subtract)
                if first:
                    TT(PL, nxt[:, 0:16], cur[:, 16:32], rcfix[:, m * 16:(m + 1) * 16], ALU.mult)
                    TT(PL, pT[:, m * NB:m * NB + 16], nxt[:, 0:16], u[:, 16:32], ALU.subtract)
            TT("dve", cuT, cuT, cgT, ALU.mult)
            S.stage = f'b{b}l{l}:Y'
            k3, tp = acquire(l, 20)
            k4_, vre = acquire(l, 21)
            k5, vim = acquire(l, 22)
            Y5 = v(Ysb[0:NQ, :], "q (f j g c) -> q f j g c", f=4, j=8, g=8)
            for ft in range(4):
                for hb in range(2):
                    bk = next_bank()
                    for pj in range(2):
                        gp = ft * 4 + hb * 2 + pj
                        first_mm = pj == 0
                        cols = slice(pj * 256, (pj + 1) * 256)
                        MM(bk[0:NQ, cols], v(Hs[0], "p (g q) -> p g q", q=65)[:, gp, 0:NQ], vre[:, gp * 256:(gp + 1) * 256], first_mm, False)
                        MM(bk[0:NQ, cols], v(Hs[1], "p (g q) -> p g q", q=65)[:, gp, 0:NQ], vim[:, gp * 256:(gp + 1) * 256], False, False)
                        for g2 in range(2):
                            g = 2 * gp + g2
                            c0 = pj * 256 + g2 * 128
                            MM(bk[0:NQ, c0:c0 + 128], Xim[:, g * NQ:(g + 1) * NQ], tp[:, g * 128:(g + 1) * 128], False, pj == 1 and g2 == 1)
                    evac_copy(Y5[:, ft, :, hb * 4:(hb + 1) * 4, :].rearrange("q j g c -> q g j c"), v(bk[0:NQ, :], "q (g j c) -> q g j c", j=8, c=16))
            release(k3); release(k4_); release(k5)
            S.stage = f'b{b}l{l}:trout'
            for ft in range(4):
                bk = next_bank()
                bkb = bk[:, :].bitcast(BF16)
                for j in range(8):
                    TR(bkb[:, j * NQ:(j + 1) * NQ], Ysb[0:NQ, (ft * 8 + j) * 128:(ft * 8 + j + 1) * 128], identb[0:NQ, 0:NQ])
                ACT(v(ygT[:, ft * NB:(ft + 1) * NB], "p (q j) -> p j q", j=8), v(bkb[:, 0:8 * NQ], "p (j q) -> p j q", j=8), AF.Gelu_apprx_tanh)
            S.stage = f'b{b}l{l}:cv'
            k10, sl = acquire(l, 5)
            for tt in range(NTT):
                bk = next_bank()
                for kt in range(KT):
                    MM(bk[:, :], hT[:, kt * NB + tt * 128: kt * NB + (tt + 1) * 128], sl[:, kt * 512:(kt + 1) * 512], kt == 0, kt == KT - 1)
                st6 = lnst[:, tt * 6:(tt + 1) * 6]
                mv = lnst[:, 24 + tt * 2: 24 + (tt + 1) * 2]
                rsd = lnst[:, 32 + tt: 33 + tt]
                S.op("dve", lambda e, st6=st6, bk=bk: e.bn_stats(st6, bk[:, :]), reads=[bk[:, :]], writes=[st6])
                S.op("dve", lambda e, st6=st6, mv=mv: e.bn_aggr(mv, st6), reads=[st6], writes=[mv])
                TS("pool", rsd, mv[:, 1:2], LN_EPS, None, ALU.add)
                TT("pool", rsd, rsd, epsc[:, 2:3], ALU.pow)
                TS("dve", vn[:, tt * DB:(tt + 1) * DB], bk[:, :], mv[:, 0:1], rsd, ALU.subtract, ALU.mult)
                TT("dve", vn[:, tt * DB:(tt + 1) * DB], vn[:, tt * DB:(tt + 1) * DB], R["lnG"][:], ALU.mult)
                TT("dve", vn[:, tt * DB:(tt + 1) * DB], vn[:, tt * DB:(tt + 1) * DB], R["lnB"][:], ALU.add)
            release(k10)
            S.stage = f'b{b}l{l}:glu'
            k6, sl = acquire(l, 13)
            gb = [next_bank() for _ in range(4)]
            for kt in range(4):
                for m in range(4):
                    MM(gb[m][:, :], sl[:, kt * 512 + m * 128: kt * 512 + (m + 1) * 128], ygT[:, kt * NB:(kt + 1) * NB], kt == 0, kt == 3)
            for m in range(4):
                ACT(sig[:, m * NB:(m + 1) * NB], gb[m][:, :], AF.Sigmoid, bias=gcol[:, 8 + m:9 + m])
            release(k6)
            TT("dve", yT[0][:], ygT, sig, ALU.mult)
            TT("dve", yT[0][:], yT[0][:], agT, ALU.mult)
            def merge_branch(k):
                for hg in range(2):
                    kg, sl_ = acquire(l, 7 + 2 * k + hg)
                    fm_tiles(sl_, 4, lambda m, bk_, hg=hg: ACT(gk[:, (hg * 4 + m) * NB:(hg * 4 + m + 1) * NB], bk_[:, :], AF.Sigmoid))
                    release(kg)
                kb, sl_ = acquire(l, 14 + k)
                for d8 in range(8):
                    bk_ = next_bank()
                    for kt in range(4):
                        MM(bk_[:, :], sl_[:, kt * D + d8 * 128: kt * D + (d8 + 1) * 128], yT[k][:, kt * NB:(kt + 1) * NB], kt == 0, kt == 3)
                    gsl = gk[:, d8 * NB:(d8 + 1) * NB]
                    mf = mergedF[:, d8 * NB:(d8 + 1) * NB]
                    if k == 0:
                        TT("dve", mf, bk_[:, :], gsl, ALU.mult)
                    else:
                        tmp = mtmp[d8 % 2]
                        TT("dve", tmp, bk_[:, :], gsl, ALU.mult)
                        if k == 1:
                            TT(PL, mf, mf, tmp, ALU.add)
                        else:
                            TT("dve" if d8 % 2 else PL, mergedT[:, d8 * NB:(d8 + 1) * NB], mf, tmp, ALU.add)
                release(kb)

            S.stage = f'b{b}l{l}:mergeA'
            merge_branch(0)
            S.stage = f'b{b}l{l}:sgu'
            for h in range(4):
                bk = next_bank()
                for tt in range(NTT):
                    o_ = bk[:, tt * 128:(tt + 1) * 128]
                    MM(o_, vn[:, tt * DB + h * 128: tt * DB + (h + 1) * 128], R["wsT"][:, h * 128:(h + 1) * 128], tt == 0, False)
                    MM(o_, onesb[:], R["bsz"][:, h * 128:(h + 1) * 128], False, tt == NTT - 1)
                TT("dve", yT[2][:, h * NB:(h + 1) * NB], bk[:, :], cuT[:, h * NB:(h + 1) * NB], ALU.mult)
            S.stage = f'b{b}l{l}:poolmm'
            for m in range(4):
                bk = next_bank()
                MM(bk[:, :], R["pw"][:, m * 128:(m + 1) * 128], pT[:, m * NB:(m + 1) * NB], True, True)
                STT(yT[1][:, m * NB:(m + 1) * NB], bk[:, :], gcol[:, 12 + m:13 + m], bgT[:, m * NB:(m + 1) * NB], ALU.mult, ALU.mult)

            if l == layers[-1] and b + 1 < nblk:
                emit_x_load(b + 1)
            S.stage = f'b{b}l{l}:mergeB'
            merge_branch(1)
            S.stage = f'b{b}l{l}:mergeC'
            merge_branch(2)
            S.stage = f'b{b}l{l}:wout'
            if pre_wout is not None:
                pre_wout()
            ko0, sl0 = acquire(l, 17)
            ko1, sl1 = acquire(l, 18)
            ob = [next_bank() for _ in range(8)]
            for kt in range(KT):
                for d8 in range(8):
                    sl_ = sl0 if d8 < 4 else sl1
                    m = d8 % 4
                    MM(ob[d8][:, :], sl_[:, kt * 512 + m * 128: kt * 512 + (m + 1) * 128], mergedT[:, kt * NB:(kt + 1) * NB], kt == 0, kt == KT - 1)
            for d8 in range(8):
                TT("dve", xT[:, d8 * NB:(d8 + 1) * NB], xT[:, d8 * NB:(d8 + 1) * NB], ob[d8][:, :], ALU.add)
            release(ko0); release(ko1)

        assert T3o == XIN_LO, (T3o, XIN_LO)
        xin = AV(T3o, 4096, F32)

        def emit_x_load(b):
            S.dma("sp", v(xin, "p (k t) -> p k t", t=NB), bass.AP(x_d.tensor, b * NB, [[NTOK, 128], [128 * NTOK, KT], [1, NB]]), "xin")

        fsq = AV(10272, 4096, BF16)
        frs2 = AV(3072, 512, F32)
        frs = AV(3584, 512, F32)

        def emit_x_to_xT():
            S.dma("sp", xT[:, :], xin, "x2x")

        def emit_final_norm(b):
            S.stage = f'b{b}:fnorm'
            ost = xtok
            if do_final:
                bk = next_bank()
                for kt in range(KT):
                    ACT(fsq[:, kt * NB:(kt + 1) * NB], xT[:, kt * NB:(kt + 1) * NB], AF.Square)
                    MM(bk[:, :], onesb[:], fsq[:, kt * NB:(kt + 1) * NB], kt == 0, kt == KT - 1)
                ACT(frs2, bk[:, :], AF.Sqrt, bias=epsc[:, 0:1], scale=1.0 / D)
                S.op("dve", lambda e: e.reciprocal(frs, frs2), reads=[frs2], writes=[frs])
                for kt in range(KT):
                    STT(ost[:, kt * NB:(kt + 1) * NB], xT[:, kt * NB:(kt + 1) * NB], fngc[:, kt:kt + 1], frs, ALU.mult, ALU.mult)
            else:
                for kt in range(KT):
                    CP("dve", ost[:, kt * NB:(kt + 1) * NB], xT[:, kt * NB:(kt + 1) * NB])
            S.dma("sp", bass.AP(out_d.tensor, b * NB, [[NTOK, 128], [128 * NTOK, KT], [1, NB]]), v(ost, "p (k t) -> p k t", t=NB), "out0")

        for l in layers:
            emit_prep_early(l)
        gens = [emit_prep(l, AR if i == 0 else BIG) for i, l in enumerate(layers)]
        for g_ in gens:
            next(g_)
        emit_x_load(0)
        gate = S.sbuf("gate", [128, 2], F32)
        S.op("pool", lambda e: e.memset(gate[:], 0.0), reads=list(load_bufs), writes=[gate[:]])
        emit_wconv(layers[0])
        while gens:
            for g_ in list(gens):
                try:
                    next(g_)
                except StopIteration:
                    gens.remove(g_)
        for l in layers[1:]:
            emit_wconv(l)
        emit_L1(0, layers[0], xin)
        emit_x_to_xT()
        for b in range(nblk):
            for li, l in enumerate(layers):
                last = li == len(layers) - 1
                pre = None
                if last and b + 1 < nblk:
                    pre = (lambda b=b: emit_L1(b + 1, layers[0], xin))
                emit_layer(b, l, None, skip_l1=(li == 0), pre_wout=pre)
            if b + 1 < nblk:
                emit_final_norm(b)
                emit_x_to_xT()
        emit_final_norm(nblk - 1)
        S.emit(final_streams=["out0"])
        build_program.last_sched = S
    return nc


def kernel(**inputs):
    x = np.asarray(inputs["x"], dtype=np.float32)
    nc = build_program("full")
    consts = pack_consts()
    weights = {n: np.ascontiguousarray(np.asarray(inputs[n], dtype=np.float32)) for n in WEIGHT_NAMES}
    in_maps = []
    for c in range(NCORES):
        m = dict(weights)
        m["x"] = np.ascontiguousarray(x[c * SPC:(c + 1) * SPC].reshape(SPC * SEQ, D).T)
        m["consts"] = consts
        m["zeros"] = np.zeros((128, SLOT // 2), np.float32)
        in_maps.append(m)
    res = run_bass_kernel_spmd(nc, in_maps, core_ids=list(range(NCORES)))
    out = np.concatenate([np.ascontiguousarray(np.asarray(r["out"]).T).reshape(SPC, SEQ, D) for r in res.results], axis=0)
    return out.astype(np.float32)
```
